# Optimizing a Trainium2 kernel written in Bass

```python
import math
import jax, jax.numpy as jnp
from jax import lax
import numpy as np

D_MODEL = 1024
BATCH = 8
SEQ = 4096
DEPTH = 1

GRID_W = 64
CTX_LEN = 256
N_MOD = 9
D_FF = 2816
MLA_HEADS = 12
QK_NOPE = 64
QK_ROPE = 32
V_HEAD = 64
Q_RANK = 256
KV_RANK = 128
MLA_WIDTH = MLA_HEADS * V_HEAD
SSM_GROUPS = 16
SSM_CH = 16
SSM_WIDTH = SSM_GROUPS * SSM_CH
SSM_STATE = 64
D_MIX = MLA_WIDTH + SSM_WIDTH
D_IN = Q_RANK + KV_RANK + QK_ROPE + SSM_WIDTH
ROPE_BASE = 10000.0
ATTN_SCALE = (QK_NOPE + QK_ROPE) ** -0.5
Q_BLOCK = 128
EPS = 1e-6
DT_MIN = 1e-3
DT_MAX = 1e-1

kernel_name = "hymba_mla_s5_macaron_dit"


def rmsnorm(x, g):
    xf = x.astype(jnp.float32)
    y = xf * lax.rsqrt(jnp.mean(xf * xf, axis=-1, keepdims=True) + EPS)
    return (y * g.astype(jnp.float32)).astype(x.dtype)


def adaln(cvec, w_mod, b_mod):
    m = jax.nn.silu(cvec) @ w_mod + b_mod
    m = m.reshape(cvec.shape[0], 1, N_MOD, D_MODEL)
    return [m[:, :, i] for i in range(N_MOD)]


def modulate(x, g, shift, scale):
    return rmsnorm(x, g) * (1.0 + scale) + shift


def swiglu(h, w_gu, w_down):
    gate, up = jnp.split(h @ w_gu, 2, axis=-1)
    return (jax.nn.silu(gate) * up) @ w_down


def axial_rope_tables(n_tokens, dtype):
    rows = n_tokens // GRID_W
    row = jnp.repeat(jnp.arange(rows), GRID_W).astype(jnp.float32)
    col = jnp.tile(jnp.arange(GRID_W), rows).astype(jnp.float32)
    per_axis = QK_ROPE // 2
    inv_freq = ROPE_BASE ** (-jnp.arange(0, per_axis, 2, dtype=jnp.float32) / per_axis)
    ang = jnp.concatenate([row[:, None] * inv_freq, col[:, None] * inv_freq], axis=-1)
    return jnp.cos(ang).astype(dtype), jnp.sin(ang).astype(dtype)


def apply_rope(t, cos, sin):
    tp = t.reshape(t.shape[:-1] + (t.shape[-1] // 2, 2))
    t1, t2 = tp[..., 0], tp[..., 1]
    return jnp.stack([t1 * cos - t2 * sin, t1 * sin + t2 * cos], axis=-1).reshape(t.shape)


def project(h, w_in, g_cq, w_uq, g_ckv, w_ukv):
    bsz, n = h.shape[:2]
    p = h @ w_in
    c_q, c_kv, k_rope, u = jnp.split(p, [Q_RANK, Q_RANK + KV_RANK, Q_RANK + KV_RANK + QK_ROPE], axis=-1)
    q = (rmsnorm(c_q, g_cq) @ w_uq).reshape(bsz, n, MLA_HEADS, QK_NOPE + QK_ROPE)
    kv = (rmsnorm(c_kv, g_ckv) @ w_ukv).reshape(bsz, n, MLA_HEADS, QK_NOPE + V_HEAD)
    q_nope, q_rope = jnp.split(q, [QK_NOPE], axis=-1)
    k_nope, v = jnp.split(kv, [QK_NOPE], axis=-1)
    return q_nope, q_rope, k_nope, v, k_rope, u


def attend(q_nope, q_rope, k_nope, k_rope, v):
    s = (jnp.einsum('bqhd,bkhd->bhqk', q_nope, k_nope)
         + jnp.einsum('bqhr,bkr->bhqk', q_rope, k_rope)).astype(jnp.float32) * ATTN_SCALE
    p = jax.nn.softmax(s, axis=-1).astype(v.dtype)
    return jnp.einsum('bhqk,bkhd->bqhd', p, v)


def latent_attention(q_nope, q_rope, k_nope, k_rope, v):
    bsz, n = q_nope.shape[:2]
    nblk = n // Q_BLOCK

    def blocks(t):
        return t.reshape((bsz, nblk, Q_BLOCK) + t.shape[2:]).swapaxes(0, 1)

    o = lax.map(lambda q: attend(q[0], q[1], k_nope, k_rope, v), (blocks(q_nope), blocks(q_rope)))
    return o.swapaxes(0, 1).reshape(bsz, n, MLA_WIDTH)


def s5_discretize(lam_re, lam_im, log_dt, b_re, b_im):
    dt = jnp.exp(log_dt.astype(jnp.float32))[:, None]
    lr = jnp.minimum(lam_re.astype(jnp.float32), -1e-4)
    li = lam_im.astype(jnp.float32)
    mag = jnp.exp(lr * dt)
    ar, ai = mag * jnp.cos(li * dt), mag * jnp.sin(li * dt)
    den = lr * lr + li * li
    fr = ((ar - 1.0) * lr + ai * li) / den
    fi = (ai * lr - (ar - 1.0) * li) / den
    br, bi = b_re.astype(jnp.float32), b_im.astype(jnp.float32)
    bbr = fr[..., None] * br - fi[..., None] * bi
    bbi = fr[..., None] * bi + fi[..., None] * br
    return ar, ai, bbr, bbi


def complex_scan(ar, ai, br, bi, reverse):
    n = br.shape[1]
    a_r = jnp.broadcast_to(ar, (1, n) + ar.shape)
    a_i = jnp.broadcast_to(ai, (1, n) + ai.shape)

    def combine(e1, e2):
        a1r, a1i, b1r, b1i = e1
        a2r, a2i, b2r, b2i = e2
        return (a2r * a1r - a2i * a1i, a2r * a1i + a2i * a1r,
                a2r * b1r - a2i * b1i + b2r, a2r * b1i + a2i * b1r + b2i)

    return lax.associative_scan(combine, (a_r, a_i, br, bi), reverse=reverse, axis=1)


def s5_drive(u, bbr, bbi):
    return jnp.einsum('btgc,gpc->btgp', u, bbr), jnp.einsum('btgc,gpc->btgp', u, bbi)


def s5_readout(xr, xi, c_re, c_im):
    return (jnp.einsum('btgp,gcp->btgc', xr, c_re.astype(jnp.float32))
            - jnp.einsum('btgp,gcp->btgc', xi, c_im.astype(jnp.float32)))


def s5_glu(y, w_glu, dtype):
    bsz, n = y.shape[:2]
    z = jax.nn.gelu(y.reshape(bsz, n, SSM_WIDTH))
    a, g = jnp.split(z @ w_glu.astype(jnp.float32), 2, axis=-1)
    return (a * jax.nn.sigmoid(g)).astype(dtype)


def s5_branch(u_c, u_x, lam_re, lam_im, log_dt, b_re, b_im, c_re, c_im, d_skip, w_glu, with_ctx_out):
    bsz = u_x.shape[0]
    uc = u_c.reshape(bsz, u_c.shape[1], SSM_GROUPS, SSM_CH).astype(jnp.float32)
    ux = u_x.reshape(bsz, u_x.shape[1], SSM_GROUPS, SSM_CH).astype(jnp.float32)
    dsk = d_skip.reshape(SSM_GROUPS, SSM_CH).astype(jnp.float32)
    y_x = ux * dsk
    y_c = uc * dsk
    for d, rev in ((0, False), (1, True)):
        ar, ai, bbr, bbi = s5_discretize(lam_re[d], lam_im[d], log_dt[d], b_re[d], b_im[d])
        bcr, bci = s5_drive(uc, bbr, bbi)
        _, _, xcr, xci = complex_scan(ar, ai, bcr, bci, rev)
        end = 0 if rev else -1
        h0r, h0i = xcr[:, end][:, None], xci[:, end][:, None]
        bxr, bxi = s5_drive(ux, bbr, bbi)
        apr, api, xr, xi = complex_scan(ar, ai, bxr, bxi, rev)
        xr, xi = xr + apr * h0r - api * h0i, xi + apr * h0i + api * h0r
        y_x = y_x + s5_readout(xr, xi, c_re[d], c_im[d])
        if with_ctx_out:
            y_c = y_c + s5_readout(xcr, xci, c_re[d], c_im[d])
    out_x = s5_glu(y_x, w_glu, u_x.dtype)
    out_c = s5_glu(y_c, w_glu, u_c.dtype) if with_ctx_out else None
    return out_x, out_c


def hybrid_mixer(hx, hc, cos, sin, w_in, g_cq, w_uq, g_ckv, w_ukv, lam_re, lam_im, log_dt,
                 b_re, b_im, c_re, c_im, d_skip, w_glu, g_mla_out, g_ssm_out, w_out, with_ctx_out):
    bsz, n = hx.shape[:2]
    qn_x, qr_x, kn_x, v_x, kr_x, u_x = project(hx, w_in, g_cq, w_uq, g_ckv, w_ukv)
    qn_c, qr_c, kn_c, v_c, kr_c, u_c = project(hc, w_in, g_cq, w_uq, g_ckv, w_ukv)
    qr_x = apply_rope(qr_x, cos[:, None, :], sin[:, None, :])
    kr_x = apply_rope(kr_x, cos, sin)
    kn_all = jnp.concatenate([kn_c, kn_x], axis=1)
    kr_all = jnp.concatenate([kr_c, kr_x], axis=1)
    v_all = jnp.concatenate([v_c, v_x], axis=1)
    attn_x = latent_attention(qn_x, qr_x, kn_all, kr_all, v_all)
    ssm_x, ssm_c = s5_branch(u_c, u_x, lam_re, lam_im, log_dt, b_re, b_im, c_re, c_im,
                             d_skip, w_glu, with_ctx_out)
    out_x = jnp.concatenate([rmsnorm(attn_x, g_mla_out), rmsnorm(ssm_x, g_ssm_out)], axis=-1) @ w_out
    out_c = None
    if with_ctx_out:
        attn_c = attend(qn_c, qr_c, kn_c, kr_c, v_c).reshape(bsz, hc.shape[1], MLA_WIDTH)
        out_c = jnp.concatenate([rmsnorm(attn_c, g_mla_out), rmsnorm(ssm_c, g_ssm_out)], axis=-1) @ w_out
    return out_x, out_c


def setup_inputs(seed: int = 0) -> dict:
    key = jax.random.key(seed)
    ks = jax.random.split(key, 32)
    f32 = jnp.float32
    L = DEPTH

    def nrm(k, shape, s):
        return jax.random.normal(k, shape, f32) * s

    def gain(k, shape):
        return 1.0 + 0.01 * jax.random.normal(k, shape, f32)

    n_idx = jnp.arange(SSM_STATE, dtype=f32)
    ssm_shape = (L, 2, SSM_GROUPS, SSM_STATE)
    return {
        "x": nrm(ks[0], (BATCH, SEQ, D_MODEL), 1.0),
        "c": nrm(ks[1], (BATCH, D_MODEL), 1.0),
        "ctx": nrm(ks[2], (BATCH, CTX_LEN, D_MODEL), 1.0),
        "c_ctx": nrm(ks[3], (D_MODEL,), 1.0),
        "w_mod": nrm(ks[4], (L, D_MODEL, N_MOD * D_MODEL), 0.5 * D_MODEL ** -0.5),
        "b_mod": nrm(ks[5], (L, N_MOD * D_MODEL), 0.01),
        "g_ffn1": gain(ks[6], (L, D_MODEL)),
        "w_gu1": nrm(ks[7], (L, D_MODEL, 2 * D_FF), D_MODEL ** -0.5),
        "w_down1": nrm(ks[8], (L, D_FF, D_MODEL), D_FF ** -0.5),
        "g_mix": gain(ks[9], (L, D_MODEL)),
        "w_in": nrm(ks[10], (L, D_MODEL, D_IN), D_MODEL ** -0.5),
        "g_cq": gain(ks[11], (L, Q_RANK)),
        "w_uq": nrm(ks[12], (L, Q_RANK, MLA_HEADS * (QK_NOPE + QK_ROPE)), Q_RANK ** -0.5),
        "g_ckv": gain(ks[13], (L, KV_RANK)),
        "w_ukv": nrm(ks[14], (L, KV_RANK, MLA_HEADS * (QK_NOPE + V_HEAD)), KV_RANK ** -0.5),
        "lam_re": -0.5 + nrm(ks[15], ssm_shape, 0.01),
        "lam_im": math.pi * n_idx + nrm(ks[16], ssm_shape, 0.01),
        "log_dt": jax.random.uniform(ks[17], (L, 2, SSM_GROUPS), f32, math.log(DT_MIN), math.log(DT_MAX)),
        "b_re": nrm(ks[18], (L, 2, SSM_GROUPS, SSM_STATE, SSM_CH), (2 * SSM_CH) ** -0.5),
        "b_im": nrm(ks[19], (L, 2, SSM_GROUPS, SSM_STATE, SSM_CH), (2 * SSM_CH) ** -0.5),
        "c_re": nrm(ks[20], (L, 2, SSM_GROUPS, SSM_CH, SSM_STATE), SSM_STATE ** -0.5),
        "c_im": nrm(ks[21], (L, 2, SSM_GROUPS, SSM_CH, SSM_STATE), SSM_STATE ** -0.5),
        "d_skip": nrm(ks[22], (L, SSM_WIDTH), 1.0),
        "w_glu": nrm(ks[23], (L, SSM_WIDTH, 2 * SSM_WIDTH), SSM_WIDTH ** -0.5),
        "g_mla_out": gain(ks[24], (L, MLA_WIDTH)),
        "g_ssm_out": gain(ks[25], (L, SSM_WIDTH)),
        "w_out": nrm(ks[26], (L, D_MIX, D_MODEL), D_MIX ** -0.5),
        "g_ffn2": gain(ks[27], (L, D_MODEL)),
        "w_gu2": nrm(ks[28], (L, D_MODEL, 2 * D_FF), D_MODEL ** -0.5),
        "w_down2": nrm(ks[29], (L, D_FF, D_MODEL), D_FF ** -0.5),
        "g_final": gain(ks[30], (D_MODEL,)),
    }


def reference(x, c, ctx, c_ctx, w_mod, b_mod, g_ffn1, w_gu1, w_down1, g_mix, w_in, g_cq, w_uq,
              g_ckv, w_ukv, lam_re, lam_im, log_dt, b_re, b_im, c_re, c_im, d_skip, w_glu,
              g_mla_out, g_ssm_out, w_out, g_ffn2, w_gu2, w_down2, g_final):
    cos, sin = axial_rope_tables(x.shape[1], x.dtype)
    for l in range(DEPTH):
        last = l == DEPTH - 1
        mx = adaln(c, w_mod[l], b_mod[l])
        mc = adaln(c_ctx[None], w_mod[l], b_mod[l])
        x = x + 0.5 * mx[2] * swiglu(modulate(x, g_ffn1[l], mx[0], mx[1]), w_gu1[l], w_down1[l])
        ctx = ctx + 0.5 * mc[2] * swiglu(modulate(ctx, g_ffn1[l], mc[0], mc[1]), w_gu1[l], w_down1[l])
        hx = modulate(x, g_mix[l], mx[3], mx[4])
        hc = modulate(ctx, g_mix[l], mc[3], mc[4])
        mix_x, mix_c = hybrid_mixer(hx, hc, cos, sin, w_in[l], g_cq[l], w_uq[l], g_ckv[l], w_ukv[l],
                                    lam_re[l], lam_im[l], log_dt[l], b_re[l], b_im[l], c_re[l], c_im[l],
                                    d_skip[l], w_glu[l], g_mla_out[l], g_ssm_out[l], w_out[l],
                                    not last)
        x = x + mx[5] * mix_x
        x = x + 0.5 * mx[8] * swiglu(modulate(x, g_ffn2[l], mx[6], mx[7]), w_gu2[l], w_down2[l])
        if not last:
            ctx = ctx + mc[5] * mix_c
            ctx = ctx + 0.5 * mc[8] * swiglu(modulate(ctx, g_ffn2[l], mc[6], mc[7]), w_gu2[l], w_down2[l])
    return rmsnorm(x, g_final)
```

```python
import contextlib
import math
import numpy as np
import concourse.bass as bass
import concourse.mybir as mybir
from concourse.bass_utils import run_bass_kernel_spmd

F32 = mybir.dt.float32
BF16 = mybir.dt.bfloat16
I32 = mybir.dt.int32
ALU = mybir.AluOpType
AF = mybir.ActivationFunctionType

ENGS = ("sp", "act", "pool", "dve", "pe")
N_DMA_SEMS = 24

D = 1024
T = 4096
NCTX = 256
DFF = 2816
NFC = 22
H = 12
EPS = 1e-6
ATTN_SCALE = 96 ** -0.5
TN = 256
NJ = 576
TWO_PI = 2.0 * math.pi


class Buf:
    __slots__ = ("name", "w", "r")

    def __init__(self, name=""):
        self.name = name
        self.w = None
        self.r = []


class Sched:
    def __init__(self):
        self.ops = []
        self.last = {e: None for e in ENGS}
        self.dmas_since = []

    def op(self, eng, fn, reads=(), writes=(), dma=False, extra=()):
        i = len(self.ops)
        deps = set(extra)
        for b in reads:
            if b.w is not None:
                deps.add(b.w)
        for b in writes:
            if b.w is not None:
                deps.add(b.w)
            for r in b.r:
                deps.add(r)
        for b in reads:
            b.r.append(i)
        for b in writes:
            b.w = i
            b.r = []
        deps.discard(i)
        self.ops.append(dict(eng=eng, fn=fn, deps=deps, dma=dma))
        self.last[eng] = i
        if dma:
            self.dmas_since.append(i)
        return i

    def barrier(self):
        deps = [v for v in self.last.values() if v is not None] + list(self.dmas_since)
        self.dmas_since = []
        ids = []
        for e in ENGS:
            ids.append(self.op(e, lambda eng: eng.nop(), extra=deps))
        return ids

    def emit(self, nc):
        ops = self.ops
        n = len(ops)
        needed = [False] * n
        for i, o in enumerate(ops):
            keep = set()
            for d in o["deps"]:
                od = ops[d]
                if od["eng"] == "pe" and o["eng"] == "pe" and not od["dma"] and not o["dma"]:
                    continue
                keep.add(d)
            o["deps"] = keep
            for d in keep:
                needed[d] = True
        cnt = {e: 0 for e in ENGS}
        dma_idx = {e: 0 for e in ENGS}
        for i, o in enumerate(ops):
            if o["dma"]:
                k = dma_idx[o["eng"]]
                dma_idx[o["eng"]] += 1
                o["dsem"] = k % N_DMA_SEMS
                o["dval"] = 16 * (k // N_DMA_SEMS + 1)
            elif needed[i]:
                cnt[o["eng"]] += 1
                o["seq"] = cnt[o["eng"]]
        with contextlib.ExitStack() as st:
            esem = {e: st.enter_context(nc.semaphore(f"s_{e}")) for e in ENGS}
            dsem = {e: [st.enter_context(nc.semaphore(f"d_{e}{k}")) for k in range(N_DMA_SEMS)]
                    for e in ("sp", "pool")}
            block = st.enter_context(nc.Block())
            hook = dict(sp=block.sync, act=block.scalar, pool=block.gpsimd, dve=block.vector, pe=block.tensor)

            def run_engine(ename):
                def body(eng):
                    waited = {}
                    for i, o in enumerate(ops):
                        if o["eng"] != ename:
                            continue
                        for d in sorted(o["deps"]):
                            od = ops[d]
                            if od["dma"]:
                                key = ("d", od["eng"], od["dsem"])
                                val = od["dval"]
                                sem = dsem[od["eng"]][od["dsem"]]
                            else:
                                key = ("e", od["eng"])
                                val = od["seq"]
                                sem = esem[od["eng"]]
                            if waited.get(key, 0) >= val:
                                continue
                            eng.wait_ge(sem, val)
                            waited[key] = val
                        if o["dma"]:
                            key = ("d", ename, o["dsem"])
                            pv = o["dval"] - 16
                            if pv > 0 and waited.get(key, 0) < pv:
                                eng.wait_ge(dsem[ename][o["dsem"]], pv)
                                waited[key] = pv
                            ins = o["fn"](eng)
                            ins.then_inc(dsem[ename][o["dsem"]], 16)
                        else:
                            ins = o["fn"](eng)
                            if needed[i]:
                                ins.then_inc(esem[ename], 1)
                return body

            for e in ENGS:
                hook[e](run_engine(e))


class Rot:
    def __init__(self, items):
        self.items = list(items)
        self.i = 0

    def next(self):
        v = self.items[self.i % len(self.items)]
        self.i += 1
        return v


def build_program(dbg=None):
    nc = bass.Bass("TRN2", target_bir_lowering=False)
    S = Sched()
    OP = S.op
    din = {}

    def inp(name, shape):
        din[name] = nc.dram_tensor(name, list(shape), F32, kind="ExternalInput").ap()
        return din[name]

    x_d = inp("x", [T, D]); c_d = inp("c", [1, D]); ctx_d = inp("ctx", [NCTX, D]); cctx_d = inp("c_ctx", [1, D])
    wmod_d = inp("w_mod", [D, 9 * D]); bmod_d = inp("b_mod", [1, 9 * D])
    gffn1_d = inp("g_ffn1", [1, D]); wgu1_d = inp("w_gu1", [D, 2 * DFF]); wdn1_d = inp("w_down1", [DFF, D])
    gmix_d = inp("g_mix", [1, D]); win_d = inp("w_in", [D, 672]); gcq_d = inp("g_cq", [1, 256])
    wuq_d = inp("w_uq", [256, 1152]); gckv_d = inp("g_ckv", [1, 128]); wukv_d = inp("w_ukv", [128, 1536])
    lamre_d = inp("lam_re", [32, 64]); lamim_d = inp("lam_im", [32, 64]); logdt_d = inp("log_dt", [1, 32])
    bre_d = inp("b_re", [32, 64, 16]); bim_d = inp("b_im", [32, 64, 16])
    cre_d = inp("c_re", [512, 64]); cim_d = inp("c_im", [512, 64])
    dskip_d = inp("d_skip", [1, 256]); wglu_d = inp("w_glu", [256, 512])
    gmla_d = inp("g_mla_out", [1, 768]); gssm_d = inp("g_ssm_out", [1, 256]); wout_d = inp("w_out", [D, D])
    gffn2_d = inp("g_ffn2", [1, D]); wgu2_d = inp("w_gu2", [D, 2 * DFF]); wdn2_d = inp("w_down2", [DFF, D])
    gfin_d = inp("g_final", [1, D])
    out_d = nc.dram_tensor("out", [T, D], F32, kind="ExternalOutput").ap()

    def scratch(name, shape, dt):
        return nc.dram_tensor(name, list(shape), dt, kind=("ExternalOutput" if dbg else "Internal")).ap()

    x1_s = scratch("x1_s", [T, D], F32)
    cqn_s = scratch("cqn_s", [256, T], BF16)
    ckvn_s = scratch("ckvn_s", [128, T + NCTX], BF16)
    kr_s = scratch("kr_s", [2, 32, T + NCTX], F32)
    u_s = scratch("u_s", [8, 256, NJ], BF16)
    y_s = scratch("y_s", [8, 256, 512], F32)
    attn_s = scratch("attn_s", [768, T], BF16)
    mrow_s = scratch("mrow_s", [2, 9 * D], F32)
    B_x1 = Buf(); B_cqn = Buf(); B_ckvn = Buf(); B_kr = Buf(); B_u = Buf(); B_y = Buf(); B_attn = Buf(); B_mrow = Buf()

    dbg_out = {}

    def dbg_dump(name, src_ap, shape, dt, buf):
        t = nc.dram_tensor("dbg_" + name, list(shape), dt, kind="ExternalOutput").ap()
        dbg_out[name] = OP("sp", lambda e: e.dma_start(out=t, in_=src_ap), reads=[buf], dma=True)

    out_dmas = []
    pst = contextlib.ExitStack()
    with pst:
        _names = {}

        def sb(st, name, shape, dt):
            _names[name] = _names.get(name, 0) + 1
            if _names[name] > 1:
                name = f"{name}_v{_names[name]}"
            return st.enter_context(nc.sbuf_tensor(name, list(shape), dt))

        psall = pst.enter_context(nc.psum_tensor("psall", [128, 4096], F32))
        psb = [psall[:, i * 512:(i + 1) * 512] for i in range(8)]
        PB = [Buf(f"ps{i}") for i in range(8)]

        ident = sb(pst, "ident", [128, 128], F32); B_ident = Buf()
        ones = sb(pst, "ones", [128, 128], F32); B_ones = Buf()
        onesb = sb(pst, "onesb", [128, 128], BF16); B_onesb = Buf()
        iot = sb(pst, "iot", [128, 128], F32); B_iot = Buf()
        sel = sb(pst, "sel", [2, 2, 128], F32); B_sel = Buf()
        modfm = sb(pst, "modfm", [128, 72, 2], F32); B_modfm = Buf()
        AB = sb(pst, "AB", [128, 3, 2, 2, 8], F32); B_AB = Buf()
        gfm = sb(pst, "gfm", [128, 3, 8], F32); B_gfm = Buf()
        gsm = sb(pst, "gsm", [128, 4, 2], F32); B_gsm = Buf()
        gmla = sb(pst, "gmla", [128, 6], F32); B_gmla = Buf()
        scs = sb(pst, "scs", [128, 8, 2], F32); B_scs = Buf()
        rbsh = sb(pst, "rbsh", [2, 512], F32); B_rbsh = Buf()
        halfpi = sb(pst, "halfpi", [128, 1], F32); B_halfpi = Buf()
        OP("dve", lambda e: e.memset(halfpi[:], math.pi / 2.0), writes=[B_halfpi])

        OP("pool", lambda e: e.iota(iot[:], pattern=[[1, 128]], base=0, channel_multiplier=-1, allow_small_or_imprecise_dtypes=True), writes=[B_iot])
        OP("dve", lambda e: e.tensor_single_scalar(out=ident[:], in_=iot[:], scalar=0.0, op=ALU.is_equal),
           reads=[B_iot], writes=[B_ident])
        OP("dve", lambda e: e.memset(ones[:], 1.0), writes=[B_ones])
        OP("dve", lambda e: e.memset(onesb[:], 1.0), writes=[B_onesb])
        OP("dve", lambda e: e.tensor_copy(out=sel[0:2, 0, :], in_=ident[0:2, 0:1].to_broadcast([2, 128])),
           reads=[B_ident], writes=[B_sel])
        OP("dve", lambda e: e.tensor_copy(out=sel[0:2, 1, :], in_=ident[0:2, 1:2].to_broadcast([2, 128])),
           reads=[B_ident], writes=[B_sel])

        def small_fm_load(dst_ap, src_row_ap, nk, buf):
            OP("sp", lambda e: e.dma_start(out=dst_ap, in_=src_row_ap.rearrange("o (k p) -> p (o k)", p=128),
                                           allow_slow_non_contiguous=True), writes=[buf], dma=True)

        small_fm_load(gfm[:, 0, :], gffn1_d, 8, B_gfm)
        small_fm_load(gfm[:, 1, :], gmix_d, 8, B_gfm)
        small_fm_load(gfm[:, 2, :], gffn2_d, 8, B_gfm)
        small_fm_load(gsm[:, 0, :], gcq_d, 2, B_gsm)
        small_fm_load(gsm[:, 1, 0:1], gckv_d, 1, B_gsm)
        small_fm_load(gsm[:, 2, :], gssm_d, 2, B_gsm)
        small_fm_load(gmla[:, :], gmla_d, 6, B_gmla)

        def load_w_bf16(st, name, src, nk, ncols, chunk_cols=2048):
            t = sb(st, name, [128, nk, ncols], BF16)
            bufs = [Buf() for _ in range(nk)]
            v = src.rearrange("(k p) n -> p k n", p=128)
            for k in range(nk):
                OP("pool", lambda e, k=k: e.dma_start(out=t[:, k, :], in_=v[:, k, :], max_dma_last_dim=chunk_cols * 4),
                   writes=[bufs[k]], dma=True)
            return t, bufs

        stw1 = contextlib.ExitStack()
        wgu, B_wgu = load_w_bf16(stw1, "wgu1", wgu1_d, 8, 2 * DFF)
        wdn, B_wdn = load_w_bf16(stw1, "wdn1", wdn1_d, NFC, D)
        win, B_win = load_w_bf16(stw1, "win", win_d, 8, 672)
        wmod_v = wmod_d.rearrange("(k p) n -> p k n", p=128)

        def adaln_block(blk, w, bw, bm, bb, rb, brb):
            cs = slice(blk * 512, (blk + 1) * 512)
            OP("sp", lambda e: e.dma_start(out=w[:], in_=wmod_v[:, :, cs]), writes=[bw], dma=True)
            OP("sp", lambda e: e.dma_start(out=bm[:], in_=bmod_d[:, cs]), writes=[bb], dma=True)
            for k in range(8):
                OP("pe", lambda e, k=k: e.matmul(psb[0][0:2, :], lhsT=scs[:, k, :], rhs=w[:, k, :], start=(k == 0), stop=False),
                   reads=[B_scs, bw], writes=[PB[0]])
            OP("pe", lambda e: e.matmul(psb[0][0:2, :], lhsT=ones[0:1, 0:2], rhs=bm[0:1, :], start=False, stop=True),
               reads=[B_ones, bb], writes=[PB[0]])
            OP("act", lambda e: e.activation(out=rb[:], in_=psb[0][0:2, :], func=AF.Copy), reads=[PB[0]], writes=[brb])
            OP("sp", lambda e: e.dma_start(out=mrow_s[:, cs], in_=rb[:]), reads=[brb], writes=[B_mrow], dma=True)
            for q in range(4):
                OP("pe", lambda e, q=q: e.matmul(psb[1][:, 2 * q:2 * q + 2], lhsT=rb[0:2, q * 128:(q + 1) * 128], rhs=ident[0:2, 0:2],
                                                 start=True, stop=True), reads=[brb, B_ident], writes=[PB[1]])
            OP("dve", lambda e: e.tensor_copy(out=modfm[:, blk * 4:(blk + 1) * 4, :], in_=psb[1][:, 0:8].rearrange("p (q r) -> p q r", r=2)),
               reads=[PB[1]], writes=[B_modfm])

        def adaln_AB(sites):
            for site, (shv, scv) in sites:
                for r in range(2):
                    OP("dve", lambda e, site=site, scv=scv, r=r: e.scalar_tensor_tensor(
                        out=AB[:, site, 0, r, :], in0=modfm[:, scv * 8:(scv + 1) * 8, r], scalar=1.0,
                        in1=gfm[:, site, :], op0=ALU.add, op1=ALU.mult), reads=[B_modfm, B_gfm], writes=[B_AB])
                    OP("dve", lambda e, site=site, shv=shv, r=r: e.tensor_copy(
                        out=AB[:, site, 1, r, :], in_=modfm[:, shv * 8:(shv + 1) * 8, r]), reads=[B_modfm], writes=[B_AB])

        NBLK0 = 10
        with contextlib.ExitStack() as st0:
            scr = sb(st0, "scr", [128, 8, 2], F32); B_scr = Buf()
            wmb = [sb(st0, f"wmb{i}", [128, 8, 512], F32) for i in range(2)]; B_wmb = [Buf(), Buf()]
            bmb = [sb(st0, f"bmb{i}", [1, 512], F32) for i in range(2)]; B_bmb = [Buf(), Buf()]
            rowb = [sb(st0, f"rowb{i}", [2, 512], F32) for i in range(2)]; B_rowb = [Buf(), Buf()]
            OP("sp", lambda e: e.dma_start(out=scr[:, :, 0], in_=c_d.rearrange("o (k p) -> p (o k)", p=128),
                                           allow_slow_non_contiguous=True), writes=[B_scr], dma=True)
            OP("sp", lambda e: e.dma_start(out=scr[:, :, 1], in_=cctx_d.rearrange("o (k p) -> p (o k)", p=128),
                                           allow_slow_non_contiguous=True), writes=[B_scr], dma=True)
            OP("act", lambda e: e.activation(out=scs[:], in_=scr[:], func=AF.Silu), reads=[B_scr], writes=[B_scs])
            for blk in range(NBLK0):
                adaln_block(blk, wmb[blk % 2], B_wmb[blk % 2], bmb[blk % 2], B_bmb[blk % 2], rowb[blk % 2], B_rowb[blk % 2])
            adaln_AB([(0, (0, 1)), (1, (3, 4))])
        S.barrier()


        def bcast_rows(st, name, col0, r, scale):
            t = sb(st, name, [128, D], F32); bt = Buf()
            for hf in range(2):
                cs = slice(col0 + hf * 512, col0 + (hf + 1) * 512)
                OP("sp", lambda e, cs=cs: e.dma_start(out=rbsh[:], in_=mrow_s[:, cs]), reads=[B_mrow],
                   writes=[B_rbsh], dma=True)
                OP("pe", lambda e: e.matmul(psb[0][:, :], lhsT=sel[0:2, r, :], rhs=rbsh[0:2, :],
                                            start=True, stop=True), reads=[B_rbsh, B_sel], writes=[PB[0]])
                OP("act", lambda e, hf=hf: e.activation(out=t[:, hf * 512:(hf + 1) * 512], in_=psb[0][:, :],
                                                        func=AF.Copy, scale=scale), reads=[PB[0]], writes=[bt])
            return t, bt

        def bcast_vec(st, name, src_row, n):
            t = sb(st, name, [128, n], F32); bt = Buf()
            for c0 in range(0, n, 512):
                c1 = min(n, c0 + 512)
                OP("sp", lambda e, c0=c0, c1=c1: e.dma_start(out=rbsh[0:1, 0:c1 - c0], in_=src_row[:, c0:c1]),
                   writes=[B_rbsh], dma=True)
                OP("pe", lambda e, c0=c0, c1=c1: e.matmul(psb[0][:, 0:c1 - c0], lhsT=ones[0:1, :], rhs=rbsh[0:1, 0:c1 - c0],
                                                            start=True, stop=True), reads=[B_rbsh, B_ones], writes=[PB[0]])
                OP("act", lambda e, c0=c0, c1=c1: e.activation(out=t[:, c0:c1], in_=psb[0][:, 0:c1 - c0], func=AF.Copy),
                   reads=[PB[0]], writes=[bt])
            return t, bt

        def norm_to_hT(xt, B_xt, nsub, site, r, hT, B_hT, W, part=0):
            ntok = nsub * 128
            if part in (0, 1):
                norm_p1(xt, B_xt, nsub, W)
            if part in (0, 2):
                norm_p2(nsub, site, r, hT, B_hT, W)

        def norm_p1(xt, B_xt, nsub, W):
            for i in range(nsub):
                OP("act", lambda e, i=i: e.activation(out=W["xn"][:, i, :], in_=xt[:, i, :], func=AF.Square,
                                                      accum_out=W["ssq"][:, i:i + 1]),
                   reads=[B_xt], writes=[W["B_xn"][i], W["B_ssq"]])
            OP("act", lambda e: e.activation(out=W["rstd"][:, 0:nsub], in_=W["ssq"][:, 0:nsub], func=AF.Sqrt,
                                             scale=1.0 / D, bias=W["eps"][:, 0:1]),
               reads=[W["B_ssq"], W["B_eps"]], writes=[W["B_rstd"]])
            OP("dve", lambda e: e.reciprocal(out=W["rstd"][:, 0:nsub], in_=W["rstd"][:, 0:nsub]),
               reads=[W["B_rstd"]], writes=[W["B_rstd"]])
            for i in range(nsub):
                xn = W["xn"]
                OP("dve", lambda e, i=i: e.tensor_scalar(out=xn[:, i, :], in0=xt[:, i, :], scalar1=W["rstd"][:, i:i + 1],
                                                         scalar2=None, op0=ALU.mult),
                   reads=[B_xt, W["B_rstd"]], writes=[W["B_xn"][i]])

        def norm_p2(nsub, site, r, hT, B_hT, W):
            ntok = nsub * 128
            for k in range(8):
                pb = W["tp_rot"].next()
                for i in range(nsub):
                    OP("pe", lambda e, i=i, k=k, pb=pb: e.transpose(out=psb[pb][:, i * 128:(i + 1) * 128],
                                                                     in_=W["xn"][:, i, k * 128:(k + 1) * 128],
                                                                     identity=ident[:]),
                       reads=[W["B_xn"][i], B_ident], writes=[PB[pb]])
                OP("act", lambda e, k=k, pb=pb: e.activation(out=hT[:, k, 0:ntok], in_=psb[pb][:, 0:ntok],
                                                             func=AF.Identity, scale=AB[:, site, 0, r, k:k + 1],
                                                             bias=AB[:, site, 1, r, k:k + 1]),
                   reads=[PB[pb], B_AB], writes=[B_hT[k]])

        def ffn(xt, B_xt, nsub, hT, B_hT, wgu, B_wgu, wdn, B_wdn, gbc, B_gbc, W, bg=None):
            ntok = nsub * 128
            gu_rot = W["gu_rot"]
            dn = W["dn_banks"]
            pend = []

            def down(c, hid, bh):
                gi = 0
                for i in range(nsub):
                    for e2 in range(2):
                        pb = dn[gi]; gi += 1
                        OP("pe", lambda e, i=i, e2=e2, pb=pb, c=c, hid=hid: e.matmul(
                            psb[pb][:, :], lhsT=hid[:, i * 128:(i + 1) * 128], rhs=wdn[:, c, e2 * 512:(e2 + 1) * 512],
                            start=(c == 0), stop=(c == NFC - 1)),
                           reads=[bh, B_wdn[c]], writes=[PB[pb]])

            for c in range(NFC):
                pb = gu_rot.next()
                for half, col0 in ((0, c * 128), (1, DFF + c * 128)):
                    for k in range(8):
                        OP("pe", lambda e, k=k, pb=pb, half=half, col0=col0: e.matmul(
                            psb[pb][:, half * 256:half * 256 + ntok], lhsT=wgu[:, k, col0:col0 + 128],
                            rhs=hT[:, k, 0:ntok], start=(k == 0), stop=(k == 7)),
                           reads=[B_wgu[k], B_hT[k]], writes=[PB[pb]])
                sg, bsg = W["sg_rot"].next()
                hid, bh = W["hid_rot"].next()
                OP("act", lambda e, pb=pb, sg=sg: e.activation(out=sg[:, 0:ntok], in_=psb[pb][:, 0:ntok], func=AF.Silu),
                   reads=[PB[pb]], writes=[bsg])
                OP("dve", lambda e, pb=pb, sg=sg, hid=hid: e.tensor_tensor(out=hid[:, 0:ntok], in0=psb[pb][:, 256:256 + ntok],
                                                                           in1=sg[:, 0:ntok], op=ALU.mult),
                   reads=[PB[pb], bsg], writes=[bh])
                pend.append((c, hid, bh))
                if len(pend) > 2:
                    down(*pend.pop(0))
                for _ in range(2):
                    if bg:
                        bg.pop(0)()
            while pend:
                down(*pend.pop(0))
            gi = 0
            for i in range(nsub):
                for e2 in range(2):
                    pb = dn[gi]; gi += 1
                    tmp, btmp = W["tmp_rot"].next()
                    OP("dve", lambda e, pb=pb, e2=e2, tmp=tmp: e.tensor_tensor(
                        out=tmp[:], in0=psb[pb][:, :], in1=gbc[:, e2 * 512:(e2 + 1) * 512], op=ALU.mult),
                       reads=[PB[pb], B_gbc], writes=[btmp])
                    OP("pool" if gi % 2 else "dve", lambda e, i=i, e2=e2, tmp=tmp: e.tensor_tensor(
                        out=xt[:, i, e2 * 512:(e2 + 1) * 512], in0=xt[:, i, e2 * 512:(e2 + 1) * 512], in1=tmp[:],
                        op=ALU.add),
                       reads=[btmp, B_xt], writes=[B_xt])

        def common_work(st):
            W = {}
            W["ssq"] = sb(st, "ssq", [128, 2], F32); W["B_ssq"] = Buf()
            W["rstd"] = sb(st, "rstd", [128, 2], F32); W["B_rstd"] = Buf()
            W["eps"] = sb(st, "epsc", [128, 1], F32); W["B_eps"] = Buf()
            OP("dve", lambda e: e.memset(W["eps"][:], EPS), writes=[W["B_eps"]])
            W["xn"] = sb(st, "xn", [128, 2, D], F32); W["B_xn"] = [Buf(), Buf()]
            W["tp_rot"] = Rot([0, 1])
            W["gu_rot"] = Rot([2, 3])
            W["dn_banks"] = [4, 5, 6, 7]
            sgs = [(sb(st, f"sg{i}", [128, TN], F32), Buf()) for i in range(2)]
            W["sg_rot"] = Rot(sgs)
            hids = [(sb(st, f"hid{i}", [128, TN], BF16), Buf()) for i in range(4)]
            W["hid_rot"] = Rot(hids)
            tmps = [(sb(st, f"tmpr{i}", [128, 512], F32), Buf()) for i in range(2)]
            W["tmp_rot"] = Rot(tmps)
            return W

        with contextlib.ExitStack() as st1:
            B_win1 = Buf()
            winsw = sb(st1, "winsw", [128, 8, 32], BF16); B_winsw = Buf()
            for k in range(8):
                OP("dve", lambda e, k=k: e.tensor_scalar(
                    out=winsw[:, k, :].rearrange("p (i two) -> p i two", two=2)[:, :, 0],
                    in0=win[:, k, 384:416].rearrange("p (i two) -> p i two", two=2)[:, :, 1],
                    scalar1=-1.0, scalar2=None, op0=ALU.mult), reads=[B_win[k]], writes=[B_winsw])
                OP("dve", lambda e, k=k: e.tensor_copy(
                    out=winsw[:, k, :].rearrange("p (i two) -> p i two", two=2)[:, :, 1],
                    in_=win[:, k, 384:416].rearrange("p (i two) -> p i two", two=2)[:, :, 0]),
                   reads=[B_win[k]], writes=[B_winsw])
            g2x, B_g2x = bcast_rows(st1, "g2x", 2 * D, 0, 0.5)
            g2c, B_g2c = bcast_rows(st1, "g2c", 2 * D, 1, 0.5)
            W = common_work(st1)
            xts = [(sb(st1, "xtA", [128, 2, D], F32), Buf()), (sb(st1, "xtB", [128, 2, D], F32), Buf())]

            def load_x(ti):
                t0_ = 0 if ti == 0 else (ti - 1) * TN
                src_ = ctx_d if ti == 0 else x_d
                xt_l, B_l = xts[ti % 2]
                OP("sp", lambda e, src_=src_, t0_=t0_, xt_l=xt_l: e.dma_start(
                    out=xt_l[:], in_=src_[t0_:t0_ + TN, :].rearrange("(i p) d -> p i d", p=128)), writes=[B_l], dma=True)
            hT = sb(st1, "hT", [128, 8, TN], BF16); B_hT = [Buf() for _ in range(8)]
            cqf = sb(st1, "cqf", [128, 3, TN], F32); B_cqf = [Buf() for _ in range(3)]
            sqf = sb(st1, "sqf", [128, 3, TN], F32); B_sqf = [Buf() for _ in range(3)]
            rbc = sb(st1, "rbc", [128, 2, TN], F32); B_rbc = [Buf(), Buf()]
            cqn_t = sb(st1, "cqn_t", [128, 3, TN], BF16); B_cqn_t = [Buf() for _ in range(3)]
            krt = sb(st1, "krt", [128, 2, TN], F32); B_krt = Buf()
            usj = sb(st1, "usj", [128, 2, 8, TN // 8], BF16); B_usj = [Buf(), Buf()]
            print("phase1 sbuf remaining", nc.sbuf_bytes_remaining)
            pj_rot = Rot([2, 3, 4, 5, 6, 7])
            ntiles = 1 + T // TN
            load_x(0)
            for ti in range(ntiles):
                is_ctx = ti == 0
                r = 1 if is_ctx else 0
                t0 = 0 if is_ctx else (ti - 1) * TN
                kcol0 = 0 if is_ctx else NCTX + t0
                if ti + 1 < ntiles:
                    load_x(ti + 1)
                xt, B_xt = xts[ti % 2]
                norm_to_hT(xt, B_xt, 2, 0, r, hT, B_hT, W, part=(0 if ti == 0 else 2))
                ffn(xt, B_xt, 2, hT, B_hT, wgu, B_wgu, wdn, B_wdn, g2c if is_ctx else g2x, B_g2c if is_ctx else B_g2x, W)
                if not is_ctx:
                    OP("pool", lambda e, t0=t0, xt=xt: e.dma_start(
                        out=x1_s[t0:t0 + TN, :].rearrange("(i p) d -> p i d", p=128), in_=xt[:]),
                       reads=[B_xt], writes=[B_x1], dma=True)
                norm_to_hT(xt, B_xt, 2, 1, r, hT, B_hT, W)
                if ti + 1 < ntiles:
                    xt_n, B_xt_n = xts[(ti + 1) % 2]
                    norm_to_hT(xt_n, B_xt_n, 2, 0, 0, hT, B_hT, W, part=1)
                chunks = ([] if is_ctx else [(0, 0), (1, 128)]) + [(2, 256)]
                for ci, col0 in chunks:
                    pb = pj_rot.next()
                    for k in range(8):
                        OP("pe", lambda e, k=k, pb=pb, col0=col0: e.matmul(
                            psb[pb][:, 0:TN], lhsT=win[:, k, col0:col0 + 128], rhs=hT[:, k, :],
                            start=(k == 0), stop=(k == 7)), reads=[B_win[k], B_hT[k]], writes=[PB[pb]])
                    OP("act", lambda e, pb=pb, ci=ci: e.activation(out=cqf[:, ci, :], in_=psb[pb][:, 0:TN], func=AF.Copy),
                       reads=[PB[pb]], writes=[B_cqf[ci]])
                    OP("dve", lambda e, ci=ci: e.tensor_tensor(out=sqf[:, ci, :], in0=cqf[:, ci, :], in1=cqf[:, ci, :],
                                                               op=ALU.mult), reads=[B_cqf[ci]], writes=[B_sqf[ci]])
                groups = ([] if is_ctx else [((0, 1), 0, 256.0, 0)]) + [((2,), 1, 128.0, 1)]
                for cis, ri, nfeat, gi in groups:
                    pb = pj_rot.next()
                    for n_, ci in enumerate(cis):
                        OP("pe", lambda e, pb=pb, ci=ci, n_=n_, cis=cis: e.matmul(
                            psb[pb][:, 0:TN], lhsT=ones[:, :], rhs=sqf[:, ci, :], start=(n_ == 0),
                            stop=(n_ == len(cis) - 1)), reads=[B_ones, B_sqf[ci]], writes=[PB[pb]])
                    OP("act", lambda e, pb=pb, ri=ri, nfeat=nfeat: e.activation(
                        out=rbc[:, ri, :], in_=psb[pb][:, 0:TN], func=AF.Sqrt, scale=1.0 / nfeat, bias=W["eps"][:, 0:1]),
                       reads=[PB[pb], W["B_eps"]], writes=[B_rbc[ri]])
                    OP("dve", lambda e, ri=ri: e.reciprocal(out=rbc[:, ri, :], in_=rbc[:, ri, :]),
                       reads=[B_rbc[ri]], writes=[B_rbc[ri]])
                    for n_, ci in enumerate(cis):
                        OP("dve", lambda e, ci=ci, ri=ri, gi=gi, n_=n_: e.scalar_tensor_tensor(
                            out=cqn_t[:, ci, :], in0=cqf[:, ci, :], scalar=gsm[:, gi, n_:n_ + 1], in1=rbc[:, ri, :],
                            op0=ALU.mult, op1=ALU.mult), reads=[B_cqf[ci], B_gsm, B_rbc[ri]], writes=[B_cqn_t[ci]])
                        if ci < 2:
                            OP("pool", lambda e, ci=ci, t0=t0: e.dma_start(
                                out=cqn_s[ci * 128:(ci + 1) * 128, t0:t0 + TN], in_=cqn_t[:, ci, :]),
                               reads=[B_cqn_t[ci]], writes=[B_cqn], dma=True)
                        else:
                            OP("pool", lambda e, kcol0=kcol0: e.dma_start(
                                out=ckvn_s[:, kcol0:kcol0 + TN], in_=cqn_t[:, 2, :]),
                               reads=[B_cqn_t[2]], writes=[B_ckvn], dma=True)
                pb = pj_rot.next()
                for k in range(8):
                    OP("pe", lambda e, k=k, pb=pb: e.matmul(psb[pb][64:96, 0:TN], lhsT=win[:, k, 384:416], rhs=hT[:, k, :],
                                                            start=(k == 0), stop=(k == 7)),
                       reads=[B_win[k], B_hT[k]], writes=[PB[pb]])
                for k in range(8):
                    OP("pe", lambda e, k=k, pb=pb: e.matmul(psb[pb][64:96, 256:256 + TN], lhsT=winsw[:, k, :], rhs=hT[:, k, :],
                                                            start=(k == 0), stop=(k == 7)),
                       reads=[B_winsw, B_hT[k]], writes=[PB[pb]])
                OP("act", lambda e, pb=pb: e.activation(
                    out=krt[64:96, :, :], in_=psb[pb][64:96, :].rearrange("p (a t) -> p a t", a=2), func=AF.Copy),
                   reads=[PB[pb]], writes=[B_krt])
                for a in range(2):
                    OP("pool", lambda e, a=a, kcol0=kcol0: e.dma_start(out=kr_s[a, :, kcol0:kcol0 + TN], in_=krt[64:96, a, :]),
                       reads=[B_krt], writes=[B_kr], dma=True)
                for gc in range(2):
                    pb = pj_rot.next()
                    col0 = 416 + gc * 128
                    for k in range(8):
                        OP("pe", lambda e, k=k, pb=pb, col0=col0: e.matmul(
                            psb[pb][:, 0:TN], lhsT=win[:, k, col0:col0 + 128], rhs=hT[:, k, :],
                            start=(k == 0), stop=(k == 7)), reads=[B_win[k], B_hT[k]], writes=[PB[pb]])
                    OP("act", lambda e, pb=pb, gc=gc: e.activation(
                        out=usj[:, gc, :, :], in_=psb[pb][:, 0:TN].rearrange("p (j s) -> p s j", s=8), func=AF.Copy),
                       reads=[PB[pb]], writes=[B_usj[gc]])
                    nj = TN // 8
                    j0s = [0, 544] if is_ctx else [32 + t0 // 8]
                    for j0 in j0s:
                        OP("pool", lambda e, gc=gc, j0=j0: e.dma_start(
                            out=u_s[:, gc * 128:(gc + 1) * 128, j0:j0 + nj].rearrange("s p j -> p s j"),
                            in_=usj[:, gc, :, :]), reads=[B_usj[gc]], writes=[B_u], dma=True)
        S.barrier()
        stw1.close()

        if dbg == "p1":
            OP("sp", lambda e: e.nop())
            S.emit(nc)
            return nc, {}

        TPS = 6.283185

        MAGIC = 12582912.0

        def frac_sincos(sin_t, cos_t, F_ap, T1, T2, bF, bT1, bT2, bsin, bcos, hp):
            OP("dve", lambda e: e.tensor_scalar(out=T1, in0=F_ap, scalar1=MAGIC, scalar2=None, op0=ALU.add), reads=[bF], writes=[bT1])
            OP("dve", lambda e: e.scalar_tensor_tensor(out=T2, in0=T1, scalar=MAGIC, in1=F_ap, op0=ALU.subtract, op1=ALU.subtract),
               reads=[bT1, bF], writes=[bT2])
            OP("act", lambda e: e.activation(out=sin_t, in_=T2, func=AF.Sin, scale=-TPS), reads=[bT2], writes=[bsin])
            OP("dve", lambda e: e.scalar_tensor_tensor(out=T1, in0=T2, scalar=-1.0, in1=T2, op0=ALU.mult, op1=ALU.max), reads=[bT2], writes=[bT1])
            OP("act", lambda e: e.activation(out=cos_t, in_=T1, func=AF.Sin, scale=-TPS, bias=hp), reads=[bT1, B_halfpi], writes=[bcos])

        with contextlib.ExitStack() as st2:
            Kmat = sb(st2, "Kmat", [128, 32, 128], BF16); B_Kmat = Buf()
            Smat = sb(st2, "Smat", [128, 32, 128], BF16); B_Smat = Buf()
            SmatT = sb(st2, "SmatT", [128, 32, 128], BF16); B_SmatT = Buf()
            Dm1 = sb(st2, "Dm1", [128, 32, 8, 16], BF16); B_Dm1 = Buf()
            Dm2 = sb(st2, "Dm2", [128, 32, 8, 16], BF16); B_Dm2 = Buf()
            FPHI = sb(st2, "FPHI", [128, 32], F32); B_FPHI = Buf()
            RHO = sb(st2, "RHO", [128, 32], F32); B_RHO = Buf()
            u8all = sb(st2, "u8all", [128, 16, NJ], BF16); B_u8 = Buf()
            for s_ in range(8):
                OP("sp", lambda e, s_=s_: e.dma_start(out=u8all[16 * s_:16 * s_ + 16, :, :],
                                                      in_=u_s[s_].rearrange("(g c) j -> c g j", c=16)),
                   reads=[B_u], writes=[B_u8], dma=True)
            with contextlib.ExitStack() as st2a:
                LRr = sb(st2a, "LRr", [32, 2, 64], F32); B_LRr = Buf()
                LIr = sb(st2a, "LIr", [32, 2, 64], F32); B_LIr = Buf()
                CRr = sb(st2a, "CRr", [128, 4, 2, 64], F32); B_CRr = Buf()
                CIr = sb(st2a, "CIr", [128, 4, 2, 64], F32); B_CIr = Buf()
                BR = sb(st2a, "BR", [128, 32, 16], F32); B_BR = Buf()
                BI = sb(st2a, "BI", [128, 32, 16], F32); B_BI = Buf()
                DSK = sb(st2a, "DSK", [128, 16], F32); B_DSK = Buf()
                LD = sb(st2a, "LD", [1, 32], F32); B_LD = Buf()
                for a in range(2):
                    OP("sp", lambda e, a=a: e.dma_start(out=LRr[:, a, :], in_=lamre_d), writes=[B_LRr], dma=True)
                    OP("sp", lambda e, a=a: e.dma_start(out=LIr[:, a, :], in_=lamim_d), writes=[B_LIr], dma=True)
                    OP("sp", lambda e, a=a: e.dma_start(out=CRr[:, :, a, :], in_=cre_d.rearrange("(q r) n -> r q n", r=128)),
                       writes=[B_CRr], dma=True)
                    OP("sp", lambda e, a=a: e.dma_start(out=CIr[:, :, a, :], in_=cim_d.rearrange("(q r) n -> r q n", r=128)),
                       writes=[B_CIr], dma=True)
                    OP("sp", lambda e, a=a: e.dma_start(out=BR[a * 64:(a + 1) * 64, :, :], in_=bre_d.rearrange("g n c -> n g c")),
                       writes=[B_BR], dma=True)
                    OP("sp", lambda e, a=a: e.dma_start(out=BI[a * 64:(a + 1) * 64, :, :], in_=bim_d.rearrange("g n c -> n g c")),
                       writes=[B_BI], dma=True)
                for s_ in range(8):
                    OP("sp", lambda e, s_=s_: e.dma_start(out=DSK[16 * s_:16 * s_ + 16, :],
                                                          in_=dskip_d.rearrange("o (g c) -> c (o g)", c=16),
                                                          allow_slow_non_contiguous=True), writes=[B_DSK], dma=True)
                OP("sp", lambda e: e.dma_start(out=LD[:], in_=logdt_d), writes=[B_LD], dma=True)

                def t32(name, shape=(128, 32)):
                    return sb(st2a, name, list(shape), F32), Buf()
                LR, B_LR = t32("LR"); LI, B_LI = t32("LI"); DT, B_DT = t32("DT")
                CR = sb(st2a, "CR", [128, 32, 16], F32); B_CR = Buf()
                CI = sb(st2a, "CI", [128, 32, 16], F32); B_CI = Buf()
                OP("pe", lambda e: e.transpose(out=psb[0][:, 0:32], in_=LRr[:].rearrange("r a n -> r (a n)"),
                                               identity=ident[0:32, 0:32]), reads=[B_LRr, B_ident], writes=[PB[0]])
                OP("pe", lambda e: e.transpose(out=psb[0][:, 32:64], in_=LIr[:].rearrange("r a n -> r (a n)"),
                                               identity=ident[0:32, 0:32]), reads=[B_LIr, B_ident], writes=[PB[0]])
                OP("pe", lambda e: e.matmul(psb[0][:, 64:96], lhsT=ones[0:1, :], rhs=LD[0:1, :], start=True, stop=True),
                   reads=[B_LD, B_ones], writes=[PB[0]])
                OP("dve", lambda e: e.tensor_copy(out=LR[:], in_=psb[0][:, 0:32]), reads=[PB[0]], writes=[B_LR])
                OP("dve", lambda e: e.tensor_copy(out=LI[:], in_=psb[0][:, 32:64]), reads=[PB[0]], writes=[B_LI])
                OP("act", lambda e: e.activation(out=DT[:], in_=psb[0][:, 64:96], func=AF.Exp), reads=[PB[0]], writes=[B_DT])
                for q in range(4):
                    OP("pe", lambda e, q=q: e.transpose(out=psb[1][:, q * 128:(q + 1) * 128],
                                                        in_=CRr[:, q, :, :].rearrange("r a n -> r (a n)"), identity=ident[:]),
                       reads=[B_CRr, B_ident], writes=[PB[1]])
                    OP("pe", lambda e, q=q: e.transpose(out=psb[2][:, q * 128:(q + 1) * 128],
                                                        in_=CIr[:, q, :, :].rearrange("r a n -> r (a n)"), identity=ident[:]),
                       reads=[B_CIr, B_ident], writes=[PB[2]])
                OP("dve", lambda e: e.tensor_copy(out=CR[:].rearrange("p g c -> p (g c)"), in_=psb[1][:, :]),
                   reads=[PB[1]], writes=[B_CR])
                OP("dve", lambda e: e.tensor_copy(out=CI[:].rearrange("p g c -> p (g c)"), in_=psb[2][:, :]),
                   reads=[PB[2]], writes=[B_CI])
                LRc, B_LRc = t32("LRc"); LRDT, B_LRDT = t32("LRDT"); LIC, B_LIC = t32("LIC")
                OP("dve", lambda e: e.tensor_scalar(out=LRc[:], in0=LR[:], scalar1=-1e-4, scalar2=None, op0=ALU.min),
                   reads=[B_LR], writes=[B_LRc])
                OP("dve", lambda e: e.tensor_tensor(out=LRDT[:], in0=LRc[:], in1=DT[:], op=ALU.mult),
                   reads=[B_LRc, B_DT], writes=[B_LRDT])
                OP("dve", lambda e: e.scalar_tensor_tensor(out=LIC[:], in0=LI[:], scalar=1.0 / TWO_PI, in1=DT[:],
                                                           op0=ALU.mult, op1=ALU.mult), reads=[B_LI, B_DT], writes=[B_LIC])
                KV, B_KV = t32("KV", (128, 9, 32))
                for k in range(9):
                    OP("dve", lambda e, k=k: e.memset(KV[:, k, :], float(k)), writes=[B_KV])
                FK, B_FK = t32("FK", (128, 9, 32)); FKc, B_FKc = t32("FKc", (128, 9, 32))
                MAG, B_MAG = t32("MAG", (128, 9, 32))
                OP("dve", lambda e: e.tensor_tensor(out=FK[:], in0=KV[:], in1=LIC[:].unsqueeze(1).to_broadcast([128, 9, 32]),
                                                    op=ALU.mult), reads=[B_KV, B_LIC], writes=[B_FK])
                OP("dve", lambda e: e.tensor_scalar(out=FKc[:], in0=FK[:], scalar1=0.25, scalar2=None, op0=ALU.add),
                   reads=[B_FK], writes=[B_FKc])
                OP("dve", lambda e: e.tensor_tensor(out=MAG[:], in0=KV[:], in1=LRDT[:].unsqueeze(1).to_broadcast([128, 9, 32]),
                                                    op=ALU.mult), reads=[B_KV, B_LRDT], writes=[B_MAG])
                OP("act", lambda e: e.activation(out=MAG[:], in_=MAG[:], func=AF.Exp), reads=[B_MAG], writes=[B_MAG])
                FFt, B_FFt = t32("FFt", (128, 9, 32)); FRs, B_FRs = t32("FRs", (128, 9, 32)); FRc, B_FRc = t32("FRc", (128, 9, 32))
                SINk, B_SINk = t32("SINk", (128, 9, 32)); COSk, B_COSk = t32("COSk", (128, 9, 32))
                frac_sincos(SINk[:], COSk[:], FK[:], FFt[:], FRs[:], B_FK, B_FFt, B_FRs, B_SINk, B_COSk, halfpi[:, 0:1])
                AR, B_AR = t32("AR", (128, 9, 32)); AI, B_AI = t32("AI", (128, 9, 32))
                OP("dve", lambda e: e.tensor_tensor(out=AR[:], in0=MAG[:], in1=COSk[:], op=ALU.mult), reads=[B_MAG, B_COSk], writes=[B_AR])
                OP("dve", lambda e: e.tensor_tensor(out=AI[:], in0=MAG[:], in1=SINk[:], op=ALU.mult), reads=[B_MAG, B_SINk], writes=[B_AI])
                OP("dve", lambda e: e.tensor_scalar(out=FPHI[:], in0=FRs[:, 8, :], scalar1=-1.0, scalar2=None, op0=ALU.mult), reads=[B_FRs], writes=[B_FPHI])
                OP("dve", lambda e: e.tensor_copy(out=RHO[:], in_=MAG[:, 8, :]), reads=[B_MAG], writes=[B_RHO])
                ta, B_ta = t32("ta"); tb, B_tb = t32("tb"); rden, B_rden = t32("rden"); arm1, B_arm1 = t32("arm1")
                FRf, B_FRf = t32("FRf"); FIf, B_FIf = t32("FIf")
                OP("dve", lambda e: e.tensor_tensor(out=ta[:], in0=LRc[:], in1=LRc[:], op=ALU.mult), reads=[B_LRc], writes=[B_ta])
                OP("dve", lambda e: e.tensor_tensor(out=tb[:], in0=LI[:], in1=LI[:], op=ALU.mult), reads=[B_LI], writes=[B_tb])
                OP("dve", lambda e: e.tensor_tensor(out=ta[:], in0=ta[:], in1=tb[:], op=ALU.add), reads=[B_ta, B_tb], writes=[B_ta])
                OP("dve", lambda e: e.reciprocal(out=rden[:], in_=ta[:]), reads=[B_ta], writes=[B_rden])
                OP("dve", lambda e: e.tensor_scalar(out=arm1[:], in0=AR[:, 1, :], scalar1=-1.0, scalar2=None, op0=ALU.add),
                   reads=[B_AR], writes=[B_arm1])
                OP("dve", lambda e: e.tensor_tensor(out=ta[:], in0=arm1[:], in1=LRc[:], op=ALU.mult), reads=[B_arm1, B_LRc, B_rden], writes=[B_ta])
                OP("dve", lambda e: e.tensor_tensor(out=tb[:], in0=AI[:, 1, :], in1=LI[:], op=ALU.mult), reads=[B_AI, B_LI], writes=[B_tb])
                OP("dve", lambda e: e.tensor_tensor(out=ta[:], in0=ta[:], in1=tb[:], op=ALU.add), reads=[B_ta, B_tb], writes=[B_ta])
                OP("dve", lambda e: e.tensor_tensor(out=FRf[:], in0=ta[:], in1=rden[:], op=ALU.mult), reads=[B_ta, B_rden], writes=[B_FRf])
                OP("dve", lambda e: e.tensor_tensor(out=ta[:], in0=AI[:, 1, :], in1=LRc[:], op=ALU.mult), reads=[B_AI, B_LRc, B_FRf], writes=[B_ta])
                OP("dve", lambda e: e.tensor_tensor(out=tb[:], in0=arm1[:], in1=LI[:], op=ALU.mult), reads=[B_arm1, B_LI], writes=[B_tb])
                OP("dve", lambda e: e.tensor_tensor(out=ta[:], in0=ta[:], in1=tb[:], op=ALU.subtract), reads=[B_ta, B_tb], writes=[B_ta])
                OP("dve", lambda e: e.tensor_tensor(out=FIf[:], in0=ta[:], in1=rden[:], op=ALU.mult), reads=[B_ta, B_rden], writes=[B_FIf])
                BBR = sb(st2a, "BBR", [128, 32, 16], F32); B_BBR = Buf()
                BBI = sb(st2a, "BBI", [128, 32, 16], F32); B_BBI = Buf()
                w1 = sb(st2a, "w1", [128, 32, 16], F32); B_w1 = Buf()
                w2 = sb(st2a, "w2", [128, 32, 16], F32); B_w2 = Buf()

                def bc16(t, ps=slice(0, 128)):
                    n = ps.stop - ps.start
                    return t.unsqueeze(2).to_broadcast([n, 32, 16])

                OP("dve", lambda e: e.tensor_tensor(out=w1[:], in0=BR[:], in1=bc16(FRf[:]), op=ALU.mult), reads=[B_BR, B_FRf], writes=[B_w1])
                OP("dve", lambda e: e.tensor_tensor(out=w2[:], in0=BI[:], in1=bc16(FIf[:]), op=ALU.mult), reads=[B_BI, B_FIf], writes=[B_w2])
                OP("dve", lambda e: e.tensor_tensor(out=BBR[:], in0=w1[:], in1=w2[:], op=ALU.subtract), reads=[B_w1, B_w2], writes=[B_BBR])
                OP("dve", lambda e: e.tensor_tensor(out=w1[:], in0=BI[:], in1=bc16(FRf[:]), op=ALU.mult), reads=[B_BI, B_FRf, B_BBR], writes=[B_w1])
                OP("dve", lambda e: e.tensor_tensor(out=w2[:], in0=BR[:], in1=bc16(FIf[:]), op=ALU.mult), reads=[B_BR, B_FIf, B_BBR], writes=[B_w2])
                OP("dve", lambda e: e.tensor_tensor(out=BBI[:], in0=w1[:], in1=w2[:], op=ALU.add), reads=[B_w1, B_w2], writes=[B_BBI])
                Pf = sb(st2a, "Pf", [128, 32, 15, 16], F32); B_Pf = Buf()
                Pb = sb(st2a, "Pb", [128, 32, 15, 16], F32); B_Pb = Buf()
                OP("pool", lambda e: e.memset(Pf[:], 0.0), writes=[B_Pf])
                OP("pool", lambda e: e.memset(Pb[:], 0.0), writes=[B_Pb])
                for k in range(8):
                    for hf in range(2):
                        ps = slice(hf * 64, (hf + 1) * 64)
                        X1, bX1, X2, bX2 = (BBR, B_BBR, BBI, B_BBI) if hf == 0 else (BBI, B_BBI, BBR, B_BBR)
                        op2 = ALU.subtract if hf == 0 else ALU.add
                        OP("dve", lambda e, ps=ps, k=k, X1=X1: e.tensor_tensor(out=w1[ps], in0=X1[ps], in1=bc16(AR[ps, k, :], ps), op=ALU.mult),
                           reads=[bX1, B_AR], writes=[B_w1])
                        OP("dve", lambda e, ps=ps, k=k, X2=X2: e.tensor_tensor(out=w2[ps], in0=X2[ps], in1=bc16(AI[ps, k, :], ps), op=ALU.mult),
                           reads=[bX2, B_AI], writes=[B_w2])
                        OP("dve", lambda e, ps=ps, k=k, op2=op2: e.tensor_tensor(out=Pf[ps, :, 7 - k, :], in0=w1[ps], in1=w2[ps], op=op2),
                           reads=[B_w1, B_w2], writes=[B_Pf])
                        OP("pool", lambda e, ps=ps, k=k: e.tensor_copy(out=Pb[ps, :, 7 + k, :], in_=Pf[ps, :, 7 - k, :]),
                           reads=[B_Pf], writes=[B_Pb])
                Cm = sb(st2a, "Cm", [128, 32, 16], F32); B_Cm = Buf()
                OP("dve", lambda e: e.tensor_copy(out=Cm[0:64], in_=CR[0:64]), reads=[B_CR], writes=[B_Cm])
                OP("dve", lambda e: e.tensor_scalar(out=Cm[64:128], in0=CI[64:128], scalar1=-1.0, scalar2=None, op0=ALU.mult),
                   reads=[B_CI], writes=[B_Cm])
                MT = sb(st2a, "MT", [128, 32, 8, 16], F32); B_MT = Buf()
                for d_, (Pt_, bP, a0) in enumerate(((Pf, B_Pf, 0), (Pb, B_Pb, 7))):
                    gs = slice(d_ * 16, (d_ + 1) * 16)
                    sg_ = 1.0 if d_ == 0 else -1.0
                    OP("dve", lambda e, Pt_=Pt_, a0=a0, gs=gs, sg_=sg_: e.tensor_scalar(
                        out=MT[0:64, gs, :, :], in0=Pt_[64:128, gs, a0:a0 + 8, :], scalar1=sg_, scalar2=None, op0=ALU.mult),
                       reads=[bP], writes=[B_MT])
                    OP("dve", lambda e, Pt_=Pt_, a0=a0, gs=gs, sg_=sg_: e.tensor_scalar(
                        out=MT[64:128, gs, :, :], in0=Pt_[0:64, gs, a0:a0 + 8, :], scalar1=-sg_, scalar2=None, op0=ALU.mult),
                       reads=[bP], writes=[B_MT])
                ER = sb(st2a, "ER", [128, 32, 16], F32); B_ER = Buf()
                EI = sb(st2a, "EI", [128, 32, 16], F32); B_EI = Buf()
                for pw in range(1, 9):
                    OP("dve", lambda e, pw=pw: e.tensor_tensor(out=w1[:], in0=CR[:], in1=bc16(AR[:, pw, :]), op=ALU.mult), reads=[B_CR, B_AR], writes=[B_w1])
                    OP("dve", lambda e, pw=pw: e.tensor_tensor(out=w2[:], in0=CI[:], in1=bc16(AI[:, pw, :]), op=ALU.mult), reads=[B_CI, B_AI], writes=[B_w2])
                    OP("dve", lambda e: e.tensor_tensor(out=ER[:], in0=w1[:], in1=w2[:], op=ALU.subtract), reads=[B_w1, B_w2], writes=[B_ER])
                    OP("dve", lambda e, pw=pw: e.tensor_tensor(out=w1[:], in0=CR[:], in1=bc16(AI[:, pw, :]), op=ALU.mult), reads=[B_CR, B_AI, B_ER], writes=[B_w1])
                    OP("dve", lambda e, pw=pw: e.tensor_tensor(out=w2[:], in0=CI[:], in1=bc16(AR[:, pw, :]), op=ALU.mult), reads=[B_CI, B_AR, B_ER], writes=[B_w2])
                    OP("dve", lambda e: e.tensor_tensor(out=EI[:], in0=w1[:], in1=w2[:], op=ALU.add), reads=[B_w1, B_w2], writes=[B_EI])
                    for d_ in range(2):
                        gs = slice(d_ * 16, (d_ + 1) * 16)
                        tl = pw - 1 if d_ == 0 else 8 - pw
                        s2 = -1.0 if d_ == 0 else 1.0
                        OP("pool", lambda e, gs=gs, tl=tl: e.tensor_copy(out=Dm1[0:64, gs, tl, :], in_=ER[0:64, gs, :]), reads=[B_ER], writes=[B_Dm1])
                        OP("pool", lambda e, gs=gs, tl=tl: e.tensor_scalar(out=Dm1[64:128, gs, tl, :], in0=EI[64:128, gs, :], scalar1=-1.0, scalar2=0.0, op0=ALU.mult, op1=ALU.add), reads=[B_EI], writes=[B_Dm1])
                        OP("pool", lambda e, gs=gs, tl=tl, s2=s2: e.tensor_scalar(out=Dm2[0:64, gs, tl, :], in0=EI[0:64, gs, :], scalar1=s2, scalar2=0.0, op0=ALU.mult, op1=ALU.add), reads=[B_EI], writes=[B_Dm2])
                        OP("pool", lambda e, gs=gs, tl=tl, s2=s2: e.tensor_scalar(out=Dm2[64:128, gs, tl, :], in0=ER[64:128, gs, :], scalar1=s2, scalar2=0.0, op0=ALU.mult, op1=ALU.add), reads=[B_ER], writes=[B_Dm2])
                krot = Rot([3, 4, 5, 6]); srot = Rot([0, 1, 2, 7])
                for dg in range(32):
                    d_, g = dg // 16, dg % 16
                    Pt_, bP, a0 = (Pf, B_Pf, 0) if d_ == 0 else (Pb, B_Pb, 7)
                    pk = krot.next(); pS = srot.next()
                    for t_ in range(8):
                        OP("pe", lambda e, Pt_=Pt_, dg=dg, t_=t_, pk=pk: e.matmul(
                            psb[pk][:, t_ * 16:(t_ + 1) * 16], lhsT=Pt_[:, dg, 7 - t_:15 - t_, :].rearrange("p a c -> p (a c)"),
                            rhs=Cm[:, dg, :], start=True, stop=True), reads=[bP, B_Cm], writes=[PB[pk]])
                    if d_ == 0:
                        OP("dve", lambda e, dg=dg, g=g, pk=pk: e.scalar_tensor_tensor(
                            out=Kmat[:, dg, :], in0=ident[:], scalar=DSK[:, g:g + 1], in1=psb[pk][:, 0:128],
                            op0=ALU.mult, op1=ALU.add), reads=[PB[pk], B_ident, B_DSK], writes=[B_Kmat])
                    else:
                        OP("dve", lambda e, dg=dg, pk=pk: e.tensor_copy(out=Kmat[:, dg, :], in_=psb[pk][:, 0:128]),
                           reads=[PB[pk]], writes=[B_Kmat])
                    OP("pe", lambda e, Pt_=Pt_, dg=dg, a0=a0, pS=pS: e.matmul(
                        psb[pS][:, 0:128], lhsT=Pt_[:, dg, a0:a0 + 8, :].rearrange("p a c -> p (a c)"), rhs=ident[:],
                        start=True, stop=True), reads=[bP, B_ident], writes=[PB[pS]])
                    OP("pe", lambda e, dg=dg, pS=pS: e.matmul(
                        psb[pS][:, 128:256], lhsT=MT[:, dg, :, :].rearrange("p a c -> p (a c)"), rhs=ident[:],
                        start=True, stop=True), reads=[B_MT, B_ident], writes=[PB[pS]])
                    OP("act", lambda e, dg=dg, pS=pS: e.activation(out=Smat[:, dg, :], in_=psb[pS][:, 0:128], func=AF.Copy),
                       reads=[PB[pS]], writes=[B_Smat])
                    OP("act", lambda e, dg=dg, pS=pS: e.activation(out=SmatT[:, dg, :], in_=psb[pS][:, 128:256], func=AF.Copy),
                       reads=[PB[pS]], writes=[B_SmatT])
            S.barrier()
            NS = 544
            ioJ = sb(st2, "ioJ", [128, NS], F32); B_ioJ = Buf()
            OP("pool", lambda e: e.iota(ioJ[:], pattern=[[1, NS]], base=0, channel_multiplier=0, allow_small_or_imprecise_dtypes=True), writes=[B_ioJ])

            def tset(n):
                return [(sb(st2, f"{n}{i}", [128, NS], F32), Buf()) for i in range(4)]
            Fs = tset("Fs"); Fc = tset("Fc"); FFs = tset("FFs"); FFc = tset("FFc"); SINt = tset("SINt"); COSt = tset("COSt")
            Vt = tset("Vt"); V2t = tset("V2t"); Wt = tset("Wt"); RHt = tset("RHt")
            P1t = [(sb(st2, f"P1t{i}", [128, NS], BF16), Buf()) for i in range(4)]
            P2t = [(sb(st2, f"P2t{i}", [128, NS], BF16), Buf()) for i in range(4)]
            ystg = [(sb(st2, f"ystg{i}", [128, 512], F32), Buf()) for i in range(2)]
            print("S5 sbuf remaining", nc.sbuf_bytes_remaining)
            srot = Rot([0, 1, 2, 3, 4, 5])

            def sets(n):
                i2 = n % 4
                return dict(F1=Fs[i2], F2=Fc[i2], FF1=FFs[i2], SN=SINt[i2], CS=COSt[i2], V=Vt[i2], V2=V2t[i2], W=Wt[i2], RH=RHt[i2],
                            P1=P1t[i2], P2=P2t[i2])

            def stageA(n):
                g, d_ = n // 2, n % 2
                dg = d_ * 16 + g
                T_ = sets(n)
                (F1, bF1), (F2, bF2), (FF1, bFF1) = T_["F1"], T_["F2"], T_["FF1"]
                (SN, bSN), (CS, bCS), (RH, bRH) = T_["SN"], T_["CS"], T_["RH"]
                OP("dve", lambda e, F1=F1, dg=dg: e.tensor_scalar(out=F1[:], in0=ioJ[:], scalar1=FPHI[:, dg:dg + 1], scalar2=None, op0=ALU.mult),
                   reads=[B_ioJ, B_FPHI], writes=[bF1])
                frac_sincos(SN[:], CS[:], F1[:], F2[:], FF1[:], bF1, bF2, bFF1, bSN, bCS, halfpi[:, 0:1])
                OP("pool", lambda e, RH=RH, dg=dg: e.tensor_scalar(out=RH[:], in0=ioJ[:], scalar1=0.0, scalar2=RHO[:, dg:dg + 1], op0=ALU.mult, op1=ALU.add),
                   reads=[B_ioJ, B_RHO], writes=[bRH])

            def stageB(n):
                g, d_ = n // 2, n % 2
                dg = d_ * 16 + g
                j0 = 0 if d_ == 0 else 32
                T_ = sets(n)
                (SN, bSN), (CS, bCS), (RH, bRH) = T_["SN"], T_["CS"], T_["RH"]
                (V, bV), (V2, bV2), (Wt_, bW) = T_["V"], T_["V2"], T_["W"]
                (P1, bP1), (P2, bP2) = T_["P1"], T_["P2"]
                pa = srot.next(); pb_ = srot.next(); pc = srot.next()
                for (M_, bM, pm, tail0) in ((Smat, B_Smat, pa, 0), (SmatT, B_SmatT, pc, 32)):
                    OP("pe", lambda e, M_=M_, dg=dg, pm=pm, g=g, j0=j0: e.matmul(psb[pm][:, 0:512], lhsT=M_[:, dg, :], rhs=u8all[:, g, j0:j0 + 512],
                                                                               start=True, stop=True), reads=[bM, B_u8], writes=[PB[pm]])
                    OP("pe", lambda e, M_=M_, dg=dg, pb_=pb_, g=g, j0=j0, tail0=tail0: e.matmul(
                        psb[pb_][:, tail0:tail0 + 32], lhsT=M_[:, dg, :], rhs=u8all[:, g, j0 + 512:j0 + 544], start=True, stop=True),
                       reads=[bM, B_u8], writes=[PB[pb_]])
                OP("dve", lambda e, V=V, CS=CS, pa=pa: e.tensor_tensor(out=V[:, 0:512], in0=psb[pa][:, 0:512], in1=CS[:, 0:512], op=ALU.mult),
                   reads=[PB[pa], bCS], writes=[bV])
                OP("dve", lambda e, V=V, CS=CS, pb_=pb_: e.tensor_tensor(out=V[:, 512:544], in0=psb[pb_][:, 0:32], in1=CS[:, 512:544], op=ALU.mult),
                   reads=[PB[pb_], bCS], writes=[bV])
                OP("dve", lambda e, V2=V2, SN=SN, pc=pc: e.tensor_tensor(out=V2[:, 0:512], in0=psb[pc][:, 0:512], in1=SN[:, 0:512], op=ALU.mult),
                   reads=[PB[pc], bSN], writes=[bV2])
                OP("dve", lambda e, V2=V2, SN=SN, pb_=pb_: e.tensor_tensor(out=V2[:, 512:544], in0=psb[pb_][:, 32:64], in1=SN[:, 512:544], op=ALU.mult),
                   reads=[PB[pb_], bSN], writes=[bV2])
                OP("pool", lambda e, V=V, V2=V2: e.tensor_tensor(out=V[:], in0=V[:], in1=V2[:], op=ALU.add), reads=[bV, bV2], writes=[bV])
                if d_ == 0:
                    OP("dve", lambda e, Wt_=Wt_, RH=RH, V=V: e.tensor_tensor_scan(out=Wt_[:, :], data0=RH[:, :], data1=V[:, :], initial=0.0,
                                                                                  op0=ALU.mult, op1=ALU.add), reads=[bRH, bV], writes=[bW])
                else:
                    OP("dve", lambda e, Wt_=Wt_, RH=RH, V=V: e.tensor_tensor_scan(out=Wt_[:, ::-1], data0=RH[:, ::-1], data1=V[:, ::-1], initial=0.0,
                                                                                  op0=ALU.mult, op1=ALU.add), reads=[bRH, bV], writes=[bW])
                OP("dve", lambda e, P1=P1, Wt_=Wt_, CS=CS: e.tensor_tensor(out=P1[:], in0=Wt_[:], in1=CS[:], op=ALU.mult), reads=[bW, bCS], writes=[bP1])
                OP("pool", lambda e, P2=P2, Wt_=Wt_, SN=SN: e.tensor_tensor(out=P2[:], in0=Wt_[:], in1=SN[:], op=ALU.mult), reads=[bW, bSN], writes=[bP2])
                sh = 31 if d_ == 0 else 1
                return [(Kmat, B_Kmat, dg, u8all, B_u8, g, 32), (Dm1, B_Dm1, dg, P1, bP1, None, sh), (Dm2, B_Dm2, dg, P2, bP2, None, sh)]

            def finish_g(g, ymm):
                pY = 6 + (g % 2)
                for n_, (M_, bM, dg, R_, bR, gsel, c0) in enumerate(ymm):
                    if gsel is not None:
                        OP("pe", lambda e, M_=M_, dg=dg, R_=R_, gsel=gsel, c0=c0, n_=n_, pY=pY: e.matmul(
                            psb[pY][:, :], lhsT=M_[:, dg, :], rhs=R_[:, gsel, c0:c0 + 512], start=(n_ == 0), stop=(n_ == 5)),
                           reads=[bM, bR], writes=[PB[pY]])
                    else:
                        OP("pe", lambda e, M_=M_, dg=dg, R_=R_, c0=c0, n_=n_, pY=pY: e.matmul(
                            psb[pY][:, :], lhsT=M_[:, dg, :, :].rearrange("p a c -> p (a c)"), rhs=R_[:, c0:c0 + 512],
                            start=(n_ == 0), stop=(n_ == 5)), reads=[bM, bR], writes=[PB[pY]])
                ys, bys = ystg[g % 2]
                OP("act", lambda e, ys=ys, pY=pY: e.activation(out=ys[:], in_=psb[pY][:, :], func=AF.Copy), reads=[PB[pY]], writes=[bys])
                for tl in range(8):
                    OP("sp", lambda e, ys=ys, tl=tl, g=g: e.dma_start(out=y_s[tl, g * 16:(g + 1) * 16, :], in_=ys[16 * tl:16 * tl + 16, :]),
                       reads=[bys], writes=[B_y], dma=True)

            wmb2 = [sb(st2, f"wmb2_{i}", [128, 8, 512], F32) for i in range(2)]; B_wmb2 = [Buf(), Buf()]
            bmb2 = [sb(st2, f"bmb2_{i}", [1, 512], F32) for i in range(2)]; B_bmb2 = [Buf(), Buf()]
            rowb2 = [sb(st2, f"rowb2_{i}", [2, 512], F32) for i in range(2)]; B_rowb2 = [Buf(), Buf()]
            print("S5 sbuf remaining (after adaLN bufs)", nc.sbuf_bytes_remaining)
            stageA(0)
            stageA(1)
            ymm = []
            for n in range(32):
                if n % 4 == 0 and NBLK0 + n // 4 < 18:
                    b_ = NBLK0 + n // 4
                    adaln_block(b_, wmb2[b_ % 2], B_wmb2[b_ % 2], bmb2[b_ % 2], B_bmb2[b_ % 2], rowb2[b_ % 2], B_rowb2[b_ % 2])
                if n + 2 < 32:
                    stageA(n + 2)
                ymm += stageB(n)
                if n % 2 == 1:
                    finish_g(n // 2, ymm)
                    ymm = []
            adaln_AB([(2, (6, 7))])
        S.barrier()

        if dbg == "s5":
            OP("sp", lambda e: e.nop())
            S.emit(nc)
            return nc, {}

        NK = T + NCTX
        with contextlib.ExitStack() as st3:
            wuq = sb(st3, "wuq", [128, 2, 1152], BF16); B_wuq = Buf()
            wuqsw = sb(st3, "wuqsw", [128, 2, H, 32], BF16); B_wuqsw = Buf()
            wuk = sb(st3, "wuk", [128, H, 64], BF16); B_wuk = Buf()
            wv = sb(st3, "wv", [128, H, 64], BF16); B_wv = Buf()
            cqn = sb(st3, "cqn", [128, 2, T], BF16); B_cqnsb = Buf()
            ckvn = sb(st3, "ckvn", [128, NK], BF16); B_ckvnsb = Buf()
            kbuf = [sb(st3, f"kbuf{i}", [128, NK], BF16) for i in range(2)]; B_kn = [Buf(), Buf()]; B_krope = Buf()
            cosT = sb(st3, "cosT", [128, T], F32); B_cosT = Buf()
            sinT = sb(st3, "sinT", [128, T], F32); B_sinT = Buf()
            sel64 = sb(st3, "sel64", [65, 64], F32); B_sel64 = Buf()
            RP = slice(64, 96)
            OP("dve", lambda e: e.memset(sel64[0:64, :], 0.0), writes=[B_sel64])
            OP("dve", lambda e: e.memset(sel64[64:65, :], 1.0), writes=[B_sel64])
            wukv_v = wukv_d.rearrange("r (h two d) -> r h two d", two=2, d=64)
            OP("pool", lambda e: e.dma_start(out=wuk[:], in_=wukv_v[:, :, 0, :]), writes=[B_wuk], dma=True)
            OP("pool", lambda e: e.dma_start(out=wv[:], in_=wukv_v[:, :, 1, :]), writes=[B_wv], dma=True)
            OP("sp", lambda e: e.dma_start(out=cqn[:], in_=cqn_s.rearrange("(k p) t -> p k t", p=128)), reads=[B_cqn], writes=[B_cqnsb], dma=True)
            OP("sp", lambda e: e.dma_start(out=ckvn[:], in_=ckvn_s), reads=[B_ckvn], writes=[B_ckvnsb], dma=True)
            with contextlib.ExitStack() as st3a:
                wst = sb(st3a, "wst", [128, 2, 1152], F32); B_wst = Buf()
                OP("sp", lambda e: e.dma_start(out=wst[:], in_=wuq_d.rearrange("(k p) n -> p k n", p=128)), writes=[B_wst], dma=True)
                OP("dve", lambda e: e.tensor_scalar(out=wuq[:], in0=wst[:], scalar1=ATTN_SCALE, scalar2=None, op0=ALU.mult),
                   reads=[B_wst], writes=[B_wuq])
                for kc in range(2):
                    rope_v = wst[:, kc, :].rearrange("p (h c) -> p h c", c=96)[:, :, 64:96].rearrange("p h (i two) -> p h i two", two=2)
                    sw_v = wuqsw[:, kc, :, :].rearrange("p h (i two) -> p h i two", two=2)
                    OP("dve", lambda e, rope_v=rope_v, sw_v=sw_v: e.tensor_scalar(out=sw_v[:, :, :, 0], in0=rope_v[:, :, :, 1], scalar1=-ATTN_SCALE,
                                                                                  scalar2=None, op0=ALU.mult), reads=[B_wst], writes=[B_wuqsw])
                    OP("dve", lambda e, rope_v=rope_v, sw_v=sw_v: e.tensor_scalar(out=sw_v[:, :, :, 1], in0=rope_v[:, :, :, 0], scalar1=ATTN_SCALE,
                                                                                  scalar2=None, op0=ALU.mult), reads=[B_wst], writes=[B_wuqsw])
                frow = sb(st3a, "frow", [1, 64], F32); B_frow = Buf()
                fpart = sb(st3a, "fpart", [128, 2], F32); B_fpart = Buf()
                OP("dve", lambda e: e.memset(frow[:], 0.0), writes=[B_frow])
                for i_ in range(8):
                    fq = (10000.0 ** (-(2.0 * i_) / 16.0)) / TWO_PI
                    OP("dve", lambda e, i_=i_, fq=fq: e.memset(frow[0:1, 2 * i_:2 * i_ + 2], fq), writes=[B_frow])
                    OP("dve", lambda e, i_=i_, fq=fq: e.memset(frow[0:1, 48 + 2 * i_:48 + 2 * i_ + 2], fq), writes=[B_frow])
                OP("pe", lambda e: e.matmul(psb[0][64:96, 0:1], lhsT=frow[0:1, 0:32], rhs=ones[0:1, 0:1], start=True, stop=True),
                   reads=[B_frow, B_ones], writes=[PB[0]])
                OP("pe", lambda e: e.matmul(psb[0][64:96, 1:2], lhsT=frow[0:1, 32:64], rhs=ones[0:1, 0:1], start=True, stop=True),
                   reads=[B_frow, B_ones], writes=[PB[0]])
                OP("dve", lambda e: e.tensor_copy(out=fpart[RP, :], in_=psb[0][RP, 0:2]), reads=[PB[0]], writes=[B_fpart])
                rowv = sb(st3a, "rowv", [128, T], F32); B_rowv = Buf()
                colv = sb(st3a, "colv", [128, T], F32); B_colv = Buf()
                OP("pool", lambda e: e.iota(rowv[RP, :], pattern=[[1, 64], [0, 64]], base=0, channel_multiplier=0, allow_small_or_imprecise_dtypes=True), writes=[B_rowv])
                OP("pool", lambda e: e.iota(colv[RP, :], pattern=[[0, 64], [1, 64]], base=0, channel_multiplier=0, allow_small_or_imprecise_dtypes=True), writes=[B_colv])
                Fa = sb(st3a, "Fa", [128, T], F32); B_Fa = Buf()
                Fb = sb(st3a, "Fb", [128, T], F32); B_Fb = Buf()
                Ff = sb(st3a, "Ff", [128, T], F32); B_Ff = Buf()
                OP("dve", lambda e: e.tensor_scalar(out=Fa[RP, :], in0=rowv[RP, :], scalar1=fpart[RP, 0:1], scalar2=None, op0=ALU.mult),
                   reads=[B_rowv, B_fpart], writes=[B_Fa])
                OP("dve", lambda e: e.scalar_tensor_tensor(out=Fa[RP, :], in0=colv[RP, :], scalar=fpart[RP, 1:2], in1=Fa[RP, :], op0=ALU.mult, op1=ALU.add),
                   reads=[B_colv, B_fpart, B_Fa], writes=[B_Fa])
                frac_sincos(sinT[RP, :], cosT[RP, :], Fa[RP, :], Fb[RP, :], Ff[RP, :], B_Fa, B_Fb, B_Ff, B_sinT, B_cosT, halfpi[RP, 0:1])
                krA = rowv; krB = colv
                OP("sp", lambda e: e.dma_start(out=krA[RP, 0:NK - T], in_=kr_s[0, :, 0:NCTX]), reads=[B_kr, B_Fa], writes=[B_rowv], dma=True)
                for kb_ in kbuf:
                    OP("dve", lambda e, kb_=kb_: e.tensor_copy(out=kb_[RP, 0:NCTX], in_=krA[RP, 0:NCTX]), reads=[B_rowv], writes=[B_krope])
                OP("sp", lambda e: e.dma_start(out=krA[RP, :], in_=kr_s[0, :, NCTX:NK]), reads=[B_kr, B_krope], writes=[B_rowv], dma=True)
                OP("sp", lambda e: e.dma_start(out=krB[RP, :], in_=kr_s[1, :, NCTX:NK]), reads=[B_kr, B_Fa], writes=[B_colv], dma=True)
                OP("dve", lambda e: e.tensor_tensor(out=Fa[RP, :], in0=krA[RP, :], in1=cosT[RP, :], op=ALU.mult), reads=[B_rowv, B_cosT, B_sinT], writes=[B_Fa])
                OP("dve", lambda e: e.tensor_tensor(out=Fb[RP, :], in0=krB[RP, :], in1=sinT[RP, :], op=ALU.mult), reads=[B_colv, B_sinT, B_cosT], writes=[B_Fb])
                for kb_ in kbuf:
                    OP("dve", lambda e, kb_=kb_: e.tensor_tensor(out=kb_[RP, NCTX:NK], in0=Fa[RP, :], in1=Fb[RP, :], op=ALU.add),
                       reads=[B_Fa, B_Fb], writes=[B_krope])
            S.barrier()
            Vall = sb(st3, "Vall", [128, 34, H, 65], BF16); B_Vall = Buf()
            OP("pool", lambda e: e.memset(Vall[:, :, :, 64:65], 1.0), writes=[B_Vall])
            vrot = Rot([1, 2, 3, 4])
            for kt in range(34):
                for hh in range(2):
                    pv_ = vrot.next()
                    OP("pe", lambda e, kt=kt, hh=hh, pv_=pv_: e.matmul(
                        psb[pv_][:, 0:384], lhsT=ckvn[:, kt * 128:(kt + 1) * 128],
                        rhs=wv[:, hh * 6:(hh + 1) * 6, :].rearrange("p h d -> p (h d)"), start=True, stop=True),
                       reads=[B_ckvnsb, B_wv], writes=[PB[pv_]])
                    OP("act" if hh else "dve", (lambda e, kt=kt, hh=hh, pv_=pv_: e.activation(
                        out=Vall[:, kt, hh * 6:(hh + 1) * 6, 0:64], in_=psb[pv_][:, 0:384].rearrange("p (h d) -> p h d", d=64), func=AF.Copy))
                       if hh else (lambda e, kt=kt, hh=hh, pv_=pv_: e.tensor_copy(
                           out=Vall[:, kt, hh * 6:(hh + 1) * 6, 0:64], in_=psb[pv_][:, 0:384].rearrange("p (h d) -> p h d", d=64))),
                       reads=[PB[pv_]], writes=[B_Vall])
            S.barrier()
            Qh = [(sb(st3, f"Qh{i}", [128, 512], BF16), Buf()) for i in range(2)]
            qt1 = sb(st3, "qt1", [128, 512], F32); B_qt1 = Buf()
            qt2 = sb(st3, "qt2", [128, 512], F32); B_qt2 = Buf()
            Pts = [(sb(st3, f"Pt{i}", [128, 1024], BF16), Buf()) for i in range(3)]
            Osb = [(sb(st3, f"Osb{i}", [65, 512], F32), Buf()) for i in range(2)]
            atts = [(sb(st3, f"att{i}", [64, 512], BF16), Buf()) for i in range(2)]
            print("attn sbuf remaining", nc.sbuf_bytes_remaining)
            SP_ = [(0, Buf()), (2, Buf())]
            NB_ = H * 8

            def gen_K(h):
                kb_ = kbuf[h % 2]; bkn = B_kn[h % 2]
                for n_, c0 in enumerate(range(0, NK, 512)):
                    c1 = min(NK, c0 + 512)
                    pk = 6 + (n_ % 2)
                    OP("pe", lambda e, h=h, c0=c0, c1=c1, pk=pk: e.matmul(psb[pk][0:64, 0:c1 - c0], lhsT=wuk[:, h, :], rhs=ckvn[:, c0:c1],
                                                                         start=True, stop=True), reads=[B_wuk, B_ckvnsb], writes=[PB[pk]])
                    OP("dve", lambda e, kb_=kb_, c0=c0, c1=c1, pk=pk: e.tensor_copy(out=kb_[0:64, c0:c1], in_=psb[pk][0:64, 0:c1 - c0]),
                       reads=[PB[pk]], writes=[bkn])

            def gen_Q(b):
                h, qb = b // 8, b % 8
                qs = slice(qb * 512, (qb + 1) * 512)
                Q, bQ = Qh[b % 2]
                for kc in range(2):
                    OP("pe", lambda e, h=h, kc=kc, qs=qs: e.matmul(psb[6][0:96, :], lhsT=wuq[:, kc, h * 96:(h + 1) * 96], rhs=cqn[:, kc, qs],
                                                                start=(kc == 0), stop=(kc == 1)), reads=[B_wuq, B_cqnsb], writes=[PB[6]])
                for kc in range(2):
                    OP("pe", lambda e, h=h, kc=kc, qs=qs: e.matmul(psb[7][64:96, :], lhsT=wuqsw[:, kc, h, :], rhs=cqn[:, kc, qs],
                                                                start=(kc == 0), stop=(kc == 1)), reads=[B_wuqsw, B_cqnsb], writes=[PB[7]])
                OP("dve", lambda e, Q=Q: e.tensor_copy(out=Q[0:64, :], in_=psb[6][0:64, :]), reads=[PB[6]], writes=[bQ])
                OP("dve", lambda e, qs=qs: e.tensor_tensor(out=qt1[RP, :], in0=psb[6][RP, :], in1=cosT[RP, qs], op=ALU.mult),
                   reads=[PB[6], B_cosT], writes=[B_qt1])
                OP("dve", lambda e, qs=qs: e.tensor_tensor(out=qt2[RP, :], in0=psb[7][RP, :], in1=sinT[RP, qs], op=ALU.mult),
                   reads=[PB[7], B_sinT], writes=[B_qt2])
                OP("dve", lambda e, Q=Q: e.tensor_tensor(out=Q[RP, :], in0=qt1[RP, :], in1=qt2[RP, :], op=ALU.add),
                   reads=[B_qt1, B_qt2], writes=[bQ])

            def evac1(b):
                pO = 4 + (b % 2)
                Ob, bOb = Osb[b % 2]
                OP("dve", lambda e, Ob=Ob, pO=pO: e.tensor_copy(out=Ob[0:65, :], in_=psb[pO][0:65, :]), reads=[PB[pO]], writes=[bOb])
                OP("dve", lambda e, Ob=Ob: e.reciprocal(out=Ob[64:65, :], in_=Ob[64:65, :]), reads=[bOb], writes=[bOb])

            def evac2(b):
                h, qb = b // 8, b % 8
                qs = slice(qb * 512, (qb + 1) * 512)
                Ob, bOb = Osb[b % 2]
                at, bat = atts[b % 2]
                OP("pe", lambda e, Ob=Ob: e.matmul(psb[7][0:64, :], lhsT=sel64[0:65, :], rhs=Ob[0:65, :], start=True, stop=True),
                   reads=[B_sel64, bOb], writes=[PB[7]])
                OP("dve", lambda e, at=at, Ob=Ob: e.tensor_tensor(out=at[:, :], in0=Ob[0:64, :], in1=psb[7][0:64, :], op=ALU.mult),
                   reads=[bOb, PB[7]], writes=[bat])
                OP("sp", lambda e, at=at, h=h, qs=qs: e.dma_start(out=attn_s[h * 64:(h + 1) * 64, qs], in_=at[:, :]),
                   reads=[bat], writes=[B_attn], dma=True)

            gen_K(0)
            gen_Q(0)
            p_i = 0
            for b in range(NB_):
                h, qb = b // 8, b % 8
                kb_ = kbuf[h % 2]; bkn = B_kn[h % 2]
                Q, bQ = Qh[b % 2]
                pO = 4 + (b % 2)
                pend = []

                def pv_pair(kt0, Pt_, bPt, h=h, pO=pO):
                    for j in range(2):
                        kt = kt0 + j
                        OP("pe", lambda e, kt=kt, j=j, Pt_=Pt_, h=h, pO=pO: e.matmul(psb[pO][0:65, :], lhsT=Vall[:, kt, h, :], rhs=Pt_[:, j * 512:(j + 1) * 512],
                                                                                  start=(kt == 0), stop=(kt == 33)), reads=[B_Vall, bPt], writes=[PB[pO]])
                for pr_ in range(17):
                    kt0 = 2 * pr_
                    sp0, bSp = SP_[pr_ % 2]
                    for j in range(2):
                        kt = kt0 + j
                        OP("pe", lambda e, kt=kt, j=j, kb_=kb_, Q=Q, sp0=sp0: e.matmul(psb[sp0 + j][:, :], lhsT=kb_[0:96, kt * 128:(kt + 1) * 128], rhs=Q[0:96, :],
                                                                                    start=True, stop=True), reads=[bkn, B_krope, bQ], writes=[bSp])
                    Pt_, bPt = Pts[p_i % 3]; p_i += 1
                    OP("act", lambda e, Pt_=Pt_, sp0=sp0: e.activation(out=Pt_[:, :], in_=psall[:, sp0 * 512:(sp0 + 2) * 512], func=AF.Exp),
                       reads=[bSp], writes=[bPt])
                    pend.append((kt0, Pt_, bPt))
                    if len(pend) > 2:
                        pv_pair(*pend.pop(0))
                    if pr_ == 1 and b > 0:
                        evac1(b - 1)
                    if pr_ == 3 and b + 1 < NB_:
                        gen_Q(b + 1)
                    if pr_ == 7 and b > 0:
                        evac2(b - 1)
                    if pr_ == 10 and qb == 7 and h + 1 < H:
                        gen_K(h + 1)
                while pend:
                    pv_pair(*pend.pop(0))
            evac1(NB_ - 1)
            evac2(NB_ - 1)
        S.barrier()

        if dbg == "attn":
            OP("sp", lambda e: e.nop())
            S.emit(nc)
            return nc, {}

        with contextlib.ExitStack() as st4:
            wgu2v, B_wgu2 = load_w_bf16(st4, "wgu2", wgu2_d, 8, 2 * DFF)
            wdn2v, B_wdn2 = load_w_bf16(st4, "wdn2", wdn2_d, NFC, D)
            wglu, B_wglu = load_w_bf16(st4, "wglu", wglu_d, 2, 512)
            g5, B_g5 = bcast_rows(st4, "g5", 5 * D, 0, 1.0)
            g8, B_g8 = bcast_rows(st4, "g8", 8 * D, 0, 0.5)
            gfb, B_gfb = bcast_vec(st4, "gfb", gfin_d, D)
            W3 = common_work(st4)
            xt3v = sb(st4, "xt3", [128, 2, D], F32); B_xt3 = Buf()
            hT3v = sb(st4, "hT3", [128, 8, TN], BF16); B_hT3 = [Buf() for _ in range(8)]
            wo_t, B_wo = load_w_bf16(st4, "wout", wout_d, 8, D)
            ysb = sb(st4, "ysb", [128, 2, 8, TN // 8], F32); B_ysb = [Buf(), Buf()]
            xnf = W3["xn"][:].rearrange("p i d -> p (i d)")
            yn = xnf[:, 0:256]; B_yn = W3["B_xn"][0]
            tg = xnf[:, 256:512]; B_tg = W3["B_xn"][0]
            sgl = xnf[:, 512:768]; B_sgl = W3["B_xn"][0]
            zT = sb(st4, "zT", [128, 2, TN], BF16); B_zT = [Buf(), Buf()]
            ssm = sb(st4, "ssm", [128, 2, TN], F32); B_ssm = [Buf(), Buf()]
            sq1 = [(xnf[:, 1024 + i * 256:1024 + (i + 1) * 256], W3["B_xn"][1]) for i in range(2)]
            sq2 = [(sb(st4, f"sq2_{i}", [128, TN], BF16), Buf()) for i in range(2)]
            rbs = xnf[:, 768:1024]; B_rbs = W3["B_xn"][0]
            rba = xnf[:, 1536:1792]; B_rba = W3["B_xn"][1]
            ssmn = sb(st4, "ssmn", [128, 2, TN], BF16); B_ssmn = [Buf(), Buf()]
            at_t = sb(st4, "at_t", [128, 6, TN], BF16); B_at = Buf()
            print("phase3 sbuf remaining", nc.sbuf_bytes_remaining)
            if dbg == "p3alloc":
                S.barrier()
                OP("sp", lambda e: e.nop())
                S.emit(nc)
                return nc, {}
            sq1_rot = Rot(sq1); sq2_rot = Rot(sq2)
            pr = Rot([2, 3])
            wout_v = wout_d.rearrange("(k p) n -> p k n", p=128)
            def gelu_stage(ti_):
                t0_ = ti_ * TN
                j0_ = ti_ * (TN // 8)
                fs = []

                def loads():
                    OP("sp", lambda e: e.dma_start(out=at_t[:], in_=attn_s.rearrange("(k p) t -> p k t", p=128)[:, :, t0_:t0_ + TN]),
                       reads=[B_attn], writes=[B_at], dma=True)
                    for gc in range(2):
                        OP("sp", lambda e, gc=gc: e.dma_start(out=ysb[:, gc, :, :],
                                                              in_=y_s[:, gc * 128:(gc + 1) * 128, j0_:j0_ + TN // 8].rearrange("t p j -> p t j")),
                           reads=[B_y], writes=[B_ysb[gc]], dma=True)
                fs.append(loads)
                for gc in range(2):
                    fs.append(lambda gc=gc: OP("dve", lambda e: e.tensor_copy(out=yn[:].rearrange("p (j t) -> p j t", t=8),
                                                                            in_=ysb[:, gc, :, :].rearrange("p t j -> p j t")), reads=[B_ysb[gc]], writes=[B_yn]))
                    fs.append(lambda: OP("dve", lambda e: e.tensor_tensor(out=tg[:], in0=yn[:], in1=yn[:], op=ALU.mult), reads=[B_yn], writes=[B_tg]))
                    fs.append(lambda: OP("dve", lambda e: e.tensor_scalar(out=tg[:], in0=tg[:], scalar1=0.044715, scalar2=1.0, op0=ALU.mult, op1=ALU.add),
                                         reads=[B_tg], writes=[B_tg]))
                    fs.append(lambda: OP("dve", lambda e: e.tensor_tensor(out=tg[:], in0=tg[:], in1=yn[:], op=ALU.mult), reads=[B_tg, B_yn], writes=[B_tg]))
                    fs.append(lambda: OP("act", lambda e: e.activation(out=sgl[:], in_=tg[:], func=AF.Sigmoid, scale=2.0 * math.sqrt(2.0 / math.pi)),
                                         reads=[B_tg], writes=[B_sgl]))
                    fs.append(lambda gc=gc: OP("dve", lambda e: e.tensor_tensor(out=zT[:, gc, :], in0=yn[:], in1=sgl[:], op=ALU.mult),
                                               reads=[B_yn, B_sgl], writes=[B_zT[gc]]))
                fs.append(lambda: None)
                fs.append(lambda: None)
                for c2 in range(2):
                    def glu_mm(c2=c2):
                        for (pp, n4) in ((0, c2), (1, 2 + c2)):
                            for gc in range(2):
                                OP("pe", lambda e, pp=pp, n4=n4, gc=gc: e.matmul(psb[pp][:, 0:TN], lhsT=wglu[:, gc, n4 * 128:(n4 + 1) * 128], rhs=zT[:, gc, :],
                                                                              start=(gc == 0), stop=(gc == 1)), reads=[B_wglu[gc], B_zT[gc]], writes=[PB[pp]])

                    def glu_ev(c2=c2):
                        OP("act", lambda e: e.activation(out=sgl[:], in_=psb[1][:, 0:TN], func=AF.Sigmoid), reads=[PB[1]], writes=[B_sgl])
                        OP("dve", lambda e, c2=c2: e.tensor_tensor(out=ssm[:, c2, :], in0=psb[0][:, 0:TN], in1=sgl[:], op=ALU.mult),
                           reads=[PB[0], B_sgl], writes=[B_ssm[c2]])
                    fs.append(glu_mm); fs.append(glu_ev)
                ssq_ = [sq1[0], sq1[1]]
                for c2 in range(2):
                    def ssm_sq(c2=c2):
                        sq, bsq = ssq_[c2]
                        OP("dve", lambda e, sq=sq, c2=c2: e.tensor_tensor(out=sq[:], in0=ssm[:, c2, :], in1=ssm[:, c2, :], op=ALU.mult),
                           reads=[B_ssm[c2]], writes=[bsq])
                    fs.append(ssm_sq)

                def at_sq(kc):
                    sq, bsq = sq2[kc % 2]
                    OP("pool", lambda e, sq=sq, kc=kc: e.tensor_tensor(out=sq[:], in0=at_t[:, kc, :], in1=at_t[:, kc, :], op=ALU.mult),
                       reads=[B_at], writes=[bsq])

                def at_mm(kc):
                    sq, bsq = sq2[kc % 2]
                    OP("pe", lambda e, sq=sq, kc=kc: e.matmul(psb[1][:, 0:TN], lhsT=onesb[:, :], rhs=sq[:], start=(kc == 0), stop=(kc == 5)),
                       reads=[bsq, B_onesb], writes=[PB[1]])

                def ssm_mm():
                    for c2 in range(2):
                        sq, bsq = ssq_[c2]
                        OP("pe", lambda e, sq=sq, c2=c2: e.matmul(psb[0][:, 0:TN], lhsT=ones[:, :], rhs=sq[:], start=(c2 == 0), stop=(c2 == 1)),
                           reads=[bsq, B_ones], writes=[PB[0]])
                fs.append(lambda: at_sq(0)); fs.append(lambda: at_sq(1)); fs.append(ssm_mm)
                for kc in range(6):
                    fs.append(lambda kc=kc: at_mm(kc))
                    if kc + 2 < 6:
                        fs.append(lambda kc=kc: at_sq(kc + 2))

                def ssm_fin():
                    OP("act", lambda e: e.activation(out=rbs[:], in_=psb[0][:, 0:TN], func=AF.Sqrt, scale=1.0 / 256.0, bias=W3["eps"][:, 0:1]),
                       reads=[PB[0], W3["B_eps"]], writes=[B_rbs])
                    OP("dve", lambda e: e.reciprocal(out=rbs[:], in_=rbs[:]), reads=[B_rbs], writes=[B_rbs])

                def ssm_n():
                    for c2 in range(2):
                        OP("dve", lambda e, c2=c2: e.scalar_tensor_tensor(out=ssmn[:, c2, :], in0=ssm[:, c2, :], scalar=gsm[:, 2, c2:c2 + 1], in1=rbs[:],
                                                                          op0=ALU.mult, op1=ALU.mult), reads=[B_ssm[c2], B_gsm, B_rbs], writes=[B_ssmn[c2]])

                def at_fin():
                    OP("act", lambda e: e.activation(out=rba[:], in_=psb[1][:, 0:TN], func=AF.Sqrt, scale=1.0 / 768.0, bias=W3["eps"][:, 0:1]),
                       reads=[PB[1], W3["B_eps"]], writes=[B_rba])
                    OP("dve", lambda e: e.reciprocal(out=rba[:], in_=rba[:]), reads=[B_rba], writes=[B_rba])

                def at_n(k0):
                    for kc in range(k0, k0 + 3):
                        OP("dve", lambda e, kc=kc: e.scalar_tensor_tensor(out=at_t[:, kc, :], in0=at_t[:, kc, :], scalar=gmla[:, kc:kc + 1], in1=rba[:],
                                                                          op0=ALU.mult, op1=ALU.mult), reads=[B_at, B_gmla, B_rba], writes=[B_at])
                fs.append(ssm_fin); fs.append(ssm_n); fs.append(at_fin); fs.append(lambda: at_n(0)); fs.append(lambda: at_n(3))
                return fs

            for ti in range(T // TN):
                t0 = ti * TN
                j0 = ti * (TN // 8)
                if ti == 0:
                    for f_ in gelu_stage(0):
                        f_()
                OP("sp", lambda e, t0=t0: e.dma_start(out=xt3v[:], in_=x1_s[t0:t0 + TN, :].rearrange("(i p) d -> p i d", p=128)),
                   reads=[B_x1], writes=[B_xt3], dma=True)
                dn = W3["dn_banks"]
                for kc in range(8):
                    wo = wo_t[:, kc, :]; bwo = B_wo[kc]
                    gi = 0
                    for i in range(2):
                        for e2 in range(2):
                            pb = dn[gi]; gi += 1
                            if kc < 6:
                                OP("pe", lambda e, wo=wo, kc=kc, i=i, e2=e2, pb=pb: e.matmul(
                                    psb[pb][:, :], lhsT=at_t[:, kc, i * 128:(i + 1) * 128], rhs=wo[:, e2 * 512:(e2 + 1) * 512],
                                    start=(kc == 0), stop=False), reads=[B_at, bwo], writes=[PB[pb]])
                            else:
                                OP("pe", lambda e, wo=wo, kc=kc, i=i, e2=e2, pb=pb: e.matmul(
                                    psb[pb][:, :], lhsT=ssmn[:, kc - 6, i * 128:(i + 1) * 128], rhs=wo[:, e2 * 512:(e2 + 1) * 512],
                                    start=False, stop=(kc == 7)), reads=[B_ssmn[kc - 6], bwo], writes=[PB[pb]])
                gi = 0
                for i in range(2):
                    for e2 in range(2):
                        pb = dn[gi]; gi += 1
                        tmp, btmp = W3["tmp_rot"].next()
                        OP("dve", lambda e, pb=pb, e2=e2, tmp=tmp: e.tensor_tensor(out=tmp[:], in0=psb[pb][:, :], in1=g5[:, e2 * 512:(e2 + 1) * 512], op=ALU.mult),
                           reads=[PB[pb], B_g5], writes=[btmp])
                        OP("pool", lambda e, i=i, e2=e2, tmp=tmp: e.tensor_tensor(out=xt3v[:, i, e2 * 512:(e2 + 1) * 512], in0=xt3v[:, i, e2 * 512:(e2 + 1) * 512],
                                                                               in1=tmp[:], op=ALU.add), reads=[btmp, B_xt3], writes=[B_xt3])
                norm_to_hT(xt3v, B_xt3, 2, 2, 0, hT3v, B_hT3, W3)
                bg_ = gelu_stage(ti + 1) if ti + 1 < T // TN else []
                ffn(xt3v, B_xt3, 2, hT3v, B_hT3, wgu2v, B_wgu2, wdn2v, B_wdn2, g8, B_g8, W3, bg=bg_)
                while bg_:
                    bg_.pop(0)()
                xo = W3["xn"]
                for i in range(2):
                    OP("act", lambda e, i=i: e.activation(out=xo[:, i, :], in_=xt3v[:, i, :], func=AF.Square, accum_out=W3["ssq"][:, i:i + 1]),
                       reads=[B_xt3], writes=[W3["B_xn"][i], W3["B_ssq"]])
                OP("act", lambda e: e.activation(out=W3["rstd"][:, 0:2], in_=W3["ssq"][:, 0:2], func=AF.Sqrt, scale=1.0 / D, bias=W3["eps"][:, 0:1]),
                   reads=[W3["B_ssq"], W3["B_eps"]], writes=[W3["B_rstd"]])
                OP("dve", lambda e: e.reciprocal(out=W3["rstd"][:, 0:2], in_=W3["rstd"][:, 0:2]), reads=[W3["B_rstd"]], writes=[W3["B_rstd"]])
                for i in range(2):
                    OP("dve", lambda e, i=i: e.scalar_tensor_tensor(out=xo[:, i, :], in0=xt3v[:, i, :], scalar=W3["rstd"][:, i:i + 1], in1=gfb[:],
                                                                    op0=ALU.mult, op1=ALU.mult), reads=[B_xt3, W3["B_rstd"], B_gfb], writes=[W3["B_xn"][i]])
                out_dmas.append(OP("pool", lambda e, t0=t0: e.dma_start(out=out_d[t0:t0 + TN, :].rearrange("(i p) d -> p i d", p=128), in_=xo[:]),
                                   reads=W3["B_xn"], dma=True))
        S.barrier()
        OP("sp", lambda e: e.nop(), extra=out_dmas)
        S.emit(nc)
    return nc, {}


def _prep_inputs(inputs):
    f = lambda a: np.ascontiguousarray(np.asarray(a, dtype=np.float32))
    shared = {
        "c_ctx": f(inputs["c_ctx"]).reshape(1, D),
        "w_mod": f(inputs["w_mod"][0]), "b_mod": f(inputs["b_mod"][0]).reshape(1, -1),
        "g_ffn1": f(inputs["g_ffn1"][0]).reshape(1, -1), "w_gu1": f(inputs["w_gu1"][0]), "w_down1": f(inputs["w_down1"][0]),
        "g_mix": f(inputs["g_mix"][0]).reshape(1, -1), "w_in": f(inputs["w_in"][0]),
        "g_cq": f(inputs["g_cq"][0]).reshape(1, -1), "w_uq": f(inputs["w_uq"][0]),
        "g_ckv": f(inputs["g_ckv"][0]).reshape(1, -1), "w_ukv": f(inputs["w_ukv"][0]),
        "lam_re": f(inputs["lam_re"][0]).reshape(32, 64), "lam_im": f(inputs["lam_im"][0]).reshape(32, 64),
        "log_dt": f(inputs["log_dt"][0]).reshape(1, 32),
        "b_re": f(inputs["b_re"][0]).reshape(32, 64, 16), "b_im": f(inputs["b_im"][0]).reshape(32, 64, 16),
        "c_re": f(inputs["c_re"][0]).reshape(512, 64), "c_im": f(inputs["c_im"][0]).reshape(512, 64),
        "d_skip": f(inputs["d_skip"][0]).reshape(1, -1), "w_glu": f(inputs["w_glu"][0]),
        "g_mla_out": f(inputs["g_mla_out"][0]).reshape(1, -1), "g_ssm_out": f(inputs["g_ssm_out"][0]).reshape(1, -1),
        "w_out": f(inputs["w_out"][0]), "g_ffn2": f(inputs["g_ffn2"][0]).reshape(1, -1),
        "w_gu2": f(inputs["w_gu2"][0]), "w_down2": f(inputs["w_down2"][0]),
        "g_final": f(inputs["g_final"]).reshape(1, -1),
    }
    x = f(inputs["x"]); c = f(inputs["c"]); ctx = f(inputs["ctx"])
    maps = []
    for b in range(8):
        m = dict(shared)
        m["x"] = x[b]; m["c"] = c[b].reshape(1, D); m["ctx"] = ctx[b]
        maps.append(m)
    return maps


def kernel(**inputs):
    nc, _ = build_program()
    in_maps = _prep_inputs(inputs)
    res = run_bass_kernel_spmd(nc, in_maps, core_ids=list(range(8)))
    out = np.stack([np.asarray(r["out"], dtype=np.float32) for r in res.results], axis=0)
    return out
```

```python
import contextlib
import math
import numpy as np
import concourse.bass as bass
import concourse.mybir as mybir
from concourse.bass_utils import run_bass_kernel_spmd

F32 = mybir.dt.float32
BF16 = mybir.dt.bfloat16
I32 = mybir.dt.int32
ALU = mybir.AluOpType
AF = mybir.ActivationFunctionType

ENGS = ("sp", "act", "pool", "dve", "pe")
N_DMA_SEMS = 24

D = 1024
T = 4096
NCTX = 256
DFF = 2816
NFC = 22
H = 12
EPS = 1e-6
ATTN_SCALE = 96 ** -0.5
TN = 256
NJ = 576
TWO_PI = 2.0 * math.pi


class Buf:
    __slots__ = ("name", "w", "r")

    def __init__(self, name=""):
        self.name = name
        self.w = None
        self.r = []


class Sched:
    def __init__(self):
        self.ops = []
        self.last = {e: None for e in ENGS}
        self.dmas_since = []

    def op(self, eng, fn, reads=(), writes=(), dma=False, extra=()):
        i = len(self.ops)
        deps = set(extra)
        for b in reads:
            if b.w is not None:
                deps.add(b.w)
        for b in writes:
            if b.w is not None:
                deps.add(b.w)
            for r in b.r:
                deps.add(r)
        for b in reads:
            b.r.append(i)
        for b in writes:
            b.w = i
            b.r = []
        deps.discard(i)
        self.ops.append(dict(eng=eng, fn=fn, deps=deps, dma=dma))
        self.last[eng] = i
        if dma:
            self.dmas_since.append(i)
        return i

    def barrier(self):
        deps = [v for v in self.last.values() if v is not None] + list(self.dmas_since)
        self.dmas_since = []
        ids = []
        for e in ENGS:
            ids.append(self.op(e, lambda eng: eng.nop(), extra=deps))
        return ids

    def emit(self, nc):
        ops = self.ops
        n = len(ops)
        needed = [False] * n
        for i, o in enumerate(ops):
            keep = set()
            for d in o["deps"]:
                od = ops[d]
                if od["eng"] == "pe" and o["eng"] == "pe" and not od["dma"] and not o["dma"]:
                    continue
                keep.add(d)
            o["deps"] = keep
            for d in keep:
                needed[d] = True
        cnt = {e: 0 for e in ENGS}
        dma_idx = {e: 0 for e in ENGS}
        for i, o in enumerate(ops):
            if o["dma"]:
                k = dma_idx[o["eng"]]
                dma_idx[o["eng"]] += 1
                o["dsem"] = k % N_DMA_SEMS
                o["dval"] = 16 * (k // N_DMA_SEMS + 1)
            elif needed[i]:
                cnt[o["eng"]] += 1
                o["seq"] = cnt[o["eng"]]
        with contextlib.ExitStack() as st:
            esem = {e: st.enter_context(nc.semaphore(f"s_{e}")) for e in ENGS}
            dsem = {e: [st.enter_context(nc.semaphore(f"d_{e}{k}")) for k in range(N_DMA_SEMS)]
                    for e in ("sp", "pool")}
            block = st.enter_context(nc.Block())
            hook = dict(sp=block.sync, act=block.scalar, pool=block.gpsimd, dve=block.vector, pe=block.tensor)

            def run_engine(ename):
                def body(eng):
                    waited = {}
                    for i, o in enumerate(ops):
                        if o["eng"] != ename:
                            continue
                        for d in sorted(o["deps"]):
                            od = ops[d]
                            if od["dma"]:
                                key = ("d", od["eng"], od["dsem"])
                                val = od["dval"]
                                sem = dsem[od["eng"]][od["dsem"]]
                            else:
                                key = ("e", od["eng"])
                                val = od["seq"]
                                sem = esem[od["eng"]]
                            if waited.get(key, 0) >= val:
                                continue
                            eng.wait_ge(sem, val)
                            waited[key] = val
                        if o["dma"]:
                            key = ("d", ename, o["dsem"])
                            pv = o["dval"] - 16
                            if pv > 0 and waited.get(key, 0) < pv:
                                eng.wait_ge(dsem[ename][o["dsem"]], pv)
                                waited[key] = pv
                            ins = o["fn"](eng)
                            ins.then_inc(dsem[ename][o["dsem"]], 16)
                        else:
                            ins = o["fn"](eng)
                            if needed[i]:
                                ins.then_inc(esem[ename], 1)
                return body

            for e in ENGS:
                hook[e](run_engine(e))


class Rot:
    def __init__(self, items):
        self.items = list(items)
        self.i = 0

    def next(self):
        v = self.items[self.i % len(self.items)]
        self.i += 1
        return v


def build_program(dbg=None):
    nc = bass.Bass("TRN2", target_bir_lowering=False)
    S = Sched()
    OP = S.op
    din = {}

    def inp(name, shape):
        din[name] = nc.dram_tensor(name, list(shape), F32, kind="ExternalInput").ap()
        return din[name]

    x_d = inp("x", [T, D]); c_d = inp("c", [1, D]); ctx_d = inp("ctx", [NCTX, D]); cctx_d = inp("c_ctx", [1, D])
    wmod_d = inp("w_mod", [D, 9 * D]); bmod_d = inp("b_mod", [1, 9 * D])
    gffn1_d = inp("g_ffn1", [1, D]); wgu1_d = inp("w_gu1", [D, 2 * DFF]); wdn1_d = inp("w_down1", [DFF, D])
    gmix_d = inp("g_mix", [1, D]); win_d = inp("w_in", [D, 672]); gcq_d = inp("g_cq", [1, 256])
    wuq_d = inp("w_uq", [256, 1152]); gckv_d = inp("g_ckv", [1, 128]); wukv_d = inp("w_ukv", [128, 1536])
    lamre_d = inp("lam_re", [32, 64]); lamim_d = inp("lam_im", [32, 64]); logdt_d = inp("log_dt", [1, 32])
    bre_d = inp("b_re", [32, 64, 16]); bim_d = inp("b_im", [32, 64, 16])
    cre_d = inp("c_re", [512, 64]); cim_d = inp("c_im", [512, 64])
    dskip_d = inp("d_skip", [1, 256]); wglu_d = inp("w_glu", [256, 512])
    gmla_d = inp("g_mla_out", [1, 768]); gssm_d = inp("g_ssm_out", [1, 256]); wout_d = inp("w_out", [D, D])
    gffn2_d = inp("g_ffn2", [1, D]); wgu2_d = inp("w_gu2", [D, 2 * DFF]); wdn2_d = inp("w_down2", [DFF, D])
    gfin_d = inp("g_final", [1, D])
    out_d = nc.dram_tensor("out", [T, D], F32, kind="ExternalOutput").ap()

    def scratch(name, shape, dt):
        return nc.dram_tensor(name, list(shape), dt, kind=("ExternalOutput" if dbg else "Internal")).ap()

    x1_s = scratch("x1_s", [T, D], F32)
    cqn_s = scratch("cqn_s", [256, T], BF16)
    ckvn_s = scratch("ckvn_s", [128, T + NCTX], BF16)
    kr_s = scratch("kr_s", [2, 32, T + NCTX], F32)
    u_s = scratch("u_s", [8, 256, NJ], BF16)
    y_s = scratch("y_s", [8, 256, 512], F32)
    attn_s = scratch("attn_s", [768, T], BF16)
    mrow_s = scratch("mrow_s", [2, 9 * D], F32)
    B_x1 = Buf(); B_cqn = Buf(); B_ckvn = Buf(); B_kr = Buf(); B_u = Buf(); B_y = Buf(); B_attn = Buf(); B_mrow = Buf()

    dbg_out = {}

    def dbg_dump(name, src_ap, shape, dt, buf):
        t = nc.dram_tensor("dbg_" + name, list(shape), dt, kind="ExternalOutput").ap()
        dbg_out[name] = OP("sp", lambda e: e.dma_start(out=t, in_=src_ap), reads=[buf], dma=True)

    out_dmas = []
    pst = contextlib.ExitStack()
    with pst:
        _names = {}

        def sb(st, name, shape, dt):
            _names[name] = _names.get(name, 0) + 1
            if _names[name] > 1:
                name = f"{name}_v{_names[name]}"
            return st.enter_context(nc.sbuf_tensor(name, list(shape), dt))

        psall = pst.enter_context(nc.psum_tensor("psall", [128, 4096], F32))
        psb = [psall[:, i * 512:(i + 1) * 512] for i in range(8)]
        PB = [Buf(f"ps{i}") for i in range(8)]

        ident = sb(pst, "ident", [128, 128], F32); B_ident = Buf()
        ones = sb(pst, "ones", [128, 128], F32); B_ones = Buf()
        onesb = sb(pst, "onesb", [128, 128], BF16); B_onesb = Buf()
        iot = sb(pst, "iot", [128, 128], F32); B_iot = Buf()
        sel = sb(pst, "sel", [2, 2, 128], F32); B_sel = Buf()
        modfm = sb(pst, "modfm", [128, 72, 2], F32); B_modfm = Buf()
        AB = sb(pst, "AB", [128, 3, 2, 2, 8], F32); B_AB = Buf()
        gfm = sb(pst, "gfm", [128, 3, 8], F32); B_gfm = Buf()
        gsm = sb(pst, "gsm", [128, 4, 2], F32); B_gsm = Buf()
        gmla = sb(pst, "gmla", [128, 6], F32); B_gmla = Buf()
        scs = sb(pst, "scs", [128, 8, 2], F32); B_scs = Buf()
        rbsh = sb(pst, "rbsh", [2, 512], F32); B_rbsh = Buf()
        halfpi = sb(pst, "halfpi", [128, 1], F32); B_halfpi = Buf()
        OP("dve", lambda e: e.memset(halfpi[:], math.pi / 2.0), writes=[B_halfpi])

        OP("pool", lambda e: e.iota(iot[:], pattern=[[1, 128]], base=0, channel_multiplier=-1, allow_small_or_imprecise_dtypes=True), writes=[B_iot])
        OP("dve", lambda e: e.tensor_single_scalar(out=ident[:], in_=iot[:], scalar=0.0, op=ALU.is_equal),
           reads=[B_iot], writes=[B_ident])
        OP("dve", lambda e: e.memset(ones[:], 1.0), writes=[B_ones])
        OP("dve", lambda e: e.memset(onesb[:], 1.0), writes=[B_onesb])
        OP("dve", lambda e: e.tensor_copy(out=sel[0:2, 0, :], in_=ident[0:2, 0:1].to_broadcast([2, 128])),
           reads=[B_ident], writes=[B_sel])
        OP("dve", lambda e: e.tensor_copy(out=sel[0:2, 1, :], in_=ident[0:2, 1:2].to_broadcast([2, 128])),
           reads=[B_ident], writes=[B_sel])

        def small_fm_load(dst_ap, src_row_ap, nk, buf):
            OP("sp", lambda e: e.dma_start(out=dst_ap, in_=src_row_ap.rearrange("o (k p) -> p (o k)", p=128),
                                           allow_slow_non_contiguous=True), writes=[buf], dma=True)

        small_fm_load(gfm[:, 0, :], gffn1_d, 8, B_gfm)
        small_fm_load(gfm[:, 1, :], gmix_d, 8, B_gfm)
        small_fm_load(gfm[:, 2, :], gffn2_d, 8, B_gfm)
        small_fm_load(gsm[:, 0, :], gcq_d, 2, B_gsm)
        small_fm_load(gsm[:, 1, 0:1], gckv_d, 1, B_gsm)
        small_fm_load(gsm[:, 2, :], gssm_d, 2, B_gsm)
        small_fm_load(gmla[:, :], gmla_d, 6, B_gmla)

        def load_w_bf16(st, name, src, nk, ncols, chunk_cols=2048):
            t = sb(st, name, [128, nk, ncols], BF16)
            bufs = [Buf() for _ in range(nk)]
            v = src.rearrange("(k p) n -> p k n", p=128)
            for k in range(nk):
                OP("pool", lambda e, k=k: e.dma_start(out=t[:, k, :], in_=v[:, k, :], max_dma_last_dim=chunk_cols * 4),
                   writes=[bufs[k]], dma=True)
            return t, bufs

        stw1 = contextlib.ExitStack()
        wgu, B_wgu = load_w_bf16(stw1, "wgu1", wgu1_d, 8, 2 * DFF)
        wdn, B_wdn = load_w_bf16(stw1, "wdn1", wdn1_d, NFC, D)
        win, B_win = load_w_bf16(stw1, "win", win_d, 8, 672)
        wmod_v = wmod_d.rearrange("(k p) n -> p k n", p=128)

        def adaln_issue(blk, w, bw, bm, bb):
            cs = slice(blk * 512, (blk + 1) * 512)
            OP("sp", lambda e: e.dma_start(out=w[:], in_=wmod_v[:, :, cs]), writes=[bw], dma=True)
            OP("sp", lambda e: e.dma_start(out=bm[:], in_=bmod_d[:, cs]), writes=[bb], dma=True)

        def adaln_block(blk, w, bw, bm, bb, rb, brb, issue=True):
            cs = slice(blk * 512, (blk + 1) * 512)
            if issue:
                adaln_issue(blk, w, bw, bm, bb)
            for k in range(8):
                OP("pe", lambda e, k=k: e.matmul(psb[0][0:2, :], lhsT=scs[:, k, :], rhs=w[:, k, :], start=(k == 0), stop=False),
                   reads=[B_scs, bw], writes=[PB[0]])
            OP("pe", lambda e: e.matmul(psb[0][0:2, :], lhsT=ones[0:1, 0:2], rhs=bm[0:1, :], start=False, stop=True),
               reads=[B_ones, bb], writes=[PB[0]])
            OP("act", lambda e: e.activation(out=rb[:], in_=psb[0][0:2, :], func=AF.Copy), reads=[PB[0]], writes=[brb])
            OP("sp", lambda e: e.dma_start(out=mrow_s[:, cs], in_=rb[:]), reads=[brb], writes=[B_mrow], dma=True)
            for q in range(4):
                OP("pe", lambda e, q=q: e.matmul(psb[1][:, 2 * q:2 * q + 2], lhsT=rb[0:2, q * 128:(q + 1) * 128], rhs=ident[0:2, 0:2],
                                                 start=True, stop=True), reads=[brb, B_ident], writes=[PB[1]])
            OP("dve", lambda e: e.tensor_copy(out=modfm[:, blk * 4:(blk + 1) * 4, :], in_=psb[1][:, 0:8].rearrange("p (q r) -> p q r", r=2)),
               reads=[PB[1]], writes=[B_modfm])

        def adaln_AB(sites):
            for site, (shv, scv) in sites:
                for r in range(2):
                    OP("dve", lambda e, site=site, scv=scv, r=r: e.scalar_tensor_tensor(
                        out=AB[:, site, 0, r, :], in0=modfm[:, scv * 8:(scv + 1) * 8, r], scalar=1.0,
                        in1=gfm[:, site, :], op0=ALU.add, op1=ALU.mult), reads=[B_modfm, B_gfm], writes=[B_AB])
                    OP("dve", lambda e, site=site, shv=shv, r=r: e.tensor_copy(
                        out=AB[:, site, 1, r, :], in_=modfm[:, shv * 8:(shv + 1) * 8, r]), reads=[B_modfm], writes=[B_AB])

        NBLK0 = 10
        with contextlib.ExitStack() as st0:
            scr = sb(st0, "scr", [128, 8, 2], F32); B_scr = Buf()
            wmb = [sb(st0, f"wmb{i}", [128, 8, 512], F32) for i in range(2)]; B_wmb = [Buf(), Buf()]
            bmb = [sb(st0, f"bmb{i}", [1, 512], F32) for i in range(2)]; B_bmb = [Buf(), Buf()]
            rowb = [sb(st0, f"rowb{i}", [2, 512], F32) for i in range(2)]; B_rowb = [Buf(), Buf()]
            OP("sp", lambda e: e.dma_start(out=scr[:, :, 0], in_=c_d.rearrange("o (k p) -> p (o k)", p=128),
                                           allow_slow_non_contiguous=True), writes=[B_scr], dma=True)
            OP("sp", lambda e: e.dma_start(out=scr[:, :, 1], in_=cctx_d.rearrange("o (k p) -> p (o k)", p=128),
                                           allow_slow_non_contiguous=True), writes=[B_scr], dma=True)
            OP("act", lambda e: e.activation(out=scs[:], in_=scr[:], func=AF.Silu), reads=[B_scr], writes=[B_scs])
            for blk in range(NBLK0):
                adaln_block(blk, wmb[blk % 2], B_wmb[blk % 2], bmb[blk % 2], B_bmb[blk % 2], rowb[blk % 2], B_rowb[blk % 2])
            adaln_AB([(0, (0, 1)), (1, (3, 4))])
        S.barrier()


        def bcast_rows(st, name, col0, r, scale):
            t = sb(st, name, [128, D], F32); bt = Buf()
            for hf in range(2):
                cs = slice(col0 + hf * 512, col0 + (hf + 1) * 512)
                OP("sp", lambda e, cs=cs: e.dma_start(out=rbsh[:], in_=mrow_s[:, cs]), reads=[B_mrow],
                   writes=[B_rbsh], dma=True)
                OP("pe", lambda e: e.matmul(psb[0][:, :], lhsT=sel[0:2, r, :], rhs=rbsh[0:2, :],
                                            start=True, stop=True), reads=[B_rbsh, B_sel], writes=[PB[0]])
                OP("act", lambda e, hf=hf: e.activation(out=t[:, hf * 512:(hf + 1) * 512], in_=psb[0][:, :],
                                                        func=AF.Copy, scale=scale), reads=[PB[0]], writes=[bt])
            return t, bt

        def bcast_vec(st, name, src_row, n):
            t = sb(st, name, [128, n], F32); bt = Buf()
            for c0 in range(0, n, 512):
                c1 = min(n, c0 + 512)
                OP("sp", lambda e, c0=c0, c1=c1: e.dma_start(out=rbsh[0:1, 0:c1 - c0], in_=src_row[:, c0:c1]),
                   writes=[B_rbsh], dma=True)
                OP("pe", lambda e, c0=c0, c1=c1: e.matmul(psb[0][:, 0:c1 - c0], lhsT=ones[0:1, :], rhs=rbsh[0:1, 0:c1 - c0],
                                                            start=True, stop=True), reads=[B_rbsh, B_ones], writes=[PB[0]])
                OP("act", lambda e, c0=c0, c1=c1: e.activation(out=t[:, c0:c1], in_=psb[0][:, 0:c1 - c0], func=AF.Copy),
                   reads=[PB[0]], writes=[bt])
            return t, bt

        def norm_to_hT(xt, B_xt, nsub, site, r, hT, B_hT, W, part=0):
            ntok = nsub * 128
            if part in (0, 1):
                norm_p1(xt, B_xt, nsub, W)
            if part in (0, 2):
                norm_p2(nsub, site, r, hT, B_hT, W)

        def norm_p1(xt, B_xt, nsub, W):
            for i in range(nsub):
                OP("act", lambda e, i=i: e.activation(out=W["xn"][:, i, :], in_=xt[:, i, :], func=AF.Square,
                                                      accum_out=W["ssq"][:, i:i + 1]),
                   reads=[B_xt], writes=[W["B_xn"][i], W["B_ssq"]])
            OP("act", lambda e: e.activation(out=W["rstd"][:, 0:nsub], in_=W["ssq"][:, 0:nsub], func=AF.Sqrt,
                                             scale=1.0 / D, bias=W["eps"][:, 0:1]),
               reads=[W["B_ssq"], W["B_eps"]], writes=[W["B_rstd"]])
            OP("dve", lambda e: e.reciprocal(out=W["rstd"][:, 0:nsub], in_=W["rstd"][:, 0:nsub]),
               reads=[W["B_rstd"]], writes=[W["B_rstd"]])
            for i in range(nsub):
                xn = W["xn"]
                OP("dve", lambda e, i=i: e.tensor_scalar(out=xn[:, i, :], in0=xt[:, i, :], scalar1=W["rstd"][:, i:i + 1],
                                                         scalar2=None, op0=ALU.mult),
                   reads=[B_xt, W["B_rstd"]], writes=[W["B_xn"][i]])

        def norm_p2(nsub, site, r, hT, B_hT, W):
            ntok = nsub * 128
            for k in range(8):
                pb = W["tp_rot"].next()
                for i in range(nsub):
                    OP("pe", lambda e, i=i, k=k, pb=pb: e.transpose(out=psb[pb][:, i * 128:(i + 1) * 128],
                                                                     in_=W["xn"][:, i, k * 128:(k + 1) * 128],
                                                                     identity=ident[:]),
                       reads=[W["B_xn"][i], B_ident], writes=[PB[pb]])
                OP("act", lambda e, k=k, pb=pb: e.activation(out=hT[:, k, 0:ntok], in_=psb[pb][:, 0:ntok],
                                                             func=AF.Identity, scale=AB[:, site, 0, r, k:k + 1],
                                                             bias=AB[:, site, 1, r, k:k + 1]),
                   reads=[PB[pb], B_AB], writes=[B_hT[k]])

        def ffn(xt, B_xt, nsub, hT, B_hT, wgu, B_wgu, wdn, B_wdn, gbc, B_gbc, W, bg=None):
            ntok = nsub * 128
            gu_rot = W["gu_rot"]
            dn = W["dn_banks"]
            pend = []

            def down(c, hid, bh):
                gi = 0
                for i in range(nsub):
                    for e2 in range(2):
                        pb = dn[gi]; gi += 1
                        OP("pe", lambda e, i=i, e2=e2, pb=pb, c=c, hid=hid: e.matmul(
                            psb[pb][:, :], lhsT=hid[:, i * 128:(i + 1) * 128], rhs=wdn[:, c, e2 * 512:(e2 + 1) * 512],
                            start=(c == 0), stop=(c == NFC - 1)),
                           reads=[bh, B_wdn[c]], writes=[PB[pb]])

            for c in range(NFC):
                pb = gu_rot.next()
                for half, col0 in ((0, c * 128), (1, DFF + c * 128)):
                    for k in range(8):
                        OP("pe", lambda e, k=k, pb=pb, half=half, col0=col0: e.matmul(
                            psb[pb][:, half * 256:half * 256 + ntok], lhsT=wgu[:, k, col0:col0 + 128],
                            rhs=hT[:, k, 0:ntok], start=(k == 0), stop=(k == 7)),
                           reads=[B_wgu[k], B_hT[k]], writes=[PB[pb]])
                sg, bsg = W["sg_rot"].next()
                hid, bh = W["hid_rot"].next()
                OP("act", lambda e, pb=pb, sg=sg: e.activation(out=sg[:, 0:ntok], in_=psb[pb][:, 0:ntok], func=AF.Silu),
                   reads=[PB[pb]], writes=[bsg])
                OP("dve", lambda e, pb=pb, sg=sg, hid=hid: e.tensor_tensor(out=hid[:, 0:ntok], in0=psb[pb][:, 256:256 + ntok],
                                                                           in1=sg[:, 0:ntok], op=ALU.mult),
                   reads=[PB[pb], bsg], writes=[bh])
                pend.append((c, hid, bh))
                if len(pend) > 2:
                    down(*pend.pop(0))
                for _ in range(2):
                    if bg:
                        bg.pop(0)()
            while pend:
                down(*pend.pop(0))
            gi = 0
            for i in range(nsub):
                for e2 in range(2):
                    pb = dn[gi]; gi += 1
                    tmp, btmp = W["tmp_rot"].next()
                    OP("dve", lambda e, pb=pb, e2=e2, tmp=tmp: e.tensor_tensor(
                        out=tmp[:], in0=psb[pb][:, :], in1=gbc[:, e2 * 512:(e2 + 1) * 512], op=ALU.mult),
                       reads=[PB[pb], B_gbc], writes=[btmp])
                    OP("pool" if gi % 2 else "dve", lambda e, i=i, e2=e2, tmp=tmp: e.tensor_tensor(
                        out=xt[:, i, e2 * 512:(e2 + 1) * 512], in0=xt[:, i, e2 * 512:(e2 + 1) * 512], in1=tmp[:],
                        op=ALU.add),
                       reads=[btmp, B_xt], writes=[B_xt])

        def common_work(st):
            W = {}
            W["ssq"] = sb(st, "ssq", [128, 2], F32); W["B_ssq"] = Buf()
            W["rstd"] = sb(st, "rstd", [128, 2], F32); W["B_rstd"] = Buf()
            W["eps"] = sb(st, "epsc", [128, 1], F32); W["B_eps"] = Buf()
            OP("dve", lambda e: e.memset(W["eps"][:], EPS), writes=[W["B_eps"]])
            W["xn"] = sb(st, "xn", [128, 2, D], F32); W["B_xn"] = [Buf(), Buf()]
            W["tp_rot"] = Rot([0, 1])
            W["gu_rot"] = Rot([2, 3])
            W["dn_banks"] = [4, 5, 6, 7]
            sgs = [(sb(st, f"sg{i}", [128, TN], F32), Buf()) for i in range(2)]
            W["sg_rot"] = Rot(sgs)
            hids = [(sb(st, f"hid{i}", [128, TN], BF16), Buf()) for i in range(4)]
            W["hid_rot"] = Rot(hids)
            tmps = [(sb(st, f"tmpr{i}", [128, 512], F32), Buf()) for i in range(2)]
            W["tmp_rot"] = Rot(tmps)
            return W

        with contextlib.ExitStack() as st1:
            B_win1 = Buf()
            winsw = sb(st1, "winsw", [128, 8, 32], BF16); B_winsw = Buf()
            for k in range(8):
                OP("dve", lambda e, k=k: e.tensor_scalar(
                    out=winsw[:, k, :].rearrange("p (i two) -> p i two", two=2)[:, :, 0],
                    in0=win[:, k, 384:416].rearrange("p (i two) -> p i two", two=2)[:, :, 1],
                    scalar1=-1.0, scalar2=None, op0=ALU.mult), reads=[B_win[k]], writes=[B_winsw])
                OP("dve", lambda e, k=k: e.tensor_copy(
                    out=winsw[:, k, :].rearrange("p (i two) -> p i two", two=2)[:, :, 1],
                    in_=win[:, k, 384:416].rearrange("p (i two) -> p i two", two=2)[:, :, 0]),
                   reads=[B_win[k]], writes=[B_winsw])
            g2x, B_g2x = bcast_rows(st1, "g2x", 2 * D, 0, 0.5)
            g2c, B_g2c = bcast_rows(st1, "g2c", 2 * D, 1, 0.5)
            W = common_work(st1)
            xts = [(sb(st1, "xtA", [128, 2, D], F32), Buf()), (sb(st1, "xtB", [128, 2, D], F32), Buf())]

            def load_x(ti):
                t0_ = 0 if ti == 0 else (ti - 1) * TN
                src_ = ctx_d if ti == 0 else x_d
                xt_l, B_l = xts[ti % 2]
                OP("sp", lambda e, src_=src_, t0_=t0_, xt_l=xt_l: e.dma_start(
                    out=xt_l[:], in_=src_[t0_:t0_ + TN, :].rearrange("(i p) d -> p i d", p=128)), writes=[B_l], dma=True)
            hT = sb(st1, "hT", [128, 8, TN], BF16); B_hT = [Buf() for _ in range(8)]
            cqf = sb(st1, "cqf", [128, 3, TN], F32); B_cqf = [Buf() for _ in range(3)]
            sqf = sb(st1, "sqf", [128, 3, TN], F32); B_sqf = [Buf() for _ in range(3)]
            rbc = sb(st1, "rbc", [128, 2, TN], F32); B_rbc = [Buf(), Buf()]
            cqn_t = sb(st1, "cqn_t", [128, 3, TN], BF16); B_cqn_t = [Buf() for _ in range(3)]
            krt = sb(st1, "krt", [128, 2, TN], F32); B_krt = Buf()
            usj = sb(st1, "usj", [128, 2, 8, TN // 8], BF16); B_usj = [Buf(), Buf()]
            print("phase1 sbuf remaining", nc.sbuf_bytes_remaining)
            pj_rot = Rot([2, 3, 4, 5, 6, 7])
            ntiles = 1 + T // TN
            load_x(0)
            for ti in range(ntiles):
                is_ctx = ti == 0
                r = 1 if is_ctx else 0
                t0 = 0 if is_ctx else (ti - 1) * TN
                kcol0 = 0 if is_ctx else NCTX + t0
                if ti + 1 < ntiles:
                    load_x(ti + 1)
                xt, B_xt = xts[ti % 2]
                norm_to_hT(xt, B_xt, 2, 0, r, hT, B_hT, W, part=(0 if ti == 0 else 2))
                ffn(xt, B_xt, 2, hT, B_hT, wgu, B_wgu, wdn, B_wdn, g2c if is_ctx else g2x, B_g2c if is_ctx else B_g2x, W)
                if not is_ctx:
                    OP("pool", lambda e, t0=t0, xt=xt: e.dma_start(
                        out=x1_s[t0:t0 + TN, :].rearrange("(i p) d -> p i d", p=128), in_=xt[:]),
                       reads=[B_xt], writes=[B_x1], dma=True)
                norm_to_hT(xt, B_xt, 2, 1, r, hT, B_hT, W)
                if ti + 1 < ntiles:
                    xt_n, B_xt_n = xts[(ti + 1) % 2]
                    norm_to_hT(xt_n, B_xt_n, 2, 0, 0, hT, B_hT, W, part=1)
                chunks = ([] if is_ctx else [(0, 0), (1, 128)]) + [(2, 256)]
                for ci, col0 in chunks:
                    pb = pj_rot.next()
                    for k in range(8):
                        OP("pe", lambda e, k=k, pb=pb, col0=col0: e.matmul(
                            psb[pb][:, 0:TN], lhsT=win[:, k, col0:col0 + 128], rhs=hT[:, k, :],
                            start=(k == 0), stop=(k == 7)), reads=[B_win[k], B_hT[k]], writes=[PB[pb]])
                    OP("act", lambda e, pb=pb, ci=ci: e.activation(out=cqf[:, ci, :], in_=psb[pb][:, 0:TN], func=AF.Copy),
                       reads=[PB[pb]], writes=[B_cqf[ci]])
                    OP("dve", lambda e, ci=ci: e.tensor_tensor(out=sqf[:, ci, :], in0=cqf[:, ci, :], in1=cqf[:, ci, :],
                                                               op=ALU.mult), reads=[B_cqf[ci]], writes=[B_sqf[ci]])
                groups = ([] if is_ctx else [((0, 1), 0, 256.0, 0)]) + [((2,), 1, 128.0, 1)]
                for cis, ri, nfeat, gi in groups:
                    pb = pj_rot.next()
                    for n_, ci in enumerate(cis):
                        OP("pe", lambda e, pb=pb, ci=ci, n_=n_, cis=cis: e.matmul(
                            psb[pb][:, 0:TN], lhsT=ones[:, :], rhs=sqf[:, ci, :], start=(n_ == 0),
                            stop=(n_ == len(cis) - 1)), reads=[B_ones, B_sqf[ci]], writes=[PB[pb]])
                    OP("act", lambda e, pb=pb, ri=ri, nfeat=nfeat: e.activation(
                        out=rbc[:, ri, :], in_=psb[pb][:, 0:TN], func=AF.Sqrt, scale=1.0 / nfeat, bias=W["eps"][:, 0:1]),
                       reads=[PB[pb], W["B_eps"]], writes=[B_rbc[ri]])
                    OP("dve", lambda e, ri=ri: e.reciprocal(out=rbc[:, ri, :], in_=rbc[:, ri, :]),
                       reads=[B_rbc[ri]], writes=[B_rbc[ri]])
                    for n_, ci in enumerate(cis):
                        OP("dve", lambda e, ci=ci, ri=ri, gi=gi, n_=n_: e.scalar_tensor_tensor(
                            out=cqn_t[:, ci, :], in0=cqf[:, ci, :], scalar=gsm[:, gi, n_:n_ + 1], in1=rbc[:, ri, :],
                            op0=ALU.mult, op1=ALU.mult), reads=[B_cqf[ci], B_gsm, B_rbc[ri]], writes=[B_cqn_t[ci]])
                        if ci < 2:
                            OP("pool", lambda e, ci=ci, t0=t0: e.dma_start(
                                out=cqn_s[ci * 128:(ci + 1) * 128, t0:t0 + TN], in_=cqn_t[:, ci, :]),
                               reads=[B_cqn_t[ci]], writes=[B_cqn], dma=True)
                        else:
                            OP("pool", lambda e, kcol0=kcol0: e.dma_start(
                                out=ckvn_s[:, kcol0:kcol0 + TN], in_=cqn_t[:, 2, :]),
                               reads=[B_cqn_t[2]], writes=[B_ckvn], dma=True)
                pb = pj_rot.next()
                for k in range(8):
                    OP("pe", lambda e, k=k, pb=pb: e.matmul(psb[pb][64:96, 0:TN], lhsT=win[:, k, 384:416], rhs=hT[:, k, :],
                                                            start=(k == 0), stop=(k == 7)),
                       reads=[B_win[k], B_hT[k]], writes=[PB[pb]])
                for k in range(8):
                    OP("pe", lambda e, k=k, pb=pb: e.matmul(psb[pb][64:96, 256:256 + TN], lhsT=winsw[:, k, :], rhs=hT[:, k, :],
                                                            start=(k == 0), stop=(k == 7)),
                       reads=[B_winsw, B_hT[k]], writes=[PB[pb]])
                OP("act", lambda e, pb=pb: e.activation(
                    out=krt[64:96, :, :], in_=psb[pb][64:96, :].rearrange("p (a t) -> p a t", a=2), func=AF.Copy),
                   reads=[PB[pb]], writes=[B_krt])
                for a in range(2):
                    OP("pool", lambda e, a=a, kcol0=kcol0: e.dma_start(out=kr_s[a, :, kcol0:kcol0 + TN], in_=krt[64:96, a, :]),
                       reads=[B_krt], writes=[B_kr], dma=True)
                for gc in range(2):
                    pb = pj_rot.next()
                    col0 = 416 + gc * 128
                    for k in range(8):
                        OP("pe", lambda e, k=k, pb=pb, col0=col0: e.matmul(
                            psb[pb][:, 0:TN], lhsT=win[:, k, col0:col0 + 128], rhs=hT[:, k, :],
                            start=(k == 0), stop=(k == 7)), reads=[B_win[k], B_hT[k]], writes=[PB[pb]])
                    OP("act", lambda e, pb=pb, gc=gc: e.activation(
                        out=usj[:, gc, :, :], in_=psb[pb][:, 0:TN].rearrange("p (j s) -> p s j", s=8), func=AF.Copy),
                       reads=[PB[pb]], writes=[B_usj[gc]])
                    nj = TN // 8
                    j0s = [0, 544] if is_ctx else [32 + t0 // 8]
                    for j0 in j0s:
                        OP("pool", lambda e, gc=gc, j0=j0: e.dma_start(
                            out=u_s[:, gc * 128:(gc + 1) * 128, j0:j0 + nj].rearrange("s p j -> p s j"),
                            in_=usj[:, gc, :, :]), reads=[B_usj[gc]], writes=[B_u], dma=True)
        S.barrier()
        stw1.close()

        if dbg == "p1":
            OP("sp", lambda e: e.nop())
            S.emit(nc)
            return nc, {}

        TPS = 6.283185

        MAGIC = 12582912.0

        def frac_sincos(sin_t, cos_t, F_ap, T1, T2, bF, bT1, bT2, bsin, bcos, hp):
            OP("dve", lambda e: e.tensor_scalar(out=T1, in0=F_ap, scalar1=MAGIC, scalar2=None, op0=ALU.add), reads=[bF], writes=[bT1])
            OP("dve", lambda e: e.scalar_tensor_tensor(out=T2, in0=T1, scalar=MAGIC, in1=F_ap, op0=ALU.subtract, op1=ALU.subtract),
               reads=[bT1, bF], writes=[bT2])
            OP("act", lambda e: e.activation(out=sin_t, in_=T2, func=AF.Sin, scale=-TPS), reads=[bT2], writes=[bsin])
            OP("dve", lambda e: e.scalar_tensor_tensor(out=T1, in0=T2, scalar=-1.0, in1=T2, op0=ALU.mult, op1=ALU.max), reads=[bT2], writes=[bT1])
            OP("act", lambda e: e.activation(out=cos_t, in_=T1, func=AF.Sin, scale=-TPS, bias=hp), reads=[bT1, B_halfpi], writes=[bcos])

        with contextlib.ExitStack() as st2:
            Kmat = sb(st2, "Kmat", [128, 32, 128], BF16); B_Kmat = Buf()
            Smat = sb(st2, "Smat", [128, 32, 128], BF16); B_Smat = Buf()
            SmatT = sb(st2, "SmatT", [128, 32, 128], BF16); B_SmatT = Buf()
            Dm1 = sb(st2, "Dm1", [128, 32, 8, 16], BF16); B_Dm1 = Buf()
            Dm2 = sb(st2, "Dm2", [128, 32, 8, 16], BF16); B_Dm2 = Buf()
            FPHI = sb(st2, "FPHI", [128, 32], F32); B_FPHI = Buf()
            RHO = sb(st2, "RHO", [128, 32], F32); B_RHO = Buf()
            u8all = sb(st2, "u8all", [128, 16, NJ], BF16); B_u8 = Buf()
            for s_ in range(8):
                OP("sp", lambda e, s_=s_: e.dma_start(out=u8all[16 * s_:16 * s_ + 16, :, :],
                                                      in_=u_s[s_].rearrange("(g c) j -> c g j", c=16)),
                   reads=[B_u], writes=[B_u8], dma=True)
            with contextlib.ExitStack() as st2a:
                LRr = sb(st2a, "LRr", [32, 2, 64], F32); B_LRr = Buf()
                LIr = sb(st2a, "LIr", [32, 2, 64], F32); B_LIr = Buf()
                CRr = sb(st2a, "CRr", [128, 4, 2, 64], F32); B_CRr = Buf()
                CIr = sb(st2a, "CIr", [128, 4, 2, 64], F32); B_CIr = Buf()
                BR = sb(st2a, "BR", [128, 32, 16], F32); B_BR = Buf()
                BI = sb(st2a, "BI", [128, 32, 16], F32); B_BI = Buf()
                DSK = sb(st2a, "DSK", [128, 16], F32); B_DSK = Buf()
                LD = sb(st2a, "LD", [1, 32], F32); B_LD = Buf()
                for a in range(2):
                    OP("sp", lambda e, a=a: e.dma_start(out=LRr[:, a, :], in_=lamre_d), writes=[B_LRr], dma=True)
                    OP("sp", lambda e, a=a: e.dma_start(out=LIr[:, a, :], in_=lamim_d), writes=[B_LIr], dma=True)
                    OP("sp", lambda e, a=a: e.dma_start(out=CRr[:, :, a, :], in_=cre_d.rearrange("(q r) n -> r q n", r=128)),
                       writes=[B_CRr], dma=True)
                    OP("sp", lambda e, a=a: e.dma_start(out=CIr[:, :, a, :], in_=cim_d.rearrange("(q r) n -> r q n", r=128)),
                       writes=[B_CIr], dma=True)
                    OP("sp", lambda e, a=a: e.dma_start(out=BR[a * 64:(a + 1) * 64, :, :], in_=bre_d.rearrange("g n c -> n g c")),
                       writes=[B_BR], dma=True)
                    OP("sp", lambda e, a=a: e.dma_start(out=BI[a * 64:(a + 1) * 64, :, :], in_=bim_d.rearrange("g n c -> n g c")),
                       writes=[B_BI], dma=True)
                for s_ in range(8):
                    OP("sp", lambda e, s_=s_: e.dma_start(out=DSK[16 * s_:16 * s_ + 16, :],
                                                          in_=dskip_d.rearrange("o (g c) -> c (o g)", c=16),
                                                          allow_slow_non_contiguous=True), writes=[B_DSK], dma=True)
                OP("sp", lambda e: e.dma_start(out=LD[:], in_=logdt_d), writes=[B_LD], dma=True)

                def t32(name, shape=(128, 32)):
                    return sb(st2a, name, list(shape), F32), Buf()
                LR, B_LR = t32("LR"); LI, B_LI = t32("LI"); DT, B_DT = t32("DT")
                CR = sb(st2a, "CR", [128, 32, 16], F32); B_CR = Buf()
                CI = sb(st2a, "CI", [128, 32, 16], F32); B_CI = Buf()
                OP("pe", lambda e: e.transpose(out=psb[0][:, 0:32], in_=LRr[:].rearrange("r a n -> r (a n)"),
                                               identity=ident[0:32, 0:32]), reads=[B_LRr, B_ident], writes=[PB[0]])
                OP("pe", lambda e: e.transpose(out=psb[0][:, 32:64], in_=LIr[:].rearrange("r a n -> r (a n)"),
                                               identity=ident[0:32, 0:32]), reads=[B_LIr, B_ident], writes=[PB[0]])
                OP("pe", lambda e: e.matmul(psb[0][:, 64:96], lhsT=ones[0:1, :], rhs=LD[0:1, :], start=True, stop=True),
                   reads=[B_LD, B_ones], writes=[PB[0]])
                OP("dve", lambda e: e.tensor_copy(out=LR[:], in_=psb[0][:, 0:32]), reads=[PB[0]], writes=[B_LR])
                OP("dve", lambda e: e.tensor_copy(out=LI[:], in_=psb[0][:, 32:64]), reads=[PB[0]], writes=[B_LI])
                OP("act", lambda e: e.activation(out=DT[:], in_=psb[0][:, 64:96], func=AF.Exp), reads=[PB[0]], writes=[B_DT])
                for q in range(4):
                    OP("pe", lambda e, q=q: e.transpose(out=psb[1][:, q * 128:(q + 1) * 128],
                                                        in_=CRr[:, q, :, :].rearrange("r a n -> r (a n)"), identity=ident[:]),
                       reads=[B_CRr, B_ident], writes=[PB[1]])
                    OP("pe", lambda e, q=q: e.transpose(out=psb[2][:, q * 128:(q + 1) * 128],
                                                        in_=CIr[:, q, :, :].rearrange("r a n -> r (a n)"), identity=ident[:]),
                       reads=[B_CIr, B_ident], writes=[PB[2]])
                OP("dve", lambda e: e.tensor_copy(out=CR[:].rearrange("p g c -> p (g c)"), in_=psb[1][:, :]),
                   reads=[PB[1]], writes=[B_CR])
                OP("dve", lambda e: e.tensor_copy(out=CI[:].rearrange("p g c -> p (g c)"), in_=psb[2][:, :]),
                   reads=[PB[2]], writes=[B_CI])
                LRc, B_LRc = t32("LRc"); LRDT, B_LRDT = t32("LRDT"); LIC, B_LIC = t32("LIC")
                OP("dve", lambda e: e.tensor_scalar(out=LRc[:], in0=LR[:], scalar1=-1e-4, scalar2=None, op0=ALU.min),
                   reads=[B_LR], writes=[B_LRc])
                OP("dve", lambda e: e.tensor_tensor(out=LRDT[:], in0=LRc[:], in1=DT[:], op=ALU.mult),
                   reads=[B_LRc, B_DT], writes=[B_LRDT])
                OP("dve", lambda e: e.scalar_tensor_tensor(out=LIC[:], in0=LI[:], scalar=1.0 / TWO_PI, in1=DT[:],
                                                           op0=ALU.mult, op1=ALU.mult), reads=[B_LI, B_DT], writes=[B_LIC])
                KV, B_KV = t32("KV", (128, 9, 32))
                for k in range(9):
                    OP("dve", lambda e, k=k: e.memset(KV[:, k, :], float(k)), writes=[B_KV])
                FK, B_FK = t32("FK", (128, 9, 32)); FKc, B_FKc = t32("FKc", (128, 9, 32))
                MAG, B_MAG = t32("MAG", (128, 9, 32))
                OP("dve", lambda e: e.tensor_tensor(out=FK[:], in0=KV[:], in1=LIC[:].unsqueeze(1).to_broadcast([128, 9, 32]),
                                                    op=ALU.mult), reads=[B_KV, B_LIC], writes=[B_FK])
                OP("dve", lambda e: e.tensor_scalar(out=FKc[:], in0=FK[:], scalar1=0.25, scalar2=None, op0=ALU.add),
                   reads=[B_FK], writes=[B_FKc])
                OP("dve", lambda e: e.tensor_tensor(out=MAG[:], in0=KV[:], in1=LRDT[:].unsqueeze(1).to_broadcast([128, 9, 32]),
                                                    op=ALU.mult), reads=[B_KV, B_LRDT], writes=[B_MAG])
                OP("act", lambda e: e.activation(out=MAG[:], in_=MAG[:], func=AF.Exp), reads=[B_MAG], writes=[B_MAG])
                FFt, B_FFt = t32("FFt", (128, 9, 32)); FRs, B_FRs = t32("FRs", (128, 9, 32)); FRc, B_FRc = t32("FRc", (128, 9, 32))
                SINk, B_SINk = t32("SINk", (128, 9, 32)); COSk, B_COSk = t32("COSk", (128, 9, 32))
                frac_sincos(SINk[:], COSk[:], FK[:], FFt[:], FRs[:], B_FK, B_FFt, B_FRs, B_SINk, B_COSk, halfpi[:, 0:1])
                AR, B_AR = t32("AR", (128, 9, 32)); AI, B_AI = t32("AI", (128, 9, 32))
                OP("dve", lambda e: e.tensor_tensor(out=AR[:], in0=MAG[:], in1=COSk[:], op=ALU.mult), reads=[B_MAG, B_COSk], writes=[B_AR])
                OP("dve", lambda e: e.tensor_tensor(out=AI[:], in0=MAG[:], in1=SINk[:], op=ALU.mult), reads=[B_MAG, B_SINk], writes=[B_AI])
                OP("dve", lambda e: e.tensor_scalar(out=FPHI[:], in0=FRs[:, 8, :], scalar1=-1.0, scalar2=None, op0=ALU.mult), reads=[B_FRs], writes=[B_FPHI])
                OP("dve", lambda e: e.tensor_copy(out=RHO[:], in_=MAG[:, 8, :]), reads=[B_MAG], writes=[B_RHO])
                ta, B_ta = t32("ta"); tb, B_tb = t32("tb"); rden, B_rden = t32("rden"); arm1, B_arm1 = t32("arm1")
                FRf, B_FRf = t32("FRf"); FIf, B_FIf = t32("FIf")
                OP("dve", lambda e: e.tensor_tensor(out=ta[:], in0=LRc[:], in1=LRc[:], op=ALU.mult), reads=[B_LRc], writes=[B_ta])
                OP("dve", lambda e: e.tensor_tensor(out=tb[:], in0=LI[:], in1=LI[:], op=ALU.mult), reads=[B_LI], writes=[B_tb])
                OP("dve", lambda e: e.tensor_tensor(out=ta[:], in0=ta[:], in1=tb[:], op=ALU.add), reads=[B_ta, B_tb], writes=[B_ta])
                OP("dve", lambda e: e.reciprocal(out=rden[:], in_=ta[:]), reads=[B_ta], writes=[B_rden])
                OP("dve", lambda e: e.tensor_scalar(out=arm1[:], in0=AR[:, 1, :], scalar1=-1.0, scalar2=None, op0=ALU.add),
                   reads=[B_AR], writes=[B_arm1])
                OP("dve", lambda e: e.tensor_tensor(out=ta[:], in0=arm1[:], in1=LRc[:], op=ALU.mult), reads=[B_arm1, B_LRc, B_rden], writes=[B_ta])
                OP("dve", lambda e: e.tensor_tensor(out=tb[:], in0=AI[:, 1, :], in1=LI[:], op=ALU.mult), reads=[B_AI, B_LI], writes=[B_tb])
                OP("dve", lambda e: e.tensor_tensor(out=ta[:], in0=ta[:], in1=tb[:], op=ALU.add), reads=[B_ta, B_tb], writes=[B_ta])
                OP("dve", lambda e: e.tensor_tensor(out=FRf[:], in0=ta[:], in1=rden[:], op=ALU.mult), reads=[B_ta, B_rden], writes=[B_FRf])
                OP("dve", lambda e: e.tensor_tensor(out=ta[:], in0=AI[:, 1, :], in1=LRc[:], op=ALU.mult), reads=[B_AI, B_LRc, B_FRf], writes=[B_ta])
                OP("dve", lambda e: e.tensor_tensor(out=tb[:], in0=arm1[:], in1=LI[:], op=ALU.mult), reads=[B_arm1, B_LI], writes=[B_tb])
                OP("dve", lambda e: e.tensor_tensor(out=ta[:], in0=ta[:], in1=tb[:], op=ALU.subtract), reads=[B_ta, B_tb], writes=[B_ta])
                OP("dve", lambda e: e.tensor_tensor(out=FIf[:], in0=ta[:], in1=rden[:], op=ALU.mult), reads=[B_ta, B_rden], writes=[B_FIf])
                BBR = sb(st2a, "BBR", [128, 32, 16], F32); B_BBR = Buf()
                BBI = sb(st2a, "BBI", [128, 32, 16], F32); B_BBI = Buf()
                w1 = sb(st2a, "w1", [128, 32, 16], F32); B_w1 = Buf()
                w2 = sb(st2a, "w2", [128, 32, 16], F32); B_w2 = Buf()

                def bc16(t, ps=slice(0, 128)):
                    n = ps.stop - ps.start
                    return t.unsqueeze(2).to_broadcast([n, 32, 16])

                OP("dve", lambda e: e.tensor_tensor(out=w1[:], in0=BR[:], in1=bc16(FRf[:]), op=ALU.mult), reads=[B_BR, B_FRf], writes=[B_w1])
                OP("dve", lambda e: e.tensor_tensor(out=w2[:], in0=BI[:], in1=bc16(FIf[:]), op=ALU.mult), reads=[B_BI, B_FIf], writes=[B_w2])
                OP("dve", lambda e: e.tensor_tensor(out=BBR[:], in0=w1[:], in1=w2[:], op=ALU.subtract), reads=[B_w1, B_w2], writes=[B_BBR])
                OP("dve", lambda e: e.tensor_tensor(out=w1[:], in0=BI[:], in1=bc16(FRf[:]), op=ALU.mult), reads=[B_BI, B_FRf, B_BBR], writes=[B_w1])
                OP("dve", lambda e: e.tensor_tensor(out=w2[:], in0=BR[:], in1=bc16(FIf[:]), op=ALU.mult), reads=[B_BR, B_FIf, B_BBR], writes=[B_w2])
                OP("dve", lambda e: e.tensor_tensor(out=BBI[:], in0=w1[:], in1=w2[:], op=ALU.add), reads=[B_w1, B_w2], writes=[B_BBI])
                Pf = sb(st2a, "Pf", [128, 32, 15, 16], F32); B_Pf = Buf()
                Pb = sb(st2a, "Pb", [128, 32, 15, 16], F32); B_Pb = Buf()
                OP("pool", lambda e: e.memset(Pf[:], 0.0), writes=[B_Pf])
                OP("pool", lambda e: e.memset(Pb[:], 0.0), writes=[B_Pb])
                for k in range(8):
                    for hf in range(2):
                        ps = slice(hf * 64, (hf + 1) * 64)
                        X1, bX1, X2, bX2 = (BBR, B_BBR, BBI, B_BBI) if hf == 0 else (BBI, B_BBI, BBR, B_BBR)
                        op2 = ALU.subtract if hf == 0 else ALU.add
                        OP("dve", lambda e, ps=ps, k=k, X1=X1: e.tensor_tensor(out=w1[ps], in0=X1[ps], in1=bc16(AR[ps, k, :], ps), op=ALU.mult),
                           reads=[bX1, B_AR], writes=[B_w1])
                        OP("dve", lambda e, ps=ps, k=k, X2=X2: e.tensor_tensor(out=w2[ps], in0=X2[ps], in1=bc16(AI[ps, k, :], ps), op=ALU.mult),
                           reads=[bX2, B_AI], writes=[B_w2])
                        OP("dve", lambda e, ps=ps, k=k, op2=op2: e.tensor_tensor(out=Pf[ps, :, 7 - k, :], in0=w1[ps], in1=w2[ps], op=op2),
                           reads=[B_w1, B_w2], writes=[B_Pf])
                        OP("pool", lambda e, ps=ps, k=k: e.tensor_copy(out=Pb[ps, :, 7 + k, :], in_=Pf[ps, :, 7 - k, :]),
                           reads=[B_Pf], writes=[B_Pb])
                Cm = sb(st2a, "Cm", [128, 32, 16], F32); B_Cm = Buf()
                OP("dve", lambda e: e.tensor_copy(out=Cm[0:64], in_=CR[0:64]), reads=[B_CR], writes=[B_Cm])
                OP("dve", lambda e: e.tensor_scalar(out=Cm[64:128], in0=CI[64:128], scalar1=-1.0, scalar2=None, op0=ALU.mult),
                   reads=[B_CI], writes=[B_Cm])
                MT = sb(st2a, "MT", [128, 32, 8, 16], F32); B_MT = Buf()
                for d_, (Pt_, bP, a0) in enumerate(((Pf, B_Pf, 0), (Pb, B_Pb, 7))):
                    gs = slice(d_ * 16, (d_ + 1) * 16)
                    sg_ = 1.0 if d_ == 0 else -1.0
                    OP("dve", lambda e, Pt_=Pt_, a0=a0, gs=gs, sg_=sg_: e.tensor_scalar(
                        out=MT[0:64, gs, :, :], in0=Pt_[64:128, gs, a0:a0 + 8, :], scalar1=sg_, scalar2=None, op0=ALU.mult),
                       reads=[bP], writes=[B_MT])
                    OP("dve", lambda e, Pt_=Pt_, a0=a0, gs=gs, sg_=sg_: e.tensor_scalar(
                        out=MT[64:128, gs, :, :], in0=Pt_[0:64, gs, a0:a0 + 8, :], scalar1=-sg_, scalar2=None, op0=ALU.mult),
                       reads=[bP], writes=[B_MT])
                ER = sb(st2a, "ER", [128, 32, 16], F32); B_ER = Buf()
                EI = sb(st2a, "EI", [128, 32, 16], F32); B_EI = Buf()
                for pw in range(1, 9):
                    OP("dve", lambda e, pw=pw: e.tensor_tensor(out=w1[:], in0=CR[:], in1=bc16(AR[:, pw, :]), op=ALU.mult), reads=[B_CR, B_AR], writes=[B_w1])
                    OP("dve", lambda e, pw=pw: e.tensor_tensor(out=w2[:], in0=CI[:], in1=bc16(AI[:, pw, :]), op=ALU.mult), reads=[B_CI, B_AI], writes=[B_w2])
                    OP("dve", lambda e: e.tensor_tensor(out=ER[:], in0=w1[:], in1=w2[:], op=ALU.subtract), reads=[B_w1, B_w2], writes=[B_ER])
                    OP("dve", lambda e, pw=pw: e.tensor_tensor(out=w1[:], in0=CR[:], in1=bc16(AI[:, pw, :]), op=ALU.mult), reads=[B_CR, B_AI, B_ER], writes=[B_w1])
                    OP("dve", lambda e, pw=pw: e.tensor_tensor(out=w2[:], in0=CI[:], in1=bc16(AR[:, pw, :]), op=ALU.mult), reads=[B_CI, B_AR, B_ER], writes=[B_w2])
                    OP("dve", lambda e: e.tensor_tensor(out=EI[:], in0=w1[:], in1=w2[:], op=ALU.add), reads=[B_w1, B_w2], writes=[B_EI])
                    for d_ in range(2):
                        gs = slice(d_ * 16, (d_ + 1) * 16)
                        tl = pw - 1 if d_ == 0 else 8 - pw
                        s2 = -1.0 if d_ == 0 else 1.0
                        OP("pool", lambda e, gs=gs, tl=tl: e.tensor_copy(out=Dm1[0:64, gs, tl, :], in_=ER[0:64, gs, :]), reads=[B_ER], writes=[B_Dm1])
                        OP("pool", lambda e, gs=gs, tl=tl: e.tensor_scalar(out=Dm1[64:128, gs, tl, :], in0=EI[64:128, gs, :], scalar1=-1.0, scalar2=0.0, op0=ALU.mult, op1=ALU.add), reads=[B_EI], writes=[B_Dm1])
                        OP("pool", lambda e, gs=gs, tl=tl, s2=s2: e.tensor_scalar(out=Dm2[0:64, gs, tl, :], in0=EI[0:64, gs, :], scalar1=s2, scalar2=0.0, op0=ALU.mult, op1=ALU.add), reads=[B_EI], writes=[B_Dm2])
                        OP("pool", lambda e, gs=gs, tl=tl, s2=s2: e.tensor_scalar(out=Dm2[64:128, gs, tl, :], in0=ER[64:128, gs, :], scalar1=s2, scalar2=0.0, op0=ALU.mult, op1=ALU.add), reads=[B_ER], writes=[B_Dm2])
                krot = Rot([3, 4, 5, 6]); srot = Rot([0, 1, 2, 7])
                for dg in range(32):
                    d_, g = dg // 16, dg % 16
                    Pt_, bP, a0 = (Pf, B_Pf, 0) if d_ == 0 else (Pb, B_Pb, 7)
                    pk = krot.next(); pS = srot.next()
                    for t_ in range(8):
                        OP("pe", lambda e, Pt_=Pt_, dg=dg, t_=t_, pk=pk: e.matmul(
                            psb[pk][:, t_ * 16:(t_ + 1) * 16], lhsT=Pt_[:, dg, 7 - t_:15 - t_, :].rearrange("p a c -> p (a c)"),
                            rhs=Cm[:, dg, :], start=True, stop=True), reads=[bP, B_Cm], writes=[PB[pk]])
                    if d_ == 0:
                        OP("dve", lambda e, dg=dg, g=g, pk=pk: e.scalar_tensor_tensor(
                            out=Kmat[:, dg, :], in0=ident[:], scalar=DSK[:, g:g + 1], in1=psb[pk][:, 0:128],
                            op0=ALU.mult, op1=ALU.add), reads=[PB[pk], B_ident, B_DSK], writes=[B_Kmat])
                    else:
                        OP("dve", lambda e, dg=dg, pk=pk: e.tensor_copy(out=Kmat[:, dg, :], in_=psb[pk][:, 0:128]),
                           reads=[PB[pk]], writes=[B_Kmat])
                    OP("pe", lambda e, Pt_=Pt_, dg=dg, a0=a0, pS=pS: e.matmul(
                        psb[pS][:, 0:128], lhsT=Pt_[:, dg, a0:a0 + 8, :].rearrange("p a c -> p (a c)"), rhs=ident[:],
                        start=True, stop=True), reads=[bP, B_ident], writes=[PB[pS]])
                    OP("pe", lambda e, dg=dg, pS=pS: e.matmul(
                        psb[pS][:, 128:256], lhsT=MT[:, dg, :, :].rearrange("p a c -> p (a c)"), rhs=ident[:],
                        start=True, stop=True), reads=[B_MT, B_ident], writes=[PB[pS]])
                    OP("act", lambda e, dg=dg, pS=pS: e.activation(out=Smat[:, dg, :], in_=psb[pS][:, 0:128], func=AF.Copy),
                       reads=[PB[pS]], writes=[B_Smat])
                    OP("act", lambda e, dg=dg, pS=pS: e.activation(out=SmatT[:, dg, :], in_=psb[pS][:, 128:256], func=AF.Copy),
                       reads=[PB[pS]], writes=[B_SmatT])
            S.barrier()
            NS = 544
            ioJ = sb(st2, "ioJ", [128, NS], F32); B_ioJ = Buf()
            OP("pool", lambda e: e.iota(ioJ[:], pattern=[[1, NS]], base=0, channel_multiplier=0, allow_small_or_imprecise_dtypes=True), writes=[B_ioJ])

            def tset(n):
                return [(sb(st2, f"{n}{i}", [128, NS], F32), Buf()) for i in range(4)]
            Fs = tset("Fs"); Fc = tset("Fc"); FFs = tset("FFs"); FFc = tset("FFc"); SINt = tset("SINt"); COSt = tset("COSt")
            Vt = tset("Vt"); V2t = tset("V2t"); Wt = tset("Wt"); RHt = tset("RHt")
            P1t = [(sb(st2, f"P1t{i}", [128, NS], BF16), Buf()) for i in range(4)]
            P2t = [(sb(st2, f"P2t{i}", [128, NS], BF16), Buf()) for i in range(4)]
            ystg = [(sb(st2, f"ystg{i}", [128, 512], F32), Buf()) for i in range(2)]
            print("S5 sbuf remaining", nc.sbuf_bytes_remaining)
            srot = Rot([0, 1, 2, 3, 4, 5])

            def sets(n):
                i2 = n % 4
                return dict(F1=Fs[i2], F2=Fc[i2], FF1=FFs[i2], SN=SINt[i2], CS=COSt[i2], V=Vt[i2], V2=V2t[i2], W=Wt[i2], RH=RHt[i2],
                            P1=P1t[i2], P2=P2t[i2])

            def stageA(n):
                g, d_ = n // 2, n % 2
                dg = d_ * 16 + g
                T_ = sets(n)
                (F1, bF1), (F2, bF2), (FF1, bFF1) = T_["F1"], T_["F2"], T_["FF1"]
                (SN, bSN), (CS, bCS), (RH, bRH) = T_["SN"], T_["CS"], T_["RH"]
                OP("dve", lambda e, F1=F1, dg=dg: e.tensor_scalar(out=F1[:], in0=ioJ[:], scalar1=FPHI[:, dg:dg + 1], scalar2=None, op0=ALU.mult),
                   reads=[B_ioJ, B_FPHI], writes=[bF1])
                frac_sincos(SN[:], CS[:], F1[:], F2[:], FF1[:], bF1, bF2, bFF1, bSN, bCS, halfpi[:, 0:1])
                OP("pool", lambda e, RH=RH, dg=dg: e.tensor_scalar(out=RH[:], in0=ioJ[:], scalar1=0.0, scalar2=RHO[:, dg:dg + 1], op0=ALU.mult, op1=ALU.add),
                   reads=[B_ioJ, B_RHO], writes=[bRH])

            def stageB(n):
                g, d_ = n // 2, n % 2
                dg = d_ * 16 + g
                j0 = 0 if d_ == 0 else 32
                T_ = sets(n)
                (SN, bSN), (CS, bCS), (RH, bRH) = T_["SN"], T_["CS"], T_["RH"]
                (V, bV), (V2, bV2), (Wt_, bW) = T_["V"], T_["V2"], T_["W"]
                (P1, bP1), (P2, bP2) = T_["P1"], T_["P2"]
                pa = srot.next(); pb_ = srot.next(); pc = srot.next()
                for (M_, bM, pm, tail0) in ((Smat, B_Smat, pa, 0), (SmatT, B_SmatT, pc, 32)):
                    OP("pe", lambda e, M_=M_, dg=dg, pm=pm, g=g, j0=j0: e.matmul(psb[pm][:, 0:512], lhsT=M_[:, dg, :], rhs=u8all[:, g, j0:j0 + 512],
                                                                               start=True, stop=True), reads=[bM, B_u8], writes=[PB[pm]])
                    OP("pe", lambda e, M_=M_, dg=dg, pb_=pb_, g=g, j0=j0, tail0=tail0: e.matmul(
                        psb[pb_][:, tail0:tail0 + 32], lhsT=M_[:, dg, :], rhs=u8all[:, g, j0 + 512:j0 + 544], start=True, stop=True),
                       reads=[bM, B_u8], writes=[PB[pb_]])
                OP("dve", lambda e, V=V, CS=CS, pa=pa: e.tensor_tensor(out=V[:, 0:512], in0=psb[pa][:, 0:512], in1=CS[:, 0:512], op=ALU.mult),
                   reads=[PB[pa], bCS], writes=[bV])
                OP("dve", lambda e, V=V, CS=CS, pb_=pb_: e.tensor_tensor(out=V[:, 512:544], in0=psb[pb_][:, 0:32], in1=CS[:, 512:544], op=ALU.mult),
                   reads=[PB[pb_], bCS], writes=[bV])
                OP("dve", lambda e, V2=V2, SN=SN, pc=pc: e.tensor_tensor(out=V2[:, 0:512], in0=psb[pc][:, 0:512], in1=SN[:, 0:512], op=ALU.mult),
                   reads=[PB[pc], bSN], writes=[bV2])
                OP("dve", lambda e, V2=V2, SN=SN, pb_=pb_: e.tensor_tensor(out=V2[:, 512:544], in0=psb[pb_][:, 32:64], in1=SN[:, 512:544], op=ALU.mult),
                   reads=[PB[pb_], bSN], writes=[bV2])
                OP("pool", lambda e, V=V, V2=V2: e.tensor_tensor(out=V[:], in0=V[:], in1=V2[:], op=ALU.add), reads=[bV, bV2], writes=[bV])
                if d_ == 0:
                    OP("dve", lambda e, Wt_=Wt_, RH=RH, V=V: e.tensor_tensor_scan(out=Wt_[:, :], data0=RH[:, :], data1=V[:, :], initial=0.0,
                                                                                  op0=ALU.mult, op1=ALU.add), reads=[bRH, bV], writes=[bW])
                else:
                    OP("dve", lambda e, Wt_=Wt_, RH=RH, V=V: e.tensor_tensor_scan(out=Wt_[:, ::-1], data0=RH[:, ::-1], data1=V[:, ::-1], initial=0.0,
                                                                                  op0=ALU.mult, op1=ALU.add), reads=[bRH, bV], writes=[bW])
                OP("dve", lambda e, P1=P1, Wt_=Wt_, CS=CS: e.tensor_tensor(out=P1[:], in0=Wt_[:], in1=CS[:], op=ALU.mult), reads=[bW, bCS], writes=[bP1])
                OP("pool", lambda e, P2=P2, Wt_=Wt_, SN=SN: e.tensor_tensor(out=P2[:], in0=Wt_[:], in1=SN[:], op=ALU.mult), reads=[bW, bSN], writes=[bP2])
                sh = 31 if d_ == 0 else 1
                return [(Kmat, B_Kmat, dg, u8all, B_u8, g, 32), (Dm1, B_Dm1, dg, P1, bP1, None, sh), (Dm2, B_Dm2, dg, P2, bP2, None, sh)]

            def finish_g(g, ymm):
                pY = 6 + (g % 2)
                for n_, (M_, bM, dg, R_, bR, gsel, c0) in enumerate(ymm):
                    if gsel is not None:
                        OP("pe", lambda e, M_=M_, dg=dg, R_=R_, gsel=gsel, c0=c0, n_=n_, pY=pY: e.matmul(
                            psb[pY][:, :], lhsT=M_[:, dg, :], rhs=R_[:, gsel, c0:c0 + 512], start=(n_ == 0), stop=(n_ == 5)),
                           reads=[bM, bR], writes=[PB[pY]])
                    else:
                        OP("pe", lambda e, M_=M_, dg=dg, R_=R_, c0=c0, n_=n_, pY=pY: e.matmul(
                            psb[pY][:, :], lhsT=M_[:, dg, :, :].rearrange("p a c -> p (a c)"), rhs=R_[:, c0:c0 + 512],
                            start=(n_ == 0), stop=(n_ == 5)), reads=[bM, bR], writes=[PB[pY]])
                ys, bys = ystg[g % 2]
                OP("act", lambda e, ys=ys, pY=pY: e.activation(out=ys[:], in_=psb[pY][:, :], func=AF.Copy), reads=[PB[pY]], writes=[bys])
                for tl in range(8):
                    OP("sp", lambda e, ys=ys, tl=tl, g=g: e.dma_start(out=y_s[tl, g * 16:(g + 1) * 16, :], in_=ys[16 * tl:16 * tl + 16, :]),
                       reads=[bys], writes=[B_y], dma=True)

            wmb2 = [sb(st2, f"wmb2_{i}", [128, 8, 512], F32) for i in range(2)]; B_wmb2 = [Buf(), Buf()]
            bmb2 = [sb(st2, f"bmb2_{i}", [1, 512], F32) for i in range(2)]; B_bmb2 = [Buf(), Buf()]
            rowb2 = [sb(st2, f"rowb2_{i}", [2, 512], F32) for i in range(2)]; B_rowb2 = [Buf(), Buf()]
            print("S5 sbuf remaining (after adaLN bufs)", nc.sbuf_bytes_remaining)
            stageA(0)
            stageA(1)
            ymm = []
            for n in range(32):
                if n == 0:
                    for b_ in (NBLK0, NBLK0 + 1):
                        adaln_issue(b_, wmb2[b_ % 2], B_wmb2[b_ % 2], bmb2[b_ % 2], B_bmb2[b_ % 2])
                if n % 4 == 0 and n >= 4:
                    b_ = NBLK0 + n // 4 - 1
                    adaln_block(b_, wmb2[b_ % 2], B_wmb2[b_ % 2], bmb2[b_ % 2], B_bmb2[b_ % 2], rowb2[b_ % 2], B_rowb2[b_ % 2], issue=False)
                    if b_ + 2 < 18:
                        adaln_issue(b_ + 2, wmb2[b_ % 2], B_wmb2[b_ % 2], bmb2[b_ % 2], B_bmb2[b_ % 2])
                if n + 2 < 32:
                    stageA(n + 2)
                ymm += stageB(n)
                if n % 2 == 1:
                    finish_g(n // 2, ymm)
                    ymm = []
            b_ = 17
            adaln_block(b_, wmb2[b_ % 2], B_wmb2[b_ % 2], bmb2[b_ % 2], B_bmb2[b_ % 2], rowb2[b_ % 2], B_rowb2[b_ % 2], issue=False)
            adaln_AB([(2, (6, 7))])
        S.barrier()

        if dbg == "s5":
            OP("sp", lambda e: e.nop())
            S.emit(nc)
            return nc, {}

        NK = T + NCTX
        with contextlib.ExitStack() as st3:
            wuq = sb(st3, "wuq", [128, 2, 1152], BF16); B_wuq = Buf()
            wuqsw = sb(st3, "wuqsw", [128, 2, H, 32], BF16); B_wuqsw = Buf()
            wuk = sb(st3, "wuk", [128, H, 64], BF16); B_wuk = Buf()
            wv = sb(st3, "wv", [128, H, 64], BF16); B_wv = Buf()
            cqn = sb(st3, "cqn", [128, 2, T], BF16); B_cqnsb = Buf()
            ckvn = sb(st3, "ckvn", [128, NK], BF16); B_ckvnsb = Buf()
            kbuf = [sb(st3, f"kbuf{i}", [128, NK], BF16) for i in range(2)]; B_kn = [Buf(), Buf()]; B_krope = Buf()
            cosT = sb(st3, "cosT", [128, T], F32); B_cosT = Buf()
            sinT = sb(st3, "sinT", [128, T], F32); B_sinT = Buf()
            sel64 = sb(st3, "sel64", [65, 64], F32); B_sel64 = Buf()
            RP = slice(64, 96)
            OP("dve", lambda e: e.memset(sel64[0:64, :], 0.0), writes=[B_sel64])
            OP("dve", lambda e: e.memset(sel64[64:65, :], 1.0), writes=[B_sel64])
            wukv_v = wukv_d.rearrange("r (h two d) -> r h two d", two=2, d=64)
            OP("pool", lambda e: e.dma_start(out=wuk[:], in_=wukv_v[:, :, 0, :]), writes=[B_wuk], dma=True)
            OP("pool", lambda e: e.dma_start(out=wv[:], in_=wukv_v[:, :, 1, :]), writes=[B_wv], dma=True)
            OP("sp", lambda e: e.dma_start(out=cqn[:], in_=cqn_s.rearrange("(k p) t -> p k t", p=128)), reads=[B_cqn], writes=[B_cqnsb], dma=True)
            OP("sp", lambda e: e.dma_start(out=ckvn[:], in_=ckvn_s), reads=[B_ckvn], writes=[B_ckvnsb], dma=True)
            with contextlib.ExitStack() as st3a:
                wst = sb(st3a, "wst", [128, 2, 1152], F32); B_wst = Buf()
                OP("sp", lambda e: e.dma_start(out=wst[:], in_=wuq_d.rearrange("(k p) n -> p k n", p=128)), writes=[B_wst], dma=True)
                OP("dve", lambda e: e.tensor_scalar(out=wuq[:], in0=wst[:], scalar1=ATTN_SCALE, scalar2=None, op0=ALU.mult),
                   reads=[B_wst], writes=[B_wuq])
                for kc in range(2):
                    rope_v = wst[:, kc, :].rearrange("p (h c) -> p h c", c=96)[:, :, 64:96].rearrange("p h (i two) -> p h i two", two=2)
                    sw_v = wuqsw[:, kc, :, :].rearrange("p h (i two) -> p h i two", two=2)
                    OP("dve", lambda e, rope_v=rope_v, sw_v=sw_v: e.tensor_scalar(out=sw_v[:, :, :, 0], in0=rope_v[:, :, :, 1], scalar1=-ATTN_SCALE,
                                                                                  scalar2=None, op0=ALU.mult), reads=[B_wst], writes=[B_wuqsw])
                    OP("dve", lambda e, rope_v=rope_v, sw_v=sw_v: e.tensor_scalar(out=sw_v[:, :, :, 1], in0=rope_v[:, :, :, 0], scalar1=ATTN_SCALE,
                                                                                  scalar2=None, op0=ALU.mult), reads=[B_wst], writes=[B_wuqsw])
                frow = sb(st3a, "frow", [1, 64], F32); B_frow = Buf()
                fpart = sb(st3a, "fpart", [128, 2], F32); B_fpart = Buf()
                OP("dve", lambda e: e.memset(frow[:], 0.0), writes=[B_frow])
                for i_ in range(8):
                    fq = (10000.0 ** (-(2.0 * i_) / 16.0)) / TWO_PI
                    OP("dve", lambda e, i_=i_, fq=fq: e.memset(frow[0:1, 2 * i_:2 * i_ + 2], fq), writes=[B_frow])
                    OP("dve", lambda e, i_=i_, fq=fq: e.memset(frow[0:1, 48 + 2 * i_:48 + 2 * i_ + 2], fq), writes=[B_frow])
                OP("pe", lambda e: e.matmul(psb[0][64:96, 0:1], lhsT=frow[0:1, 0:32], rhs=ones[0:1, 0:1], start=True, stop=True),
                   reads=[B_frow, B_ones], writes=[PB[0]])
                OP("pe", lambda e: e.matmul(psb[0][64:96, 1:2], lhsT=frow[0:1, 32:64], rhs=ones[0:1, 0:1], start=True, stop=True),
                   reads=[B_frow, B_ones], writes=[PB[0]])
                OP("dve", lambda e: e.tensor_copy(out=fpart[RP, :], in_=psb[0][RP, 0:2]), reads=[PB[0]], writes=[B_fpart])
                rowv = sb(st3a, "rowv", [128, T], F32); B_rowv = Buf()
                colv = sb(st3a, "colv", [128, T], F32); B_colv = Buf()
                OP("pool", lambda e: e.iota(rowv[RP, :], pattern=[[1, 64], [0, 64]], base=0, channel_multiplier=0, allow_small_or_imprecise_dtypes=True), writes=[B_rowv])
                OP("pool", lambda e: e.iota(colv[RP, :], pattern=[[0, 64], [1, 64]], base=0, channel_multiplier=0, allow_small_or_imprecise_dtypes=True), writes=[B_colv])
                Fa = sb(st3a, "Fa", [128, T], F32); B_Fa = Buf()
                Fb = sb(st3a, "Fb", [128, T], F32); B_Fb = Buf()
                Ff = sb(st3a, "Ff", [128, T], F32); B_Ff = Buf()
                OP("dve", lambda e: e.tensor_scalar(out=Fa[RP, :], in0=rowv[RP, :], scalar1=fpart[RP, 0:1], scalar2=None, op0=ALU.mult),
                   reads=[B_rowv, B_fpart], writes=[B_Fa])
                OP("dve", lambda e: e.scalar_tensor_tensor(out=Fa[RP, :], in0=colv[RP, :], scalar=fpart[RP, 1:2], in1=Fa[RP, :], op0=ALU.mult, op1=ALU.add),
                   reads=[B_colv, B_fpart, B_Fa], writes=[B_Fa])
                frac_sincos(sinT[RP, :], cosT[RP, :], Fa[RP, :], Fb[RP, :], Ff[RP, :], B_Fa, B_Fb, B_Ff, B_sinT, B_cosT, halfpi[RP, 0:1])
                krA = rowv; krB = colv
                OP("sp", lambda e: e.dma_start(out=krA[RP, 0:NK - T], in_=kr_s[0, :, 0:NCTX]), reads=[B_kr, B_Fa], writes=[B_rowv], dma=True)
                for kb_ in kbuf:
                    OP("dve", lambda e, kb_=kb_: e.tensor_copy(out=kb_[RP, 0:NCTX], in_=krA[RP, 0:NCTX]), reads=[B_rowv], writes=[B_krope])
                OP("sp", lambda e: e.dma_start(out=krA[RP, :], in_=kr_s[0, :, NCTX:NK]), reads=[B_kr, B_krope], writes=[B_rowv], dma=True)
                OP("sp", lambda e: e.dma_start(out=krB[RP, :], in_=kr_s[1, :, NCTX:NK]), reads=[B_kr, B_Fa], writes=[B_colv], dma=True)
                OP("dve", lambda e: e.tensor_tensor(out=Fa[RP, :], in0=krA[RP, :], in1=cosT[RP, :], op=ALU.mult), reads=[B_rowv, B_cosT, B_sinT], writes=[B_Fa])
                OP("dve", lambda e: e.tensor_tensor(out=Fb[RP, :], in0=krB[RP, :], in1=sinT[RP, :], op=ALU.mult), reads=[B_colv, B_sinT, B_cosT], writes=[B_Fb])
                for kb_ in kbuf:
                    OP("dve", lambda e, kb_=kb_: e.tensor_tensor(out=kb_[RP, NCTX:NK], in0=Fa[RP, :], in1=Fb[RP, :], op=ALU.add),
                       reads=[B_Fa, B_Fb], writes=[B_krope])
            S.barrier()
            Vall = sb(st3, "Vall", [128, 34, H, 65], BF16); B_Vall = Buf()
            OP("pool", lambda e: e.memset(Vall[:, :, :, 64:65], 1.0), writes=[B_Vall])
            vrot = Rot([1, 2, 3, 4])
            for kt in range(34):
                for hh in range(2):
                    pv_ = vrot.next()
                    OP("pe", lambda e, kt=kt, hh=hh, pv_=pv_: e.matmul(
                        psb[pv_][:, 0:384], lhsT=ckvn[:, kt * 128:(kt + 1) * 128],
                        rhs=wv[:, hh * 6:(hh + 1) * 6, :].rearrange("p h d -> p (h d)"), start=True, stop=True),
                       reads=[B_ckvnsb, B_wv], writes=[PB[pv_]])
                    OP("act" if hh else "dve", (lambda e, kt=kt, hh=hh, pv_=pv_: e.activation(
                        out=Vall[:, kt, hh * 6:(hh + 1) * 6, 0:64], in_=psb[pv_][:, 0:384].rearrange("p (h d) -> p h d", d=64), func=AF.Copy))
                       if hh else (lambda e, kt=kt, hh=hh, pv_=pv_: e.tensor_copy(
                           out=Vall[:, kt, hh * 6:(hh + 1) * 6, 0:64], in_=psb[pv_][:, 0:384].rearrange("p (h d) -> p h d", d=64))),
                       reads=[PB[pv_]], writes=[B_Vall])
            S.barrier()
            Qh = [(sb(st3, f"Qh{i}", [128, 512], BF16), Buf()) for i in range(2)]
            qt1 = sb(st3, "qt1", [128, 512], F32); B_qt1 = Buf()
            qt2 = sb(st3, "qt2", [128, 512], F32); B_qt2 = Buf()
            Pts = [(sb(st3, f"Pt{i}", [128, 1024], BF16), Buf()) for i in range(3)]
            Osb = [(sb(st3, f"Osb{i}", [65, 512], F32), Buf()) for i in range(2)]
            atts = [(sb(st3, f"att{i}", [64, 512], BF16), Buf()) for i in range(2)]
            print("attn sbuf remaining", nc.sbuf_bytes_remaining)
            SP_ = [(0, Buf()), (2, Buf())]
            NB_ = H * 8

            def gen_K(h):
                kb_ = kbuf[h % 2]; bkn = B_kn[h % 2]
                for n_, c0 in enumerate(range(0, NK, 512)):
                    c1 = min(NK, c0 + 512)
                    pk = 6 + (n_ % 2)
                    OP("pe", lambda e, h=h, c0=c0, c1=c1, pk=pk: e.matmul(psb[pk][0:64, 0:c1 - c0], lhsT=wuk[:, h, :], rhs=ckvn[:, c0:c1],
                                                                         start=True, stop=True), reads=[B_wuk, B_ckvnsb], writes=[PB[pk]])
                    OP("dve", lambda e, kb_=kb_, c0=c0, c1=c1, pk=pk: e.tensor_copy(out=kb_[0:64, c0:c1], in_=psb[pk][0:64, 0:c1 - c0]),
                       reads=[PB[pk]], writes=[bkn])

            def gen_Q(b):
                h, qb = b // 8, b % 8
                qs = slice(qb * 512, (qb + 1) * 512)
                Q, bQ = Qh[b % 2]
                for kc in range(2):
                    OP("pe", lambda e, h=h, kc=kc, qs=qs: e.matmul(psb[6][0:96, :], lhsT=wuq[:, kc, h * 96:(h + 1) * 96], rhs=cqn[:, kc, qs],
                                                                start=(kc == 0), stop=(kc == 1)), reads=[B_wuq, B_cqnsb], writes=[PB[6]])
                for kc in range(2):
                    OP("pe", lambda e, h=h, kc=kc, qs=qs: e.matmul(psb[7][64:96, :], lhsT=wuqsw[:, kc, h, :], rhs=cqn[:, kc, qs],
                                                                start=(kc == 0), stop=(kc == 1)), reads=[B_wuqsw, B_cqnsb], writes=[PB[7]])
                OP("dve", lambda e, Q=Q: e.tensor_copy(out=Q[0:64, :], in_=psb[6][0:64, :]), reads=[PB[6]], writes=[bQ])
                OP("dve", lambda e, qs=qs: e.tensor_tensor(out=qt1[RP, :], in0=psb[6][RP, :], in1=cosT[RP, qs], op=ALU.mult),
                   reads=[PB[6], B_cosT], writes=[B_qt1])
                OP("dve", lambda e, qs=qs: e.tensor_tensor(out=qt2[RP, :], in0=psb[7][RP, :], in1=sinT[RP, qs], op=ALU.mult),
                   reads=[PB[7], B_sinT], writes=[B_qt2])
                OP("dve", lambda e, Q=Q: e.tensor_tensor(out=Q[RP, :], in0=qt1[RP, :], in1=qt2[RP, :], op=ALU.add),
                   reads=[B_qt1, B_qt2], writes=[bQ])

            def evac1(b):
                pO = 4 + (b % 2)
                Ob, bOb = Osb[b % 2]
                OP("dve", lambda e, Ob=Ob, pO=pO: e.tensor_copy(out=Ob[0:65, :], in_=psb[pO][0:65, :]), reads=[PB[pO]], writes=[bOb])
                OP("dve", lambda e, Ob=Ob: e.reciprocal(out=Ob[64:65, :], in_=Ob[64:65, :]), reads=[bOb], writes=[bOb])

            def evac2(b):
                h, qb = b // 8, b % 8
                qs = slice(qb * 512, (qb + 1) * 512)
                Ob, bOb = Osb[b % 2]
                at, bat = atts[b % 2]
                OP("pe", lambda e, Ob=Ob: e.matmul(psb[7][0:64, :], lhsT=sel64[0:65, :], rhs=Ob[0:65, :], start=True, stop=True),
                   reads=[B_sel64, bOb], writes=[PB[7]])
                OP("dve", lambda e, at=at, Ob=Ob: e.tensor_tensor(out=at[:, :], in0=Ob[0:64, :], in1=psb[7][0:64, :], op=ALU.mult),
                   reads=[bOb, PB[7]], writes=[bat])
                OP("sp", lambda e, at=at, h=h, qs=qs: e.dma_start(out=attn_s[h * 64:(h + 1) * 64, qs], in_=at[:, :]),
                   reads=[bat], writes=[B_attn], dma=True)

            gen_K(0)
            gen_Q(0)
            p_i = 0
            for b in range(NB_):
                h, qb = b // 8, b % 8
                kb_ = kbuf[h % 2]; bkn = B_kn[h % 2]
                Q, bQ = Qh[b % 2]
                pO = 4 + (b % 2)
                pend = []

                def pv_pair(kt0, Pt_, bPt, h=h, pO=pO):
                    for j in range(2):
                        kt = kt0 + j
                        OP("pe", lambda e, kt=kt, j=j, Pt_=Pt_, h=h, pO=pO: e.matmul(psb[pO][0:65, :], lhsT=Vall[:, kt, h, :], rhs=Pt_[:, j * 512:(j + 1) * 512],
                                                                                  start=(kt == 0), stop=(kt == 33)), reads=[B_Vall, bPt], writes=[PB[pO]])
                for pr_ in range(17):
                    kt0 = 2 * pr_
                    sp0, bSp = SP_[pr_ % 2]
                    for j in range(2):
                        kt = kt0 + j
                        OP("pe", lambda e, kt=kt, j=j, kb_=kb_, Q=Q, sp0=sp0: e.matmul(psb[sp0 + j][:, :], lhsT=kb_[0:96, kt * 128:(kt + 1) * 128], rhs=Q[0:96, :],
                                                                                    start=True, stop=True), reads=[bkn, B_krope, bQ], writes=[bSp])
                    Pt_, bPt = Pts[p_i % 3]; p_i += 1
                    OP("act", lambda e, Pt_=Pt_, sp0=sp0: e.activation(out=Pt_[:, :], in_=psall[:, sp0 * 512:(sp0 + 2) * 512], func=AF.Exp),
                       reads=[bSp], writes=[bPt])
                    pend.append((kt0, Pt_, bPt))
                    if len(pend) > 2:
                        pv_pair(*pend.pop(0))
                    if pr_ == 1 and b > 0:
                        evac1(b - 1)
                    if pr_ == 3 and b + 1 < NB_:
                        gen_Q(b + 1)
                    if pr_ == 7 and b > 0:
                        evac2(b - 1)
                    if pr_ == 10 and qb == 7 and h + 1 < H:
                        gen_K(h + 1)
                while pend:
                    pv_pair(*pend.pop(0))
            evac1(NB_ - 1)
            evac2(NB_ - 1)
        S.barrier()

        if dbg == "attn":
            OP("sp", lambda e: e.nop())
            S.emit(nc)
            return nc, {}

        with contextlib.ExitStack() as st4:
            wgu2v, B_wgu2 = load_w_bf16(st4, "wgu2", wgu2_d, 8, 2 * DFF)
            wdn2v, B_wdn2 = load_w_bf16(st4, "wdn2", wdn2_d, NFC, D)
            wglu, B_wglu = load_w_bf16(st4, "wglu", wglu_d, 2, 512)
            g5, B_g5 = bcast_rows(st4, "g5", 5 * D, 0, 1.0)
            g8, B_g8 = bcast_rows(st4, "g8", 8 * D, 0, 0.5)
            gfb, B_gfb = bcast_vec(st4, "gfb", gfin_d, D)
            W3 = common_work(st4)
            xt3v = sb(st4, "xt3", [128, 2, D], F32); B_xt3 = Buf()
            hT3v = sb(st4, "hT3", [128, 8, TN], BF16); B_hT3 = [Buf() for _ in range(8)]
            wo_t, B_wo = load_w_bf16(st4, "wout", wout_d, 8, D)
            ysb = sb(st4, "ysb", [128, 2, 8, TN // 8], F32); B_ysb = [Buf(), Buf()]
            xnf = W3["xn"][:].rearrange("p i d -> p (i d)")
            yn = xnf[:, 0:256]; B_yn = W3["B_xn"][0]
            tg = xnf[:, 256:512]; B_tg = W3["B_xn"][0]
            sgl = xnf[:, 512:768]; B_sgl = W3["B_xn"][0]
            zT = sb(st4, "zT", [128, 2, TN], BF16); B_zT = [Buf(), Buf()]
            ssm = sb(st4, "ssm", [128, 2, TN], F32); B_ssm = [Buf(), Buf()]
            sq1 = [(xnf[:, 1024 + i * 256:1024 + (i + 1) * 256], W3["B_xn"][1]) for i in range(2)]
            sq2 = [(sb(st4, f"sq2_{i}", [128, TN], BF16), Buf()) for i in range(2)]
            rbs = xnf[:, 768:1024]; B_rbs = W3["B_xn"][0]
            rba = xnf[:, 1536:1792]; B_rba = W3["B_xn"][1]
            ssmn = sb(st4, "ssmn", [128, 2, TN], BF16); B_ssmn = [Buf(), Buf()]
            at_t = sb(st4, "at_t", [128, 6, TN], BF16); B_at = Buf()
            print("phase3 sbuf remaining", nc.sbuf_bytes_remaining)
            if dbg == "p3alloc":
                S.barrier()
                OP("sp", lambda e: e.nop())
                S.emit(nc)
                return nc, {}
            sq1_rot = Rot(sq1); sq2_rot = Rot(sq2)
            pr = Rot([2, 3])
            wout_v = wout_d.rearrange("(k p) n -> p k n", p=128)
            def gelu_stage(ti_):
                t0_ = ti_ * TN
                j0_ = ti_ * (TN // 8)
                fs = []

                def loads():
                    OP("sp", lambda e: e.dma_start(out=at_t[:], in_=attn_s.rearrange("(k p) t -> p k t", p=128)[:, :, t0_:t0_ + TN]),
                       reads=[B_attn], writes=[B_at], dma=True)
                    for gc in range(2):
                        OP("sp", lambda e, gc=gc: e.dma_start(out=ysb[:, gc, :, :],
                                                              in_=y_s[:, gc * 128:(gc + 1) * 128, j0_:j0_ + TN // 8].rearrange("t p j -> p t j")),
                           reads=[B_y], writes=[B_ysb[gc]], dma=True)
                fs.append(loads)
                for gc in range(2):
                    fs.append(lambda gc=gc: OP("dve", lambda e: e.tensor_copy(out=yn[:].rearrange("p (j t) -> p j t", t=8),
                                                                            in_=ysb[:, gc, :, :].rearrange("p t j -> p j t")), reads=[B_ysb[gc]], writes=[B_yn]))
                    fs.append(lambda: OP("dve", lambda e: e.tensor_tensor(out=tg[:], in0=yn[:], in1=yn[:], op=ALU.mult), reads=[B_yn], writes=[B_tg]))
                    fs.append(lambda: OP("dve", lambda e: e.tensor_scalar(out=tg[:], in0=tg[:], scalar1=0.044715, scalar2=1.0, op0=ALU.mult, op1=ALU.add),
                                         reads=[B_tg], writes=[B_tg]))
                    fs.append(lambda: OP("dve", lambda e: e.tensor_tensor(out=tg[:], in0=tg[:], in1=yn[:], op=ALU.mult), reads=[B_tg, B_yn], writes=[B_tg]))
                    fs.append(lambda: OP("act", lambda e: e.activation(out=sgl[:], in_=tg[:], func=AF.Sigmoid, scale=2.0 * math.sqrt(2.0 / math.pi)),
                                         reads=[B_tg], writes=[B_sgl]))
                    fs.append(lambda gc=gc: OP("dve", lambda e: e.tensor_tensor(out=zT[:, gc, :], in0=yn[:], in1=sgl[:], op=ALU.mult),
                                               reads=[B_yn, B_sgl], writes=[B_zT[gc]]))
                fs.append(lambda: None)
                fs.append(lambda: None)
                for c2 in range(2):
                    def glu_mm(c2=c2):
                        for (pp, n4) in ((0, c2), (1, 2 + c2)):
                            for gc in range(2):
                                OP("pe", lambda e, pp=pp, n4=n4, gc=gc: e.matmul(psb[pp][:, 0:TN], lhsT=wglu[:, gc, n4 * 128:(n4 + 1) * 128], rhs=zT[:, gc, :],
                                                                              start=(gc == 0), stop=(gc == 1)), reads=[B_wglu[gc], B_zT[gc]], writes=[PB[pp]])

                    def glu_ev(c2=c2):
                        OP("act", lambda e: e.activation(out=sgl[:], in_=psb[1][:, 0:TN], func=AF.Sigmoid), reads=[PB[1]], writes=[B_sgl])
                        OP("dve", lambda e, c2=c2: e.tensor_tensor(out=ssm[:, c2, :], in0=psb[0][:, 0:TN], in1=sgl[:], op=ALU.mult),
                           reads=[PB[0], B_sgl], writes=[B_ssm[c2]])
                    fs.append(glu_mm); fs.append(glu_ev)
                ssq_ = [sq1[0], sq1[1]]
                for c2 in range(2):
                    def ssm_sq(c2=c2):
                        sq, bsq = ssq_[c2]
                        OP("dve", lambda e, sq=sq, c2=c2: e.tensor_tensor(out=sq[:], in0=ssm[:, c2, :], in1=ssm[:, c2, :], op=ALU.mult),
                           reads=[B_ssm[c2]], writes=[bsq])
                    fs.append(ssm_sq)

                def at_sq(kc):
                    sq, bsq = sq2[kc % 2]
                    OP("pool", lambda e, sq=sq, kc=kc: e.tensor_tensor(out=sq[:], in0=at_t[:, kc, :], in1=at_t[:, kc, :], op=ALU.mult),
                       reads=[B_at], writes=[bsq])

                def at_mm(kc):
                    sq, bsq = sq2[kc % 2]
                    OP("pe", lambda e, sq=sq, kc=kc: e.matmul(psb[1][:, 0:TN], lhsT=onesb[:, :], rhs=sq[:], start=(kc == 0), stop=(kc == 5)),
                       reads=[bsq, B_onesb], writes=[PB[1]])

                def ssm_mm():
                    for c2 in range(2):
                        sq, bsq = ssq_[c2]
                        OP("pe", lambda e, sq=sq, c2=c2: e.matmul(psb[0][:, 0:TN], lhsT=ones[:, :], rhs=sq[:], start=(c2 == 0), stop=(c2 == 1)),
                           reads=[bsq, B_ones], writes=[PB[0]])
                fs.append(lambda: at_sq(0)); fs.append(lambda: at_sq(1)); fs.append(ssm_mm)
                for kc in range(6):
                    fs.append(lambda kc=kc: at_mm(kc))
                    if kc + 2 < 6:
                        fs.append(lambda kc=kc: at_sq(kc + 2))

                def ssm_fin():
                    OP("act", lambda e: e.activation(out=rbs[:], in_=psb[0][:, 0:TN], func=AF.Sqrt, scale=1.0 / 256.0, bias=W3["eps"][:, 0:1]),
                       reads=[PB[0], W3["B_eps"]], writes=[B_rbs])
                    OP("dve", lambda e: e.reciprocal(out=rbs[:], in_=rbs[:]), reads=[B_rbs], writes=[B_rbs])

                def ssm_n():
                    for c2 in range(2):
                        OP("dve", lambda e, c2=c2: e.scalar_tensor_tensor(out=ssmn[:, c2, :], in0=ssm[:, c2, :], scalar=gsm[:, 2, c2:c2 + 1], in1=rbs[:],
                                                                          op0=ALU.mult, op1=ALU.mult), reads=[B_ssm[c2], B_gsm, B_rbs], writes=[B_ssmn[c2]])

                def at_fin():
                    OP("act", lambda e: e.activation(out=rba[:], in_=psb[1][:, 0:TN], func=AF.Sqrt, scale=1.0 / 768.0, bias=W3["eps"][:, 0:1]),
                       reads=[PB[1], W3["B_eps"]], writes=[B_rba])
                    OP("dve", lambda e: e.reciprocal(out=rba[:], in_=rba[:]), reads=[B_rba], writes=[B_rba])

                def at_n(k0):
                    for kc in range(k0, k0 + 3):
                        OP("dve", lambda e, kc=kc: e.scalar_tensor_tensor(out=at_t[:, kc, :], in0=at_t[:, kc, :], scalar=gmla[:, kc:kc + 1], in1=rba[:],
                                                                          op0=ALU.mult, op1=ALU.mult), reads=[B_at, B_gmla, B_rba], writes=[B_at])
                fs.append(ssm_fin); fs.append(ssm_n); fs.append(at_fin); fs.append(lambda: at_n(0)); fs.append(lambda: at_n(3))
                return fs

            for ti in range(T // TN):
                t0 = ti * TN
                j0 = ti * (TN // 8)
                if ti == 0:
                    for f_ in gelu_stage(0):
                        f_()
                OP("sp", lambda e, t0=t0: e.dma_start(out=xt3v[:], in_=x1_s[t0:t0 + TN, :].rearrange("(i p) d -> p i d", p=128)),
                   reads=[B_x1], writes=[B_xt3], dma=True)
                dn = W3["dn_banks"]
                for kc in range(8):
                    wo = wo_t[:, kc, :]; bwo = B_wo[kc]
                    gi = 0
                    for i in range(2):
                        for e2 in range(2):
                            pb = dn[gi]; gi += 1
                            if kc < 6:
                                OP("pe", lambda e, wo=wo, kc=kc, i=i, e2=e2, pb=pb: e.matmul(
                                    psb[pb][:, :], lhsT=at_t[:, kc, i * 128:(i + 1) * 128], rhs=wo[:, e2 * 512:(e2 + 1) * 512],
                                    start=(kc == 0), stop=False), reads=[B_at, bwo], writes=[PB[pb]])
                            else:
                                OP("pe", lambda e, wo=wo, kc=kc, i=i, e2=e2, pb=pb: e.matmul(
                                    psb[pb][:, :], lhsT=ssmn[:, kc - 6, i * 128:(i + 1) * 128], rhs=wo[:, e2 * 512:(e2 + 1) * 512],
                                    start=False, stop=(kc == 7)), reads=[B_ssmn[kc - 6], bwo], writes=[PB[pb]])
                gi = 0
                for i in range(2):
                    for e2 in range(2):
                        pb = dn[gi]; gi += 1
                        tmp, btmp = W3["tmp_rot"].next()
                        OP("dve", lambda e, pb=pb, e2=e2, tmp=tmp: e.tensor_tensor(out=tmp[:], in0=psb[pb][:, :], in1=g5[:, e2 * 512:(e2 + 1) * 512], op=ALU.mult),
                           reads=[PB[pb], B_g5], writes=[btmp])
                        OP("pool", lambda e, i=i, e2=e2, tmp=tmp: e.tensor_tensor(out=xt3v[:, i, e2 * 512:(e2 + 1) * 512], in0=xt3v[:, i, e2 * 512:(e2 + 1) * 512],
                                                                               in1=tmp[:], op=ALU.add), reads=[btmp, B_xt3], writes=[B_xt3])
                norm_to_hT(xt3v, B_xt3, 2, 2, 0, hT3v, B_hT3, W3)
                bg_ = gelu_stage(ti + 1) if ti + 1 < T // TN else []
                ffn(xt3v, B_xt3, 2, hT3v, B_hT3, wgu2v, B_wgu2, wdn2v, B_wdn2, g8, B_g8, W3, bg=bg_)
                while bg_:
                    bg_.pop(0)()
                xo = W3["xn"]
                for i in range(2):
                    OP("act", lambda e, i=i: e.activation(out=xo[:, i, :], in_=xt3v[:, i, :], func=AF.Square, accum_out=W3["ssq"][:, i:i + 1]),
                       reads=[B_xt3], writes=[W3["B_xn"][i], W3["B_ssq"]])
                OP("act", lambda e: e.activation(out=W3["rstd"][:, 0:2], in_=W3["ssq"][:, 0:2], func=AF.Sqrt, scale=1.0 / D, bias=W3["eps"][:, 0:1]),
                   reads=[W3["B_ssq"], W3["B_eps"]], writes=[W3["B_rstd"]])
                OP("dve", lambda e: e.reciprocal(out=W3["rstd"][:, 0:2], in_=W3["rstd"][:, 0:2]), reads=[W3["B_rstd"]], writes=[W3["B_rstd"]])
                for i in range(2):
                    OP("dve", lambda e, i=i: e.scalar_tensor_tensor(out=xo[:, i, :], in0=xt3v[:, i, :], scalar=W3["rstd"][:, i:i + 1], in1=gfb[:],
                                                                    op0=ALU.mult, op1=ALU.mult), reads=[B_xt3, W3["B_rstd"], B_gfb], writes=[W3["B_xn"][i]])
                out_dmas.append(OP("pool", lambda e, t0=t0: e.dma_start(out=out_d[t0:t0 + TN, :].rearrange("(i p) d -> p i d", p=128), in_=xo[:]),
                                   reads=W3["B_xn"], dma=True))
        S.barrier()
        OP("sp", lambda e: e.nop(), extra=out_dmas)
        S.emit(nc)
    return nc, {}


def _prep_inputs(inputs):
    f = lambda a: np.ascontiguousarray(np.asarray(a, dtype=np.float32))
    shared = {
        "c_ctx": f(inputs["c_ctx"]).reshape(1, D),
        "w_mod": f(inputs["w_mod"][0]), "b_mod": f(inputs["b_mod"][0]).reshape(1, -1),
        "g_ffn1": f(inputs["g_ffn1"][0]).reshape(1, -1), "w_gu1": f(inputs["w_gu1"][0]), "w_down1": f(inputs["w_down1"][0]),
        "g_mix": f(inputs["g_mix"][0]).reshape(1, -1), "w_in": f(inputs["w_in"][0]),
        "g_cq": f(inputs["g_cq"][0]).reshape(1, -1), "w_uq": f(inputs["w_uq"][0]),
        "g_ckv": f(inputs["g_ckv"][0]).reshape(1, -1), "w_ukv": f(inputs["w_ukv"][0]),
        "lam_re": f(inputs["lam_re"][0]).reshape(32, 64), "lam_im": f(inputs["lam_im"][0]).reshape(32, 64),
        "log_dt": f(inputs["log_dt"][0]).reshape(1, 32),
        "b_re": f(inputs["b_re"][0]).reshape(32, 64, 16), "b_im": f(inputs["b_im"][0]).reshape(32, 64, 16),
        "c_re": f(inputs["c_re"][0]).reshape(512, 64), "c_im": f(inputs["c_im"][0]).reshape(512, 64),
        "d_skip": f(inputs["d_skip"][0]).reshape(1, -1), "w_glu": f(inputs["w_glu"][0]),
        "g_mla_out": f(inputs["g_mla_out"][0]).reshape(1, -1), "g_ssm_out": f(inputs["g_ssm_out"][0]).reshape(1, -1),
        "w_out": f(inputs["w_out"][0]), "g_ffn2": f(inputs["g_ffn2"][0]).reshape(1, -1),
        "w_gu2": f(inputs["w_gu2"][0]), "w_down2": f(inputs["w_down2"][0]),
        "g_final": f(inputs["g_final"]).reshape(1, -1),
    }
    x = f(inputs["x"]); c = f(inputs["c"]); ctx = f(inputs["ctx"])
    maps = []
    for b in range(8):
        m = dict(shared)
        m["x"] = x[b]; m["c"] = c[b].reshape(1, D); m["ctx"] = ctx[b]
        maps.append(m)
    return maps


def kernel(**inputs):
    nc, _ = build_program()
    in_maps = _prep_inputs(inputs)
    res = run_bass_kernel_spmd(nc, in_maps, core_ids=list(range(8)))
    out = np.stack([np.asarray(r["out"], dtype=np.float32) for r in res.results], axis=0)
    return out
```

```python
import contextlib
import math
import numpy as np
import concourse.bass as bass
import concourse.mybir as mybir
from concourse.bass_utils import run_bass_kernel_spmd

F32 = mybir.dt.float32
BF16 = mybir.dt.bfloat16
I32 = mybir.dt.int32
ALU = mybir.AluOpType
AF = mybir.ActivationFunctionType

ENGS = ("sp", "act", "pool", "dve", "pe")
N_DMA_SEMS = 24

D = 1024
T = 4096
NCTX = 256
DFF = 2816
NFC = 22
H = 12
EPS = 1e-6
ATTN_SCALE = 96 ** -0.5
TN = 256
NJ = 576
TWO_PI = 2.0 * math.pi


class Buf:
    __slots__ = ("name", "w", "r")

    def __init__(self, name=""):
        self.name = name
        self.w = None
        self.r = []


class Sched:
    def __init__(self):
        self.ops = []
        self.last = {e: None for e in ENGS}
        self.dmas_since = []

    def op(self, eng, fn, reads=(), writes=(), dma=False, extra=()):
        i = len(self.ops)
        deps = set(extra)
        for b in reads:
            if b.w is not None:
                deps.add(b.w)
        for b in writes:
            if b.w is not None:
                deps.add(b.w)
            for r in b.r:
                deps.add(r)
        for b in reads:
            b.r.append(i)
        for b in writes:
            b.w = i
            b.r = []
        deps.discard(i)
        self.ops.append(dict(eng=eng, fn=fn, deps=deps, dma=dma))
        self.last[eng] = i
        if dma:
            self.dmas_since.append(i)
        return i

    def barrier(self):
        deps = [v for v in self.last.values() if v is not None] + list(self.dmas_since)
        self.dmas_since = []
        ids = []
        for e in ENGS:
            ids.append(self.op(e, lambda eng: eng.nop(), extra=deps))
        return ids

    def emit(self, nc):
        ops = self.ops
        n = len(ops)
        needed = [False] * n
        for i, o in enumerate(ops):
            keep = set()
            for d in o["deps"]:
                od = ops[d]
                if od["eng"] == "pe" and o["eng"] == "pe" and not od["dma"] and not o["dma"]:
                    continue
                keep.add(d)
            o["deps"] = keep
            for d in keep:
                needed[d] = True
        cnt = {e: 0 for e in ENGS}
        dma_idx = {e: 0 for e in ENGS}
        for i, o in enumerate(ops):
            if o["dma"]:
                k = dma_idx[o["eng"]]
                dma_idx[o["eng"]] += 1
                o["dsem"] = k % N_DMA_SEMS
                o["dval"] = 16 * (k // N_DMA_SEMS + 1)
            elif needed[i]:
                cnt[o["eng"]] += 1
                o["seq"] = cnt[o["eng"]]
        with contextlib.ExitStack() as st:
            esem = {e: st.enter_context(nc.semaphore(f"s_{e}")) for e in ENGS}
            dsem = {e: [st.enter_context(nc.semaphore(f"d_{e}{k}")) for k in range(N_DMA_SEMS)]
                    for e in ("sp", "pool")}
            block = st.enter_context(nc.Block())
            hook = dict(sp=block.sync, act=block.scalar, pool=block.gpsimd, dve=block.vector, pe=block.tensor)

            def run_engine(ename):
                def body(eng):
                    waited = {}
                    for i, o in enumerate(ops):
                        if o["eng"] != ename:
                            continue
                        for d in sorted(o["deps"]):
                            od = ops[d]
                            if od["dma"]:
                                key = ("d", od["eng"], od["dsem"])
                                val = od["dval"]
                                sem = dsem[od["eng"]][od["dsem"]]
                            else:
                                key = ("e", od["eng"])
                                val = od["seq"]
                                sem = esem[od["eng"]]
                            if waited.get(key, 0) >= val:
                                continue
                            eng.wait_ge(sem, val)
                            waited[key] = val
                        if o["dma"]:
                            key = ("d", ename, o["dsem"])
                            pv = o["dval"] - 16
                            if pv > 0 and waited.get(key, 0) < pv:
                                eng.wait_ge(dsem[ename][o["dsem"]], pv)
                                waited[key] = pv
                            ins = o["fn"](eng)
                            ins.then_inc(dsem[ename][o["dsem"]], 16)
                        else:
                            ins = o["fn"](eng)
                            if needed[i]:
                                ins.then_inc(esem[ename], 1)
                return body

            for e in ENGS:
                hook[e](run_engine(e))


class Rot:
    def __init__(self, items):
        self.items = list(items)
        self.i = 0

    def next(self):
        v = self.items[self.i % len(self.items)]
        self.i += 1
        return v


def build_program(dbg=None):
    nc = bass.Bass("TRN2", target_bir_lowering=False)
    S = Sched()
    OP = S.op
    din = {}

    def inp(name, shape):
        din[name] = nc.dram_tensor(name, list(shape), F32, kind="ExternalInput").ap()
        return din[name]

    x_d = inp("x", [T, D]); c_d = inp("c", [1, D]); ctx_d = inp("ctx", [NCTX, D]); cctx_d = inp("c_ctx", [1, D])
    wmod_d = inp("w_mod", [D, 9 * D]); bmod_d = inp("b_mod", [1, 9 * D])
    gffn1_d = inp("g_ffn1", [1, D]); wgu1_d = inp("w_gu1", [D, 2 * DFF]); wdn1_d = inp("w_down1", [DFF, D])
    gmix_d = inp("g_mix", [1, D]); win_d = inp("w_in", [D, 672]); gcq_d = inp("g_cq", [1, 256])
    wuq_d = inp("w_uq", [256, 1152]); gckv_d = inp("g_ckv", [1, 128]); wukv_d = inp("w_ukv", [128, 1536])
    lamre_d = inp("lam_re", [32, 64]); lamim_d = inp("lam_im", [32, 64]); logdt_d = inp("log_dt", [1, 32])
    bre_d = inp("b_re", [32, 64, 16]); bim_d = inp("b_im", [32, 64, 16])
    cre_d = inp("c_re", [512, 64]); cim_d = inp("c_im", [512, 64])
    dskip_d = inp("d_skip", [1, 256]); wglu_d = inp("w_glu", [256, 512])
    gmla_d = inp("g_mla_out", [1, 768]); gssm_d = inp("g_ssm_out", [1, 256]); wout_d = inp("w_out", [D, D])
    gffn2_d = inp("g_ffn2", [1, D]); wgu2_d = inp("w_gu2", [D, 2 * DFF]); wdn2_d = inp("w_down2", [DFF, D])
    gfin_d = inp("g_final", [1, D])
    out_d = nc.dram_tensor("out", [T, D], F32, kind="ExternalOutput").ap()

    def scratch(name, shape, dt):
        return nc.dram_tensor(name, list(shape), dt, kind=("ExternalOutput" if dbg else "Internal")).ap()

    x1_s = scratch("x1_s", [T, D], F32)
    cqn_s = scratch("cqn_s", [256, T], BF16)
    ckvn_s = scratch("ckvn_s", [128, T + NCTX], BF16)
    kr_s = scratch("kr_s", [2, 32, T + NCTX], F32)
    u_s = scratch("u_s", [8, 256, NJ], BF16)
    y_s = scratch("y_s", [8, 256, 512], F32)
    attn_s = scratch("attn_s", [768, T], BF16)
    mrow_s = scratch("mrow_s", [2, 9 * D], F32)
    B_x1 = Buf(); B_cqn = Buf(); B_ckvn = Buf(); B_kr = Buf(); B_u = Buf(); B_y = Buf(); B_attn = Buf(); B_mrow = Buf()

    dbg_out = {}

    def dbg_dump(name, src_ap, shape, dt, buf):
        t = nc.dram_tensor("dbg_" + name, list(shape), dt, kind="ExternalOutput").ap()
        dbg_out[name] = OP("sp", lambda e: e.dma_start(out=t, in_=src_ap), reads=[buf], dma=True)

    out_dmas = []
    pst = contextlib.ExitStack()
    with pst:
        _names = {}

        def sb(st, name, shape, dt):
            _names[name] = _names.get(name, 0) + 1
            if _names[name] > 1:
                name = f"{name}_v{_names[name]}"
            return st.enter_context(nc.sbuf_tensor(name, list(shape), dt))

        psall = pst.enter_context(nc.psum_tensor("psall", [128, 4096], F32))
        psb = [psall[:, i * 512:(i + 1) * 512] for i in range(8)]
        PB = [Buf(f"ps{i}") for i in range(8)]

        ident = sb(pst, "ident", [128, 128], F32); B_ident = Buf()
        ones = sb(pst, "ones", [128, 128], F32); B_ones = Buf()
        onesb = sb(pst, "onesb", [128, 128], BF16); B_onesb = Buf()
        iot = sb(pst, "iot", [128, 128], F32); B_iot = Buf()
        sel = sb(pst, "sel", [2, 2, 128], F32); B_sel = Buf()
        modfm = sb(pst, "modfm", [128, 72, 2], F32); B_modfm = Buf()
        AB = sb(pst, "AB", [128, 3, 2, 2, 8], F32); B_AB = Buf()
        gfm = sb(pst, "gfm", [128, 3, 8], F32); B_gfm = Buf()
        gsm = sb(pst, "gsm", [128, 4, 2], F32); B_gsm = Buf()
        gmla = sb(pst, "gmla", [128, 6], F32); B_gmla = Buf()
        scs = sb(pst, "scs", [128, 8, 2], F32); B_scs = Buf()
        rbsh = sb(pst, "rbsh", [2, 512], F32); B_rbsh = Buf()
        halfpi = sb(pst, "halfpi", [128, 1], F32); B_halfpi = Buf()
        OP("dve", lambda e: e.memset(halfpi[:], math.pi / 2.0), writes=[B_halfpi])

        OP("pool", lambda e: e.iota(iot[:], pattern=[[1, 128]], base=0, channel_multiplier=-1, allow_small_or_imprecise_dtypes=True), writes=[B_iot])
        OP("dve", lambda e: e.tensor_single_scalar(out=ident[:], in_=iot[:], scalar=0.0, op=ALU.is_equal),
           reads=[B_iot], writes=[B_ident])
        OP("dve", lambda e: e.memset(ones[:], 1.0), writes=[B_ones])
        OP("dve", lambda e: e.memset(onesb[:], 1.0), writes=[B_onesb])
        OP("dve", lambda e: e.tensor_copy(out=sel[0:2, 0, :], in_=ident[0:2, 0:1].to_broadcast([2, 128])),
           reads=[B_ident], writes=[B_sel])
        OP("dve", lambda e: e.tensor_copy(out=sel[0:2, 1, :], in_=ident[0:2, 1:2].to_broadcast([2, 128])),
           reads=[B_ident], writes=[B_sel])

        def small_fm_load(dst_ap, src_row_ap, nk, buf):
            OP("sp", lambda e: e.dma_start(out=dst_ap, in_=src_row_ap.rearrange("o (k p) -> p (o k)", p=128),
                                           allow_slow_non_contiguous=True), writes=[buf], dma=True)

        small_fm_load(gfm[:, 0, :], gffn1_d, 8, B_gfm)
        small_fm_load(gfm[:, 1, :], gmix_d, 8, B_gfm)
        small_fm_load(gfm[:, 2, :], gffn2_d, 8, B_gfm)
        small_fm_load(gsm[:, 0, :], gcq_d, 2, B_gsm)
        small_fm_load(gsm[:, 1, 0:1], gckv_d, 1, B_gsm)
        small_fm_load(gsm[:, 2, :], gssm_d, 2, B_gsm)
        small_fm_load(gmla[:, :], gmla_d, 6, B_gmla)

        def load_w_bf16(st, name, src, nk, ncols, chunk_cols=2048):
            t = sb(st, name, [128, nk, ncols], BF16)
            bufs = [Buf() for _ in range(nk)]
            v = src.rearrange("(k p) n -> p k n", p=128)
            for k in range(nk):
                OP("pool", lambda e, k=k: e.dma_start(out=t[:, k, :], in_=v[:, k, :], max_dma_last_dim=chunk_cols * 4),
                   writes=[bufs[k]], dma=True)
            return t, bufs

        stw1 = contextlib.ExitStack()
        wgu, B_wgu = load_w_bf16(stw1, "wgu1", wgu1_d, 8, 2 * DFF)
        wdn, B_wdn = load_w_bf16(stw1, "wdn1", wdn1_d, NFC, D)
        win, B_win = load_w_bf16(stw1, "win", win_d, 8, 672)
        wmod_v = wmod_d.rearrange("(k p) n -> p k n", p=128)

        def adaln_issue(blk, w, bw, bm, bb):
            cs = slice(blk * 512, (blk + 1) * 512)
            OP("sp", lambda e: e.dma_start(out=w[:], in_=wmod_v[:, :, cs]), writes=[bw], dma=True)
            OP("sp", lambda e: e.dma_start(out=bm[:], in_=bmod_d[:, cs]), writes=[bb], dma=True)

        def adaln_block(blk, w, bw, bm, bb, rb, brb, issue=True):
            cs = slice(blk * 512, (blk + 1) * 512)
            if issue:
                adaln_issue(blk, w, bw, bm, bb)
            for k in range(8):
                OP("pe", lambda e, k=k: e.matmul(psb[0][0:2, :], lhsT=scs[:, k, :], rhs=w[:, k, :], start=(k == 0), stop=False),
                   reads=[B_scs, bw], writes=[PB[0]])
            OP("pe", lambda e: e.matmul(psb[0][0:2, :], lhsT=ones[0:1, 0:2], rhs=bm[0:1, :], start=False, stop=True),
               reads=[B_ones, bb], writes=[PB[0]])
            OP("act", lambda e: e.activation(out=rb[:], in_=psb[0][0:2, :], func=AF.Copy), reads=[PB[0]], writes=[brb])
            OP("sp", lambda e: e.dma_start(out=mrow_s[:, cs], in_=rb[:]), reads=[brb], writes=[B_mrow], dma=True)
            for q in range(4):
                OP("pe", lambda e, q=q: e.matmul(psb[1][:, 2 * q:2 * q + 2], lhsT=rb[0:2, q * 128:(q + 1) * 128], rhs=ident[0:2, 0:2],
                                                 start=True, stop=True), reads=[brb, B_ident], writes=[PB[1]])
            OP("dve", lambda e: e.tensor_copy(out=modfm[:, blk * 4:(blk + 1) * 4, :], in_=psb[1][:, 0:8].rearrange("p (q r) -> p q r", r=2)),
               reads=[PB[1]], writes=[B_modfm])

        def adaln_AB(sites):
            for site, (shv, scv) in sites:
                for r in range(2):
                    OP("dve", lambda e, site=site, scv=scv, r=r: e.scalar_tensor_tensor(
                        out=AB[:, site, 0, r, :], in0=modfm[:, scv * 8:(scv + 1) * 8, r], scalar=1.0,
                        in1=gfm[:, site, :], op0=ALU.add, op1=ALU.mult), reads=[B_modfm, B_gfm], writes=[B_AB])
                    OP("dve", lambda e, site=site, shv=shv, r=r: e.tensor_copy(
                        out=AB[:, site, 1, r, :], in_=modfm[:, shv * 8:(shv + 1) * 8, r]), reads=[B_modfm], writes=[B_AB])

        NBLK0 = 10
        with contextlib.ExitStack() as st0:
            scr = sb(st0, "scr", [128, 8, 2], F32); B_scr = Buf()
            wmb = [sb(st0, f"wmb{i}", [128, 8, 512], F32) for i in range(2)]; B_wmb = [Buf(), Buf()]
            bmb = [sb(st0, f"bmb{i}", [1, 512], F32) for i in range(2)]; B_bmb = [Buf(), Buf()]
            rowb = [sb(st0, f"rowb{i}", [2, 512], F32) for i in range(2)]; B_rowb = [Buf(), Buf()]
            OP("sp", lambda e: e.dma_start(out=scr[:, :, 0], in_=c_d.rearrange("o (k p) -> p (o k)", p=128),
                                           allow_slow_non_contiguous=True), writes=[B_scr], dma=True)
            OP("sp", lambda e: e.dma_start(out=scr[:, :, 1], in_=cctx_d.rearrange("o (k p) -> p (o k)", p=128),
                                           allow_slow_non_contiguous=True), writes=[B_scr], dma=True)
            OP("act", lambda e: e.activation(out=scs[:], in_=scr[:], func=AF.Silu), reads=[B_scr], writes=[B_scs])
            for blk in range(NBLK0):
                adaln_block(blk, wmb[blk % 2], B_wmb[blk % 2], bmb[blk % 2], B_bmb[blk % 2], rowb[blk % 2], B_rowb[blk % 2])
            adaln_AB([(0, (0, 1)), (1, (3, 4))])
        S.barrier()


        def bcast_rows(st, name, col0, r, scale):
            t = sb(st, name, [128, D], F32); bt = Buf()
            for hf in range(2):
                cs = slice(col0 + hf * 512, col0 + (hf + 1) * 512)
                OP("sp", lambda e, cs=cs: e.dma_start(out=rbsh[:], in_=mrow_s[:, cs]), reads=[B_mrow],
                   writes=[B_rbsh], dma=True)
                OP("pe", lambda e: e.matmul(psb[0][:, :], lhsT=sel[0:2, r, :], rhs=rbsh[0:2, :],
                                            start=True, stop=True), reads=[B_rbsh, B_sel], writes=[PB[0]])
                OP("act", lambda e, hf=hf: e.activation(out=t[:, hf * 512:(hf + 1) * 512], in_=psb[0][:, :],
                                                        func=AF.Copy, scale=scale), reads=[PB[0]], writes=[bt])
            return t, bt

        def bcast_vec(st, name, src_row, n):
            t = sb(st, name, [128, n], F32); bt = Buf()
            for c0 in range(0, n, 512):
                c1 = min(n, c0 + 512)
                OP("sp", lambda e, c0=c0, c1=c1: e.dma_start(out=rbsh[0:1, 0:c1 - c0], in_=src_row[:, c0:c1]),
                   writes=[B_rbsh], dma=True)
                OP("pe", lambda e, c0=c0, c1=c1: e.matmul(psb[0][:, 0:c1 - c0], lhsT=ones[0:1, :], rhs=rbsh[0:1, 0:c1 - c0],
                                                            start=True, stop=True), reads=[B_rbsh, B_ones], writes=[PB[0]])
                OP("act", lambda e, c0=c0, c1=c1: e.activation(out=t[:, c0:c1], in_=psb[0][:, 0:c1 - c0], func=AF.Copy),
                   reads=[PB[0]], writes=[bt])
            return t, bt

        def norm_to_hT(xt, B_xt, nsub, site, r, hT, B_hT, W, part=0):
            ntok = nsub * 128
            if part in (0, 1):
                norm_p1(xt, B_xt, nsub, W)
            if part in (0, 2):
                norm_p2(nsub, site, r, hT, B_hT, W)

        def norm_p1(xt, B_xt, nsub, W):
            for i in range(nsub):
                OP("act", lambda e, i=i: e.activation(out=W["xn"][:, i, :], in_=xt[:, i, :], func=AF.Square,
                                                      accum_out=W["ssq"][:, i:i + 1]),
                   reads=[B_xt], writes=[W["B_xn"][i], W["B_ssq"]])
            OP("act", lambda e: e.activation(out=W["rstd"][:, 0:nsub], in_=W["ssq"][:, 0:nsub], func=AF.Sqrt,
                                             scale=1.0 / D, bias=W["eps"][:, 0:1]),
               reads=[W["B_ssq"], W["B_eps"]], writes=[W["B_rstd"]])
            OP("dve", lambda e: e.reciprocal(out=W["rstd"][:, 0:nsub], in_=W["rstd"][:, 0:nsub]),
               reads=[W["B_rstd"]], writes=[W["B_rstd"]])
            for i in range(nsub):
                xn = W["xn"]
                OP("dve", lambda e, i=i: e.tensor_scalar(out=xn[:, i, :], in0=xt[:, i, :], scalar1=W["rstd"][:, i:i + 1],
                                                         scalar2=None, op0=ALU.mult),
                   reads=[B_xt, W["B_rstd"]], writes=[W["B_xn"][i]])

        def norm_p2(nsub, site, r, hT, B_hT, W):
            ntok = nsub * 128
            for k in range(8):
                pb = W["tp_rot"].next()
                for i in range(nsub):
                    OP("pe", lambda e, i=i, k=k, pb=pb: e.transpose(out=psb[pb][:, i * 128:(i + 1) * 128],
                                                                     in_=W["xn"][:, i, k * 128:(k + 1) * 128],
                                                                     identity=ident[:]),
                       reads=[W["B_xn"][i], B_ident], writes=[PB[pb]])
                OP("act", lambda e, k=k, pb=pb: e.activation(out=hT[:, k, 0:ntok], in_=psb[pb][:, 0:ntok],
                                                             func=AF.Identity, scale=AB[:, site, 0, r, k:k + 1],
                                                             bias=AB[:, site, 1, r, k:k + 1]),
                   reads=[PB[pb], B_AB], writes=[B_hT[k]])

        def ffn(xt, B_xt, nsub, hT, B_hT, wgu, B_wgu, wdn, B_wdn, gbc, B_gbc, W, bg=None):
            ntok = nsub * 128
            gu_rot = W["gu_rot"]
            dn = W["dn_banks"]
            pend = []

            def down(c, hid, bh):
                gi = 0
                for i in range(nsub):
                    for e2 in range(2):
                        pb = dn[gi]; gi += 1
                        OP("pe", lambda e, i=i, e2=e2, pb=pb, c=c, hid=hid: e.matmul(
                            psb[pb][:, :], lhsT=hid[:, i * 128:(i + 1) * 128], rhs=wdn[:, c, e2 * 512:(e2 + 1) * 512],
                            start=(c == 0), stop=(c == NFC - 1)),
                           reads=[bh, B_wdn[c]], writes=[PB[pb]])

            for c in range(NFC):
                pb = gu_rot.next()
                for half, col0 in ((0, c * 128), (1, DFF + c * 128)):
                    for k in range(8):
                        OP("pe", lambda e, k=k, pb=pb, half=half, col0=col0: e.matmul(
                            psb[pb][:, half * 256:half * 256 + ntok], lhsT=wgu[:, k, col0:col0 + 128],
                            rhs=hT[:, k, 0:ntok], start=(k == 0), stop=(k == 7)),
                           reads=[B_wgu[k], B_hT[k]], writes=[PB[pb]])
                sg, bsg = W["sg_rot"].next()
                hid, bh = W["hid_rot"].next()
                OP("act", lambda e, pb=pb, sg=sg: e.activation(out=sg[:, 0:ntok], in_=psb[pb][:, 0:ntok], func=AF.Silu),
                   reads=[PB[pb]], writes=[bsg])
                OP("dve", lambda e, pb=pb, sg=sg, hid=hid: e.tensor_tensor(out=hid[:, 0:ntok], in0=psb[pb][:, 256:256 + ntok],
                                                                           in1=sg[:, 0:ntok], op=ALU.mult),
                   reads=[PB[pb], bsg], writes=[bh])
                pend.append((c, hid, bh))
                if len(pend) > 2:
                    down(*pend.pop(0))
                for _ in range(2):
                    if bg:
                        bg.pop(0)()
            while pend:
                down(*pend.pop(0))
            gi = 0
            for i in range(nsub):
                for e2 in range(2):
                    pb = dn[gi]; gi += 1
                    tmp, btmp = W["tmp_rot"].next()
                    OP("dve", lambda e, pb=pb, e2=e2, tmp=tmp: e.tensor_tensor(
                        out=tmp[:], in0=psb[pb][:, :], in1=gbc[:, e2 * 512:(e2 + 1) * 512], op=ALU.mult),
                       reads=[PB[pb], B_gbc], writes=[btmp])
                    OP("pool" if gi % 2 else "dve", lambda e, i=i, e2=e2, tmp=tmp: e.tensor_tensor(
                        out=xt[:, i, e2 * 512:(e2 + 1) * 512], in0=xt[:, i, e2 * 512:(e2 + 1) * 512], in1=tmp[:],
                        op=ALU.add),
                       reads=[btmp, B_xt], writes=[B_xt])

        def common_work(st):
            W = {}
            W["ssq"] = sb(st, "ssq", [128, 2], F32); W["B_ssq"] = Buf()
            W["rstd"] = sb(st, "rstd", [128, 2], F32); W["B_rstd"] = Buf()
            W["eps"] = sb(st, "epsc", [128, 1], F32); W["B_eps"] = Buf()
            OP("dve", lambda e: e.memset(W["eps"][:], EPS), writes=[W["B_eps"]])
            W["xn"] = sb(st, "xn", [128, 2, D], F32); W["B_xn"] = [Buf(), Buf()]
            W["tp_rot"] = Rot([0, 1])
            W["gu_rot"] = Rot([2, 3])
            W["dn_banks"] = [4, 5, 6, 7]
            sgs = [(sb(st, f"sg{i}", [128, TN], F32), Buf()) for i in range(2)]
            W["sg_rot"] = Rot(sgs)
            hids = [(sb(st, f"hid{i}", [128, TN], BF16), Buf()) for i in range(4)]
            W["hid_rot"] = Rot(hids)
            tmps = [(sb(st, f"tmpr{i}", [128, 512], F32), Buf()) for i in range(2)]
            W["tmp_rot"] = Rot(tmps)
            return W

        with contextlib.ExitStack() as st1:
            B_win1 = Buf()
            winsw = sb(st1, "winsw", [128, 8, 32], BF16); B_winsw = Buf()
            for k in range(8):
                OP("dve", lambda e, k=k: e.tensor_scalar(
                    out=winsw[:, k, :].rearrange("p (i two) -> p i two", two=2)[:, :, 0],
                    in0=win[:, k, 384:416].rearrange("p (i two) -> p i two", two=2)[:, :, 1],
                    scalar1=-1.0, scalar2=None, op0=ALU.mult), reads=[B_win[k]], writes=[B_winsw])
                OP("dve", lambda e, k=k: e.tensor_copy(
                    out=winsw[:, k, :].rearrange("p (i two) -> p i two", two=2)[:, :, 1],
                    in_=win[:, k, 384:416].rearrange("p (i two) -> p i two", two=2)[:, :, 0]),
                   reads=[B_win[k]], writes=[B_winsw])
            g2x, B_g2x = bcast_rows(st1, "g2x", 2 * D, 0, 0.5)
            g2c, B_g2c = bcast_rows(st1, "g2c", 2 * D, 1, 0.5)
            W = common_work(st1)
            xts = [(sb(st1, "xtA", [128, 2, D], F32), Buf()), (sb(st1, "xtB", [128, 2, D], F32), Buf())]

            def load_x(ti):
                t0_ = 0 if ti == 0 else (ti - 1) * TN
                src_ = ctx_d if ti == 0 else x_d
                xt_l, B_l = xts[ti % 2]
                OP("sp", lambda e, src_=src_, t0_=t0_, xt_l=xt_l: e.dma_start(
                    out=xt_l[:], in_=src_[t0_:t0_ + TN, :].rearrange("(i p) d -> p i d", p=128)), writes=[B_l], dma=True)
            hT = sb(st1, "hT", [128, 8, TN], BF16); B_hT = [Buf() for _ in range(8)]
            cqf = sb(st1, "cqf", [128, 3, TN], F32); B_cqf = [Buf() for _ in range(3)]
            sqf = sb(st1, "sqf", [128, 3, TN], F32); B_sqf = [Buf() for _ in range(3)]
            rbc = sb(st1, "rbc", [128, 2, TN], F32); B_rbc = [Buf(), Buf()]
            cqn_t = sb(st1, "cqn_t", [128, 3, TN], BF16); B_cqn_t = [Buf() for _ in range(3)]
            krt = sb(st1, "krt", [128, 2, TN], F32); B_krt = Buf()
            usj = sb(st1, "usj", [128, 2, 8, TN // 8], BF16); B_usj = [Buf(), Buf()]
            print("phase1 sbuf remaining", nc.sbuf_bytes_remaining)
            pj_rot = Rot([2, 3, 4, 5, 6, 7])
            ntiles = 1 + T // TN
            load_x(0)
            for ti in range(ntiles):
                is_ctx = ti == 0
                r = 1 if is_ctx else 0
                t0 = 0 if is_ctx else (ti - 1) * TN
                kcol0 = 0 if is_ctx else NCTX + t0
                if ti + 1 < ntiles:
                    load_x(ti + 1)
                xt, B_xt = xts[ti % 2]
                norm_to_hT(xt, B_xt, 2, 0, r, hT, B_hT, W, part=(0 if ti == 0 else 2))
                ffn(xt, B_xt, 2, hT, B_hT, wgu, B_wgu, wdn, B_wdn, g2c if is_ctx else g2x, B_g2c if is_ctx else B_g2x, W)
                if not is_ctx:
                    OP("pool", lambda e, t0=t0, xt=xt: e.dma_start(
                        out=x1_s[t0:t0 + TN, :].rearrange("(i p) d -> p i d", p=128), in_=xt[:]),
                       reads=[B_xt], writes=[B_x1], dma=True)
                norm_to_hT(xt, B_xt, 2, 1, r, hT, B_hT, W)
                if ti + 1 < ntiles:
                    xt_n, B_xt_n = xts[(ti + 1) % 2]
                    norm_to_hT(xt_n, B_xt_n, 2, 0, 0, hT, B_hT, W, part=1)
                chunks = ([] if is_ctx else [(0, 0), (1, 128)]) + [(2, 256)]
                for ci, col0 in chunks:
                    pb = pj_rot.next()
                    for k in range(8):
                        OP("pe", lambda e, k=k, pb=pb, col0=col0: e.matmul(
                            psb[pb][:, 0:TN], lhsT=win[:, k, col0:col0 + 128], rhs=hT[:, k, :],
                            start=(k == 0), stop=(k == 7)), reads=[B_win[k], B_hT[k]], writes=[PB[pb]])
                    OP("act", lambda e, pb=pb, ci=ci: e.activation(out=cqf[:, ci, :], in_=psb[pb][:, 0:TN], func=AF.Copy),
                       reads=[PB[pb]], writes=[B_cqf[ci]])
                    OP("dve", lambda e, ci=ci: e.tensor_tensor(out=sqf[:, ci, :], in0=cqf[:, ci, :], in1=cqf[:, ci, :],
                                                               op=ALU.mult), reads=[B_cqf[ci]], writes=[B_sqf[ci]])
                groups = ([] if is_ctx else [((0, 1), 0, 256.0, 0)]) + [((2,), 1, 128.0, 1)]
                for cis, ri, nfeat, gi in groups:
                    pb = pj_rot.next()
                    for n_, ci in enumerate(cis):
                        OP("pe", lambda e, pb=pb, ci=ci, n_=n_, cis=cis: e.matmul(
                            psb[pb][:, 0:TN], lhsT=ones[:, :], rhs=sqf[:, ci, :], start=(n_ == 0),
                            stop=(n_ == len(cis) - 1)), reads=[B_ones, B_sqf[ci]], writes=[PB[pb]])
                    OP("act", lambda e, pb=pb, ri=ri, nfeat=nfeat: e.activation(
                        out=rbc[:, ri, :], in_=psb[pb][:, 0:TN], func=AF.Sqrt, scale=1.0 / nfeat, bias=W["eps"][:, 0:1]),
                       reads=[PB[pb], W["B_eps"]], writes=[B_rbc[ri]])
                    OP("dve", lambda e, ri=ri: e.reciprocal(out=rbc[:, ri, :], in_=rbc[:, ri, :]),
                       reads=[B_rbc[ri]], writes=[B_rbc[ri]])
                    for n_, ci in enumerate(cis):
                        OP("dve", lambda e, ci=ci, ri=ri, gi=gi, n_=n_: e.scalar_tensor_tensor(
                            out=cqn_t[:, ci, :], in0=cqf[:, ci, :], scalar=gsm[:, gi, n_:n_ + 1], in1=rbc[:, ri, :],
                            op0=ALU.mult, op1=ALU.mult), reads=[B_cqf[ci], B_gsm, B_rbc[ri]], writes=[B_cqn_t[ci]])
                        if ci < 2:
                            OP("pool", lambda e, ci=ci, t0=t0: e.dma_start(
                                out=cqn_s[ci * 128:(ci + 1) * 128, t0:t0 + TN], in_=cqn_t[:, ci, :]),
                               reads=[B_cqn_t[ci]], writes=[B_cqn], dma=True)
                        else:
                            OP("pool", lambda e, kcol0=kcol0: e.dma_start(
                                out=ckvn_s[:, kcol0:kcol0 + TN], in_=cqn_t[:, 2, :]),
                               reads=[B_cqn_t[2]], writes=[B_ckvn], dma=True)
                pb = pj_rot.next()
                for k in range(8):
                    OP("pe", lambda e, k=k, pb=pb: e.matmul(psb[pb][64:96, 0:TN], lhsT=win[:, k, 384:416], rhs=hT[:, k, :],
                                                            start=(k == 0), stop=(k == 7)),
                       reads=[B_win[k], B_hT[k]], writes=[PB[pb]])
                for k in range(8):
                    OP("pe", lambda e, k=k, pb=pb: e.matmul(psb[pb][64:96, 256:256 + TN], lhsT=winsw[:, k, :], rhs=hT[:, k, :],
                                                            start=(k == 0), stop=(k == 7)),
                       reads=[B_winsw, B_hT[k]], writes=[PB[pb]])
                OP("act", lambda e, pb=pb: e.activation(
                    out=krt[64:96, :, :], in_=psb[pb][64:96, :].rearrange("p (a t) -> p a t", a=2), func=AF.Copy),
                   reads=[PB[pb]], writes=[B_krt])
                for a in range(2):
                    OP("pool", lambda e, a=a, kcol0=kcol0: e.dma_start(out=kr_s[a, :, kcol0:kcol0 + TN], in_=krt[64:96, a, :]),
                       reads=[B_krt], writes=[B_kr], dma=True)
                for gc in range(2):
                    pb = pj_rot.next()
                    col0 = 416 + gc * 128
                    for k in range(8):
                        OP("pe", lambda e, k=k, pb=pb, col0=col0: e.matmul(
                            psb[pb][:, 0:TN], lhsT=win[:, k, col0:col0 + 128], rhs=hT[:, k, :],
                            start=(k == 0), stop=(k == 7)), reads=[B_win[k], B_hT[k]], writes=[PB[pb]])
                    OP("act", lambda e, pb=pb, gc=gc: e.activation(
                        out=usj[:, gc, :, :], in_=psb[pb][:, 0:TN].rearrange("p (j s) -> p s j", s=8), func=AF.Copy),
                       reads=[PB[pb]], writes=[B_usj[gc]])
                    nj = TN // 8
                    j0s = [0, 544] if is_ctx else [32 + t0 // 8]
                    for j0 in j0s:
                        OP("pool", lambda e, gc=gc, j0=j0: e.dma_start(
                            out=u_s[:, gc * 128:(gc + 1) * 128, j0:j0 + nj].rearrange("s p j -> p s j"),
                            in_=usj[:, gc, :, :]), reads=[B_usj[gc]], writes=[B_u], dma=True)
        S.barrier()
        stw1.close()

        if dbg == "p1":
            OP("sp", lambda e: e.nop())
            S.emit(nc)
            return nc, {}

        TPS = 6.283185

        MAGIC = 12582912.0

        def frac_sincos(sin_t, cos_t, F_ap, T1, T2, bF, bT1, bT2, bsin, bcos, hp):
            OP("dve", lambda e: e.tensor_scalar(out=T1, in0=F_ap, scalar1=MAGIC, scalar2=None, op0=ALU.add), reads=[bF], writes=[bT1])
            OP("dve", lambda e: e.scalar_tensor_tensor(out=T2, in0=T1, scalar=MAGIC, in1=F_ap, op0=ALU.subtract, op1=ALU.subtract),
               reads=[bT1, bF], writes=[bT2])
            OP("act", lambda e: e.activation(out=sin_t, in_=T2, func=AF.Sin, scale=-TPS), reads=[bT2], writes=[bsin])
            OP("dve", lambda e: e.scalar_tensor_tensor(out=T1, in0=T2, scalar=-1.0, in1=T2, op0=ALU.mult, op1=ALU.max), reads=[bT2], writes=[bT1])
            OP("act", lambda e: e.activation(out=cos_t, in_=T1, func=AF.Sin, scale=-TPS, bias=hp), reads=[bT1, B_halfpi], writes=[bcos])

        with contextlib.ExitStack() as st2:
            Kmat = sb(st2, "Kmat", [128, 32, 128], BF16); B_Kmat = Buf()
            Smat = sb(st2, "Smat", [128, 32, 128], BF16); B_Smat = Buf()
            SmatT = sb(st2, "SmatT", [128, 32, 128], BF16); B_SmatT = Buf()
            Dm1 = sb(st2, "Dm1", [128, 32, 8, 16], BF16); B_Dm1 = Buf()
            Dm2 = sb(st2, "Dm2", [128, 32, 8, 16], BF16); B_Dm2 = Buf()
            FPHI = sb(st2, "FPHI", [128, 32], F32); B_FPHI = Buf()
            RHO = sb(st2, "RHO", [128, 32], F32); B_RHO = Buf()
            u8all = sb(st2, "u8all", [128, 16, NJ], BF16); B_u8 = Buf()
            for s_ in range(8):
                OP("sp", lambda e, s_=s_: e.dma_start(out=u8all[16 * s_:16 * s_ + 16, :, :],
                                                      in_=u_s[s_].rearrange("(g c) j -> c g j", c=16)),
                   reads=[B_u], writes=[B_u8], dma=True)
            with contextlib.ExitStack() as st2a:
                LRr = sb(st2a, "LRr", [32, 2, 64], F32); B_LRr = Buf()
                LIr = sb(st2a, "LIr", [32, 2, 64], F32); B_LIr = Buf()
                CRr = sb(st2a, "CRr", [128, 4, 2, 64], F32); B_CRr = Buf()
                CIr = sb(st2a, "CIr", [128, 4, 2, 64], F32); B_CIr = Buf()
                BR = sb(st2a, "BR", [128, 32, 16], F32); B_BR = Buf()
                BI = sb(st2a, "BI", [128, 32, 16], F32); B_BI = Buf()
                DSK = sb(st2a, "DSK", [128, 16], F32); B_DSK = Buf()
                LD = sb(st2a, "LD", [1, 32], F32); B_LD = Buf()
                for a in range(2):
                    OP("sp", lambda e, a=a: e.dma_start(out=LRr[:, a, :], in_=lamre_d), writes=[B_LRr], dma=True)
                    OP("sp", lambda e, a=a: e.dma_start(out=LIr[:, a, :], in_=lamim_d), writes=[B_LIr], dma=True)
                    OP("sp", lambda e, a=a: e.dma_start(out=CRr[:, :, a, :], in_=cre_d.rearrange("(q r) n -> r q n", r=128)),
                       writes=[B_CRr], dma=True)
                    OP("sp", lambda e, a=a: e.dma_start(out=CIr[:, :, a, :], in_=cim_d.rearrange("(q r) n -> r q n", r=128)),
                       writes=[B_CIr], dma=True)
                    OP("sp", lambda e, a=a: e.dma_start(out=BR[a * 64:(a + 1) * 64, :, :], in_=bre_d.rearrange("g n c -> n g c")),
                       writes=[B_BR], dma=True)
                    OP("sp", lambda e, a=a: e.dma_start(out=BI[a * 64:(a + 1) * 64, :, :], in_=bim_d.rearrange("g n c -> n g c")),
                       writes=[B_BI], dma=True)
                for s_ in range(8):
                    OP("sp", lambda e, s_=s_: e.dma_start(out=DSK[16 * s_:16 * s_ + 16, :],
                                                          in_=dskip_d.rearrange("o (g c) -> c (o g)", c=16),
                                                          allow_slow_non_contiguous=True), writes=[B_DSK], dma=True)
                OP("sp", lambda e: e.dma_start(out=LD[:], in_=logdt_d), writes=[B_LD], dma=True)

                def t32(name, shape=(128, 32)):
                    return sb(st2a, name, list(shape), F32), Buf()
                LR, B_LR = t32("LR"); LI, B_LI = t32("LI"); DT, B_DT = t32("DT")
                CR = sb(st2a, "CR", [128, 32, 16], F32); B_CR = Buf()
                CI = sb(st2a, "CI", [128, 32, 16], F32); B_CI = Buf()
                OP("pe", lambda e: e.transpose(out=psb[0][:, 0:32], in_=LRr[:].rearrange("r a n -> r (a n)"),
                                               identity=ident[0:32, 0:32]), reads=[B_LRr, B_ident], writes=[PB[0]])
                OP("pe", lambda e: e.transpose(out=psb[0][:, 32:64], in_=LIr[:].rearrange("r a n -> r (a n)"),
                                               identity=ident[0:32, 0:32]), reads=[B_LIr, B_ident], writes=[PB[0]])
                OP("pe", lambda e: e.matmul(psb[0][:, 64:96], lhsT=ones[0:1, :], rhs=LD[0:1, :], start=True, stop=True),
                   reads=[B_LD, B_ones], writes=[PB[0]])
                OP("dve", lambda e: e.tensor_copy(out=LR[:], in_=psb[0][:, 0:32]), reads=[PB[0]], writes=[B_LR])
                OP("dve", lambda e: e.tensor_copy(out=LI[:], in_=psb[0][:, 32:64]), reads=[PB[0]], writes=[B_LI])
                OP("act", lambda e: e.activation(out=DT[:], in_=psb[0][:, 64:96], func=AF.Exp), reads=[PB[0]], writes=[B_DT])
                for q in range(4):
                    OP("pe", lambda e, q=q: e.transpose(out=psb[1][:, q * 128:(q + 1) * 128],
                                                        in_=CRr[:, q, :, :].rearrange("r a n -> r (a n)"), identity=ident[:]),
                       reads=[B_CRr, B_ident], writes=[PB[1]])
                    OP("pe", lambda e, q=q: e.transpose(out=psb[2][:, q * 128:(q + 1) * 128],
                                                        in_=CIr[:, q, :, :].rearrange("r a n -> r (a n)"), identity=ident[:]),
                       reads=[B_CIr, B_ident], writes=[PB[2]])
                OP("dve", lambda e: e.tensor_copy(out=CR[:].rearrange("p g c -> p (g c)"), in_=psb[1][:, :]),
                   reads=[PB[1]], writes=[B_CR])
                OP("dve", lambda e: e.tensor_copy(out=CI[:].rearrange("p g c -> p (g c)"), in_=psb[2][:, :]),
                   reads=[PB[2]], writes=[B_CI])
                LRc, B_LRc = t32("LRc"); LRDT, B_LRDT = t32("LRDT"); LIC, B_LIC = t32("LIC")
                OP("dve", lambda e: e.tensor_scalar(out=LRc[:], in0=LR[:], scalar1=-1e-4, scalar2=None, op0=ALU.min),
                   reads=[B_LR], writes=[B_LRc])
                OP("dve", lambda e: e.tensor_tensor(out=LRDT[:], in0=LRc[:], in1=DT[:], op=ALU.mult),
                   reads=[B_LRc, B_DT], writes=[B_LRDT])
                OP("dve", lambda e: e.scalar_tensor_tensor(out=LIC[:], in0=LI[:], scalar=1.0 / TWO_PI, in1=DT[:],
                                                           op0=ALU.mult, op1=ALU.mult), reads=[B_LI, B_DT], writes=[B_LIC])
                KV, B_KV = t32("KV", (128, 9, 32))
                for k in range(9):
                    OP("dve", lambda e, k=k: e.memset(KV[:, k, :], float(k)), writes=[B_KV])
                FK, B_FK = t32("FK", (128, 9, 32)); FKc, B_FKc = t32("FKc", (128, 9, 32))
                MAG, B_MAG = t32("MAG", (128, 9, 32))
                OP("dve", lambda e: e.tensor_tensor(out=FK[:], in0=KV[:], in1=LIC[:].unsqueeze(1).to_broadcast([128, 9, 32]),
                                                    op=ALU.mult), reads=[B_KV, B_LIC], writes=[B_FK])
                OP("dve", lambda e: e.tensor_scalar(out=FKc[:], in0=FK[:], scalar1=0.25, scalar2=None, op0=ALU.add),
                   reads=[B_FK], writes=[B_FKc])
                OP("dve", lambda e: e.tensor_tensor(out=MAG[:], in0=KV[:], in1=LRDT[:].unsqueeze(1).to_broadcast([128, 9, 32]),
                                                    op=ALU.mult), reads=[B_KV, B_LRDT], writes=[B_MAG])
                OP("act", lambda e: e.activation(out=MAG[:], in_=MAG[:], func=AF.Exp), reads=[B_MAG], writes=[B_MAG])
                FFt, B_FFt = t32("FFt", (128, 9, 32)); FRs, B_FRs = t32("FRs", (128, 9, 32)); FRc, B_FRc = t32("FRc", (128, 9, 32))
                SINk, B_SINk = t32("SINk", (128, 9, 32)); COSk, B_COSk = t32("COSk", (128, 9, 32))
                frac_sincos(SINk[:], COSk[:], FK[:], FFt[:], FRs[:], B_FK, B_FFt, B_FRs, B_SINk, B_COSk, halfpi[:, 0:1])
                AR, B_AR = t32("AR", (128, 9, 32)); AI, B_AI = t32("AI", (128, 9, 32))
                OP("dve", lambda e: e.tensor_tensor(out=AR[:], in0=MAG[:], in1=COSk[:], op=ALU.mult), reads=[B_MAG, B_COSk], writes=[B_AR])
                OP("dve", lambda e: e.tensor_tensor(out=AI[:], in0=MAG[:], in1=SINk[:], op=ALU.mult), reads=[B_MAG, B_SINk], writes=[B_AI])
                OP("dve", lambda e: e.tensor_scalar(out=FPHI[:], in0=FRs[:, 8, :], scalar1=-1.0, scalar2=None, op0=ALU.mult), reads=[B_FRs], writes=[B_FPHI])
                OP("dve", lambda e: e.tensor_copy(out=RHO[:], in_=MAG[:, 8, :]), reads=[B_MAG], writes=[B_RHO])
                ta, B_ta = t32("ta"); tb, B_tb = t32("tb"); rden, B_rden = t32("rden"); arm1, B_arm1 = t32("arm1")
                FRf, B_FRf = t32("FRf"); FIf, B_FIf = t32("FIf")
                OP("dve", lambda e: e.tensor_tensor(out=ta[:], in0=LRc[:], in1=LRc[:], op=ALU.mult), reads=[B_LRc], writes=[B_ta])
                OP("dve", lambda e: e.tensor_tensor(out=tb[:], in0=LI[:], in1=LI[:], op=ALU.mult), reads=[B_LI], writes=[B_tb])
                OP("dve", lambda e: e.tensor_tensor(out=ta[:], in0=ta[:], in1=tb[:], op=ALU.add), reads=[B_ta, B_tb], writes=[B_ta])
                OP("dve", lambda e: e.reciprocal(out=rden[:], in_=ta[:]), reads=[B_ta], writes=[B_rden])
                OP("dve", lambda e: e.tensor_scalar(out=arm1[:], in0=AR[:, 1, :], scalar1=-1.0, scalar2=None, op0=ALU.add),
                   reads=[B_AR], writes=[B_arm1])
                OP("dve", lambda e: e.tensor_tensor(out=ta[:], in0=arm1[:], in1=LRc[:], op=ALU.mult), reads=[B_arm1, B_LRc, B_rden], writes=[B_ta])
                OP("dve", lambda e: e.tensor_tensor(out=tb[:], in0=AI[:, 1, :], in1=LI[:], op=ALU.mult), reads=[B_AI, B_LI], writes=[B_tb])
                OP("dve", lambda e: e.tensor_tensor(out=ta[:], in0=ta[:], in1=tb[:], op=ALU.add), reads=[B_ta, B_tb], writes=[B_ta])
                OP("dve", lambda e: e.tensor_tensor(out=FRf[:], in0=ta[:], in1=rden[:], op=ALU.mult), reads=[B_ta, B_rden], writes=[B_FRf])
                OP("dve", lambda e: e.tensor_tensor(out=ta[:], in0=AI[:, 1, :], in1=LRc[:], op=ALU.mult), reads=[B_AI, B_LRc, B_FRf], writes=[B_ta])
                OP("dve", lambda e: e.tensor_tensor(out=tb[:], in0=arm1[:], in1=LI[:], op=ALU.mult), reads=[B_arm1, B_LI], writes=[B_tb])
                OP("dve", lambda e: e.tensor_tensor(out=ta[:], in0=ta[:], in1=tb[:], op=ALU.subtract), reads=[B_ta, B_tb], writes=[B_ta])
                OP("dve", lambda e: e.tensor_tensor(out=FIf[:], in0=ta[:], in1=rden[:], op=ALU.mult), reads=[B_ta, B_rden], writes=[B_FIf])
                BBR = sb(st2a, "BBR", [128, 32, 16], F32); B_BBR = Buf()
                BBI = sb(st2a, "BBI", [128, 32, 16], F32); B_BBI = Buf()
                w1 = sb(st2a, "w1", [128, 32, 16], F32); B_w1 = Buf()
                w2 = sb(st2a, "w2", [128, 32, 16], F32); B_w2 = Buf()

                def bc16(t, ps=slice(0, 128)):
                    n = ps.stop - ps.start
                    return t.unsqueeze(2).to_broadcast([n, 32, 16])

                OP("dve", lambda e: e.tensor_tensor(out=w1[:], in0=BR[:], in1=bc16(FRf[:]), op=ALU.mult), reads=[B_BR, B_FRf], writes=[B_w1])
                OP("dve", lambda e: e.tensor_tensor(out=w2[:], in0=BI[:], in1=bc16(FIf[:]), op=ALU.mult), reads=[B_BI, B_FIf], writes=[B_w2])
                OP("dve", lambda e: e.tensor_tensor(out=BBR[:], in0=w1[:], in1=w2[:], op=ALU.subtract), reads=[B_w1, B_w2], writes=[B_BBR])
                OP("dve", lambda e: e.tensor_tensor(out=w1[:], in0=BI[:], in1=bc16(FRf[:]), op=ALU.mult), reads=[B_BI, B_FRf, B_BBR], writes=[B_w1])
                OP("dve", lambda e: e.tensor_tensor(out=w2[:], in0=BR[:], in1=bc16(FIf[:]), op=ALU.mult), reads=[B_BR, B_FIf, B_BBR], writes=[B_w2])
                OP("dve", lambda e: e.tensor_tensor(out=BBI[:], in0=w1[:], in1=w2[:], op=ALU.add), reads=[B_w1, B_w2], writes=[B_BBI])
                Pf = sb(st2a, "Pf", [128, 32, 15, 16], F32); B_Pf = Buf()
                Pb = sb(st2a, "Pb", [128, 32, 15, 16], F32); B_Pb = Buf()
                OP("pool", lambda e: e.memset(Pf[:], 0.0), writes=[B_Pf])
                OP("pool", lambda e: e.memset(Pb[:], 0.0), writes=[B_Pb])
                for k in range(8):
                    for hf in range(2):
                        ps = slice(hf * 64, (hf + 1) * 64)
                        X1, bX1, X2, bX2 = (BBR, B_BBR, BBI, B_BBI) if hf == 0 else (BBI, B_BBI, BBR, B_BBR)
                        op2 = ALU.subtract if hf == 0 else ALU.add
                        OP("dve", lambda e, ps=ps, k=k, X1=X1: e.tensor_tensor(out=w1[ps], in0=X1[ps], in1=bc16(AR[ps, k, :], ps), op=ALU.mult),
                           reads=[bX1, B_AR], writes=[B_w1])
                        OP("dve", lambda e, ps=ps, k=k, X2=X2: e.tensor_tensor(out=w2[ps], in0=X2[ps], in1=bc16(AI[ps, k, :], ps), op=ALU.mult),
                           reads=[bX2, B_AI], writes=[B_w2])
                        OP("dve", lambda e, ps=ps, k=k, op2=op2: e.tensor_tensor(out=Pf[ps, :, 7 - k, :], in0=w1[ps], in1=w2[ps], op=op2),
                           reads=[B_w1, B_w2], writes=[B_Pf])
                        OP("pool", lambda e, ps=ps, k=k: e.tensor_copy(out=Pb[ps, :, 7 + k, :], in_=Pf[ps, :, 7 - k, :]),
                           reads=[B_Pf], writes=[B_Pb])
                Cm = sb(st2a, "Cm", [128, 32, 16], F32); B_Cm = Buf()
                OP("dve", lambda e: e.tensor_copy(out=Cm[0:64], in_=CR[0:64]), reads=[B_CR], writes=[B_Cm])
                OP("dve", lambda e: e.tensor_scalar(out=Cm[64:128], in0=CI[64:128], scalar1=-1.0, scalar2=None, op0=ALU.mult),
                   reads=[B_CI], writes=[B_Cm])
                MT = sb(st2a, "MT", [128, 32, 8, 16], F32); B_MT = Buf()
                for d_, (Pt_, bP, a0) in enumerate(((Pf, B_Pf, 0), (Pb, B_Pb, 7))):
                    gs = slice(d_ * 16, (d_ + 1) * 16)
                    sg_ = 1.0 if d_ == 0 else -1.0
                    OP("dve", lambda e, Pt_=Pt_, a0=a0, gs=gs, sg_=sg_: e.tensor_scalar(
                        out=MT[0:64, gs, :, :], in0=Pt_[64:128, gs, a0:a0 + 8, :], scalar1=sg_, scalar2=None, op0=ALU.mult),
                       reads=[bP], writes=[B_MT])
                    OP("dve", lambda e, Pt_=Pt_, a0=a0, gs=gs, sg_=sg_: e.tensor_scalar(
                        out=MT[64:128, gs, :, :], in0=Pt_[0:64, gs, a0:a0 + 8, :], scalar1=-sg_, scalar2=None, op0=ALU.mult),
                       reads=[bP], writes=[B_MT])
                dm_bg = []

                def OPd(*a, **k):
                    dm_bg.append(lambda: OP(*a, **k))
                ER = sb(st2a, "ER", [128, 32, 16], F32); B_ER = Buf()
                EI = sb(st2a, "EI", [128, 32, 16], F32); B_EI = Buf()
                for pw in range(1, 9):
                    OPd("dve", lambda e, pw=pw: e.tensor_tensor(out=w1[:], in0=CR[:], in1=bc16(AR[:, pw, :]), op=ALU.mult), reads=[B_CR, B_AR], writes=[B_w1])
                    OPd("dve", lambda e, pw=pw: e.tensor_tensor(out=w2[:], in0=CI[:], in1=bc16(AI[:, pw, :]), op=ALU.mult), reads=[B_CI, B_AI], writes=[B_w2])
                    OPd("dve", lambda e: e.tensor_tensor(out=ER[:], in0=w1[:], in1=w2[:], op=ALU.subtract), reads=[B_w1, B_w2], writes=[B_ER])
                    OPd("dve", lambda e, pw=pw: e.tensor_tensor(out=w1[:], in0=CR[:], in1=bc16(AI[:, pw, :]), op=ALU.mult), reads=[B_CR, B_AI, B_ER], writes=[B_w1])
                    OPd("dve", lambda e, pw=pw: e.tensor_tensor(out=w2[:], in0=CI[:], in1=bc16(AR[:, pw, :]), op=ALU.mult), reads=[B_CI, B_AR, B_ER], writes=[B_w2])
                    OPd("dve", lambda e: e.tensor_tensor(out=EI[:], in0=w1[:], in1=w2[:], op=ALU.add), reads=[B_w1, B_w2], writes=[B_EI])
                    for d_ in range(2):
                        gs = slice(d_ * 16, (d_ + 1) * 16)
                        tl = pw - 1 if d_ == 0 else 8 - pw
                        s2 = -1.0 if d_ == 0 else 1.0
                        OPd("pool", lambda e, gs=gs, tl=tl: e.tensor_copy(out=Dm1[0:64, gs, tl, :], in_=ER[0:64, gs, :]), reads=[B_ER], writes=[B_Dm1])
                        OPd("pool", lambda e, gs=gs, tl=tl: e.tensor_scalar(out=Dm1[64:128, gs, tl, :], in0=EI[64:128, gs, :], scalar1=-1.0, scalar2=0.0, op0=ALU.mult, op1=ALU.add), reads=[B_EI], writes=[B_Dm1])
                        OPd("pool", lambda e, gs=gs, tl=tl, s2=s2: e.tensor_scalar(out=Dm2[0:64, gs, tl, :], in0=EI[0:64, gs, :], scalar1=s2, scalar2=0.0, op0=ALU.mult, op1=ALU.add), reads=[B_EI], writes=[B_Dm2])
                        OPd("pool", lambda e, gs=gs, tl=tl, s2=s2: e.tensor_scalar(out=Dm2[64:128, gs, tl, :], in0=ER[64:128, gs, :], scalar1=s2, scalar2=0.0, op0=ALU.mult, op1=ALU.add), reads=[B_ER], writes=[B_Dm2])
                krot = Rot([3, 4, 5, 6]); srot = Rot([0, 1, 2, 7])
                for dg in range(32):
                    d_, g = dg // 16, dg % 16
                    Pt_, bP, a0 = (Pf, B_Pf, 0) if d_ == 0 else (Pb, B_Pb, 7)
                    pk = krot.next(); pS = srot.next()
                    for t_ in range(8):
                        OP("pe", lambda e, Pt_=Pt_, dg=dg, t_=t_, pk=pk: e.matmul(
                            psb[pk][:, t_ * 16:(t_ + 1) * 16], lhsT=Pt_[:, dg, 7 - t_:15 - t_, :].rearrange("p a c -> p (a c)"),
                            rhs=Cm[:, dg, :], start=True, stop=True), reads=[bP, B_Cm], writes=[PB[pk]])
                    for _ in range(4):
                        if dm_bg:
                            dm_bg.pop(0)()
                    if d_ == 0:
                        OP("dve", lambda e, dg=dg, g=g, pk=pk: e.scalar_tensor_tensor(
                            out=Kmat[:, dg, :], in0=ident[:], scalar=DSK[:, g:g + 1], in1=psb[pk][:, 0:128],
                            op0=ALU.mult, op1=ALU.add), reads=[PB[pk], B_ident, B_DSK], writes=[B_Kmat])
                    else:
                        OP("dve", lambda e, dg=dg, pk=pk: e.tensor_copy(out=Kmat[:, dg, :], in_=psb[pk][:, 0:128]),
                           reads=[PB[pk]], writes=[B_Kmat])
                    OP("pe", lambda e, Pt_=Pt_, dg=dg, a0=a0, pS=pS: e.matmul(
                        psb[pS][:, 0:128], lhsT=Pt_[:, dg, a0:a0 + 8, :].rearrange("p a c -> p (a c)"), rhs=ident[:],
                        start=True, stop=True), reads=[bP, B_ident], writes=[PB[pS]])
                    OP("pe", lambda e, dg=dg, pS=pS: e.matmul(
                        psb[pS][:, 128:256], lhsT=MT[:, dg, :, :].rearrange("p a c -> p (a c)"), rhs=ident[:],
                        start=True, stop=True), reads=[B_MT, B_ident], writes=[PB[pS]])
                    OP("act", lambda e, dg=dg, pS=pS: e.activation(out=Smat[:, dg, :], in_=psb[pS][:, 0:128], func=AF.Copy),
                       reads=[PB[pS]], writes=[B_Smat])
                    OP("act", lambda e, dg=dg, pS=pS: e.activation(out=SmatT[:, dg, :], in_=psb[pS][:, 128:256], func=AF.Copy),
                       reads=[PB[pS]], writes=[B_SmatT])
                while dm_bg:
                    dm_bg.pop(0)()
            S.barrier()
            NS = 544
            ioJ = sb(st2, "ioJ", [128, NS], F32); B_ioJ = Buf()
            OP("pool", lambda e: e.iota(ioJ[:], pattern=[[1, NS]], base=0, channel_multiplier=0, allow_small_or_imprecise_dtypes=True), writes=[B_ioJ])

            def tset(n):
                return [(sb(st2, f"{n}{i}", [128, NS], F32), Buf()) for i in range(4)]
            Fs = tset("Fs"); Fc = tset("Fc"); FFs = tset("FFs"); FFc = tset("FFc"); SINt = tset("SINt"); COSt = tset("COSt")
            Vt = tset("Vt"); V2t = tset("V2t"); Wt = tset("Wt"); RHt = tset("RHt")
            P1t = [(sb(st2, f"P1t{i}", [128, NS], BF16), Buf()) for i in range(4)]
            P2t = [(sb(st2, f"P2t{i}", [128, NS], BF16), Buf()) for i in range(4)]
            ystg = [(sb(st2, f"ystg{i}", [128, 512], F32), Buf()) for i in range(2)]
            print("S5 sbuf remaining", nc.sbuf_bytes_remaining)
            srot = Rot([0, 1, 2, 3, 4, 5])

            def sets(n):
                i2 = n % 4
                return dict(F1=Fs[i2], F2=Fc[i2], FF1=FFs[i2], SN=SINt[i2], CS=COSt[i2], V=Vt[i2], V2=V2t[i2], W=Wt[i2], RH=RHt[i2],
                            P1=P1t[i2], P2=P2t[i2])

            def stageA(n):
                g, d_ = n // 2, n % 2
                dg = d_ * 16 + g
                T_ = sets(n)
                (F1, bF1), (F2, bF2), (FF1, bFF1) = T_["F1"], T_["F2"], T_["FF1"]
                (SN, bSN), (CS, bCS), (RH, bRH) = T_["SN"], T_["CS"], T_["RH"]
                OP("dve", lambda e, F1=F1, dg=dg: e.tensor_scalar(out=F1[:], in0=ioJ[:], scalar1=FPHI[:, dg:dg + 1], scalar2=None, op0=ALU.mult),
                   reads=[B_ioJ, B_FPHI], writes=[bF1])
                frac_sincos(SN[:], CS[:], F1[:], F2[:], FF1[:], bF1, bF2, bFF1, bSN, bCS, halfpi[:, 0:1])
                OP("pool", lambda e, RH=RH, dg=dg: e.tensor_scalar(out=RH[:], in0=ioJ[:], scalar1=0.0, scalar2=RHO[:, dg:dg + 1], op0=ALU.mult, op1=ALU.add),
                   reads=[B_ioJ, B_RHO], writes=[bRH])

            def stageB(n):
                g, d_ = n // 2, n % 2
                dg = d_ * 16 + g
                j0 = 0 if d_ == 0 else 32
                T_ = sets(n)
                (SN, bSN), (CS, bCS), (RH, bRH) = T_["SN"], T_["CS"], T_["RH"]
                (V, bV), (V2, bV2), (Wt_, bW) = T_["V"], T_["V2"], T_["W"]
                (P1, bP1), (P2, bP2) = T_["P1"], T_["P2"]
                pa = srot.next(); pb_ = srot.next(); pc = srot.next()
                for (M_, bM, pm, tail0) in ((Smat, B_Smat, pa, 0), (SmatT, B_SmatT, pc, 32)):
                    OP("pe", lambda e, M_=M_, dg=dg, pm=pm, g=g, j0=j0: e.matmul(psb[pm][:, 0:512], lhsT=M_[:, dg, :], rhs=u8all[:, g, j0:j0 + 512],
                                                                               start=True, stop=True), reads=[bM, B_u8], writes=[PB[pm]])
                    OP("pe", lambda e, M_=M_, dg=dg, pb_=pb_, g=g, j0=j0, tail0=tail0: e.matmul(
                        psb[pb_][:, tail0:tail0 + 32], lhsT=M_[:, dg, :], rhs=u8all[:, g, j0 + 512:j0 + 544], start=True, stop=True),
                       reads=[bM, B_u8], writes=[PB[pb_]])
                OP("dve", lambda e, V=V, CS=CS, pa=pa: e.tensor_tensor(out=V[:, 0:512], in0=psb[pa][:, 0:512], in1=CS[:, 0:512], op=ALU.mult),
                   reads=[PB[pa], bCS], writes=[bV])
                OP("dve", lambda e, V=V, CS=CS, pb_=pb_: e.tensor_tensor(out=V[:, 512:544], in0=psb[pb_][:, 0:32], in1=CS[:, 512:544], op=ALU.mult),
                   reads=[PB[pb_], bCS], writes=[bV])
                OP("dve", lambda e, V2=V2, SN=SN, pc=pc: e.tensor_tensor(out=V2[:, 0:512], in0=psb[pc][:, 0:512], in1=SN[:, 0:512], op=ALU.mult),
                   reads=[PB[pc], bSN], writes=[bV2])
                OP("dve", lambda e, V2=V2, SN=SN, pb_=pb_: e.tensor_tensor(out=V2[:, 512:544], in0=psb[pb_][:, 32:64], in1=SN[:, 512:544], op=ALU.mult),
                   reads=[PB[pb_], bSN], writes=[bV2])
                OP("pool", lambda e, V=V, V2=V2: e.tensor_tensor(out=V[:], in0=V[:], in1=V2[:], op=ALU.add), reads=[bV, bV2], writes=[bV])
                if d_ == 0:
                    OP("dve", lambda e, Wt_=Wt_, RH=RH, V=V: e.tensor_tensor_scan(out=Wt_[:, :], data0=RH[:, :], data1=V[:, :], initial=0.0,
                                                                                  op0=ALU.mult, op1=ALU.add), reads=[bRH, bV], writes=[bW])
                else:
                    OP("dve", lambda e, Wt_=Wt_, RH=RH, V=V: e.tensor_tensor_scan(out=Wt_[:, ::-1], data0=RH[:, ::-1], data1=V[:, ::-1], initial=0.0,
                                                                                  op0=ALU.mult, op1=ALU.add), reads=[bRH, bV], writes=[bW])
                OP("dve", lambda e, P1=P1, Wt_=Wt_, CS=CS: e.tensor_tensor(out=P1[:], in0=Wt_[:], in1=CS[:], op=ALU.mult), reads=[bW, bCS], writes=[bP1])
                OP("pool", lambda e, P2=P2, Wt_=Wt_, SN=SN: e.tensor_tensor(out=P2[:], in0=Wt_[:], in1=SN[:], op=ALU.mult), reads=[bW, bSN], writes=[bP2])
                sh = 31 if d_ == 0 else 1
                return [(Kmat, B_Kmat, dg, u8all, B_u8, g, 32), (Dm1, B_Dm1, dg, P1, bP1, None, sh), (Dm2, B_Dm2, dg, P2, bP2, None, sh)]

            def finish_g(g, ymm):
                pY = 6 + (g % 2)
                for n_, (M_, bM, dg, R_, bR, gsel, c0) in enumerate(ymm):
                    if gsel is not None:
                        OP("pe", lambda e, M_=M_, dg=dg, R_=R_, gsel=gsel, c0=c0, n_=n_, pY=pY: e.matmul(
                            psb[pY][:, :], lhsT=M_[:, dg, :], rhs=R_[:, gsel, c0:c0 + 512], start=(n_ == 0), stop=(n_ == 5)),
                           reads=[bM, bR], writes=[PB[pY]])
                    else:
                        OP("pe", lambda e, M_=M_, dg=dg, R_=R_, c0=c0, n_=n_, pY=pY: e.matmul(
                            psb[pY][:, :], lhsT=M_[:, dg, :, :].rearrange("p a c -> p (a c)"), rhs=R_[:, c0:c0 + 512],
                            start=(n_ == 0), stop=(n_ == 5)), reads=[bM, bR], writes=[PB[pY]])
                ys, bys = ystg[g % 2]
                OP("act", lambda e, ys=ys, pY=pY: e.activation(out=ys[:], in_=psb[pY][:, :], func=AF.Copy), reads=[PB[pY]], writes=[bys])
                for tl in range(8):
                    OP("sp", lambda e, ys=ys, tl=tl, g=g: e.dma_start(out=y_s[tl, g * 16:(g + 1) * 16, :], in_=ys[16 * tl:16 * tl + 16, :]),
                       reads=[bys], writes=[B_y], dma=True)

            wmb2 = [sb(st2, f"wmb2_{i}", [128, 8, 512], F32) for i in range(2)]; B_wmb2 = [Buf(), Buf()]
            bmb2 = [sb(st2, f"bmb2_{i}", [1, 512], F32) for i in range(2)]; B_bmb2 = [Buf(), Buf()]
            rowb2 = [sb(st2, f"rowb2_{i}", [2, 512], F32) for i in range(2)]; B_rowb2 = [Buf(), Buf()]
            print("S5 sbuf remaining (after adaLN bufs)", nc.sbuf_bytes_remaining)
            stageA(0)
            stageA(1)
            ymm = []
            for n in range(32):
                if n == 0:
                    for b_ in (NBLK0, NBLK0 + 1):
                        adaln_issue(b_, wmb2[b_ % 2], B_wmb2[b_ % 2], bmb2[b_ % 2], B_bmb2[b_ % 2])
                if n % 4 == 0 and n >= 4:
                    b_ = NBLK0 + n // 4 - 1
                    adaln_block(b_, wmb2[b_ % 2], B_wmb2[b_ % 2], bmb2[b_ % 2], B_bmb2[b_ % 2], rowb2[b_ % 2], B_rowb2[b_ % 2], issue=False)
                    if b_ + 2 < 18:
                        adaln_issue(b_ + 2, wmb2[b_ % 2], B_wmb2[b_ % 2], bmb2[b_ % 2], B_bmb2[b_ % 2])
                if n + 2 < 32:
                    stageA(n + 2)
                ymm += stageB(n)
                if n % 2 == 1:
                    finish_g(n // 2, ymm)
                    ymm = []
            b_ = 17
            adaln_block(b_, wmb2[b_ % 2], B_wmb2[b_ % 2], bmb2[b_ % 2], B_bmb2[b_ % 2], rowb2[b_ % 2], B_rowb2[b_ % 2], issue=False)
            adaln_AB([(2, (6, 7))])
        S.barrier()

        if dbg == "s5":
            OP("sp", lambda e: e.nop())
            S.emit(nc)
            return nc, {}

        NK = T + NCTX
        with contextlib.ExitStack() as st3:
            wuq = sb(st3, "wuq", [128, 2, 1152], BF16); B_wuq = Buf()
            wuqsw = sb(st3, "wuqsw", [128, 2, H, 32], BF16); B_wuqsw = Buf()
            wuk = sb(st3, "wuk", [128, H, 64], BF16); B_wuk = Buf()
            wv = sb(st3, "wv", [128, H, 64], BF16); B_wv = Buf()
            cqn = sb(st3, "cqn", [128, 2, T], BF16); B_cqnsb = Buf()
            ckvn = sb(st3, "ckvn", [128, NK], BF16); B_ckvnsb = Buf()
            kbuf = [sb(st3, f"kbuf{i}", [128, NK], BF16) for i in range(2)]; B_kn = [Buf(), Buf()]; B_krope = Buf()
            cosT = sb(st3, "cosT", [128, T], F32); B_cosT = Buf()
            sinT = sb(st3, "sinT", [128, T], F32); B_sinT = Buf()
            sel64 = sb(st3, "sel64", [65, 64], F32); B_sel64 = Buf()
            RP = slice(64, 96)
            OP("dve", lambda e: e.memset(sel64[0:64, :], 0.0), writes=[B_sel64])
            OP("dve", lambda e: e.memset(sel64[64:65, :], 1.0), writes=[B_sel64])
            wukv_v = wukv_d.rearrange("r (h two d) -> r h two d", two=2, d=64)
            OP("pool", lambda e: e.dma_start(out=wuk[:], in_=wukv_v[:, :, 0, :]), writes=[B_wuk], dma=True)
            OP("pool", lambda e: e.dma_start(out=wv[:], in_=wukv_v[:, :, 1, :]), writes=[B_wv], dma=True)
            OP("sp", lambda e: e.dma_start(out=cqn[:], in_=cqn_s.rearrange("(k p) t -> p k t", p=128)), reads=[B_cqn], writes=[B_cqnsb], dma=True)
            OP("sp", lambda e: e.dma_start(out=ckvn[:], in_=ckvn_s), reads=[B_ckvn], writes=[B_ckvnsb], dma=True)
            with contextlib.ExitStack() as st3a:
                wst = sb(st3a, "wst", [128, 2, 1152], F32); B_wst = Buf()
                OP("sp", lambda e: e.dma_start(out=wst[:], in_=wuq_d.rearrange("(k p) n -> p k n", p=128)), writes=[B_wst], dma=True)
                OP("dve", lambda e: e.tensor_scalar(out=wuq[:], in0=wst[:], scalar1=ATTN_SCALE, scalar2=None, op0=ALU.mult),
                   reads=[B_wst], writes=[B_wuq])
                for kc in range(2):
                    rope_v = wst[:, kc, :].rearrange("p (h c) -> p h c", c=96)[:, :, 64:96].rearrange("p h (i two) -> p h i two", two=2)
                    sw_v = wuqsw[:, kc, :, :].rearrange("p h (i two) -> p h i two", two=2)
                    OP("dve", lambda e, rope_v=rope_v, sw_v=sw_v: e.tensor_scalar(out=sw_v[:, :, :, 0], in0=rope_v[:, :, :, 1], scalar1=-ATTN_SCALE,
                                                                                  scalar2=None, op0=ALU.mult), reads=[B_wst], writes=[B_wuqsw])
                    OP("dve", lambda e, rope_v=rope_v, sw_v=sw_v: e.tensor_scalar(out=sw_v[:, :, :, 1], in0=rope_v[:, :, :, 0], scalar1=ATTN_SCALE,
                                                                                  scalar2=None, op0=ALU.mult), reads=[B_wst], writes=[B_wuqsw])
                frow = sb(st3a, "frow", [1, 64], F32); B_frow = Buf()
                fpart = sb(st3a, "fpart", [128, 2], F32); B_fpart = Buf()
                OP("dve", lambda e: e.memset(frow[:], 0.0), writes=[B_frow])
                for i_ in range(8):
                    fq = (10000.0 ** (-(2.0 * i_) / 16.0)) / TWO_PI
                    OP("dve", lambda e, i_=i_, fq=fq: e.memset(frow[0:1, 2 * i_:2 * i_ + 2], fq), writes=[B_frow])
                    OP("dve", lambda e, i_=i_, fq=fq: e.memset(frow[0:1, 48 + 2 * i_:48 + 2 * i_ + 2], fq), writes=[B_frow])
                OP("pe", lambda e: e.matmul(psb[0][64:96, 0:1], lhsT=frow[0:1, 0:32], rhs=ones[0:1, 0:1], start=True, stop=True),
                   reads=[B_frow, B_ones], writes=[PB[0]])
                OP("pe", lambda e: e.matmul(psb[0][64:96, 1:2], lhsT=frow[0:1, 32:64], rhs=ones[0:1, 0:1], start=True, stop=True),
                   reads=[B_frow, B_ones], writes=[PB[0]])
                OP("dve", lambda e: e.tensor_copy(out=fpart[RP, :], in_=psb[0][RP, 0:2]), reads=[PB[0]], writes=[B_fpart])
                rowv = sb(st3a, "rowv", [128, T], F32); B_rowv = Buf()
                colv = sb(st3a, "colv", [128, T], F32); B_colv = Buf()
                OP("pool", lambda e: e.iota(rowv[RP, :], pattern=[[1, 64], [0, 64]], base=0, channel_multiplier=0, allow_small_or_imprecise_dtypes=True), writes=[B_rowv])
                OP("pool", lambda e: e.iota(colv[RP, :], pattern=[[0, 64], [1, 64]], base=0, channel_multiplier=0, allow_small_or_imprecise_dtypes=True), writes=[B_colv])
                Fa = sb(st3a, "Fa", [128, T], F32); B_Fa = Buf()
                Fb = sb(st3a, "Fb", [128, T], F32); B_Fb = Buf()
                Ff = sb(st3a, "Ff", [128, T], F32); B_Ff = Buf()
                OP("dve", lambda e: e.tensor_scalar(out=Fa[RP, :], in0=rowv[RP, :], scalar1=fpart[RP, 0:1], scalar2=None, op0=ALU.mult),
                   reads=[B_rowv, B_fpart], writes=[B_Fa])
                OP("dve", lambda e: e.scalar_tensor_tensor(out=Fa[RP, :], in0=colv[RP, :], scalar=fpart[RP, 1:2], in1=Fa[RP, :], op0=ALU.mult, op1=ALU.add),
                   reads=[B_colv, B_fpart, B_Fa], writes=[B_Fa])
                frac_sincos(sinT[RP, :], cosT[RP, :], Fa[RP, :], Fb[RP, :], Ff[RP, :], B_Fa, B_Fb, B_Ff, B_sinT, B_cosT, halfpi[RP, 0:1])
                krA = rowv; krB = colv
                OP("sp", lambda e: e.dma_start(out=krA[RP, 0:NK - T], in_=kr_s[0, :, 0:NCTX]), reads=[B_kr, B_Fa], writes=[B_rowv], dma=True)
                for kb_ in kbuf:
                    OP("dve", lambda e, kb_=kb_: e.tensor_copy(out=kb_[RP, 0:NCTX], in_=krA[RP, 0:NCTX]), reads=[B_rowv], writes=[B_krope])
                OP("sp", lambda e: e.dma_start(out=krA[RP, :], in_=kr_s[0, :, NCTX:NK]), reads=[B_kr, B_krope], writes=[B_rowv], dma=True)
                OP("sp", lambda e: e.dma_start(out=krB[RP, :], in_=kr_s[1, :, NCTX:NK]), reads=[B_kr, B_Fa], writes=[B_colv], dma=True)
                OP("dve", lambda e: e.tensor_tensor(out=Fa[RP, :], in0=krA[RP, :], in1=cosT[RP, :], op=ALU.mult), reads=[B_rowv, B_cosT, B_sinT], writes=[B_Fa])
                OP("dve", lambda e: e.tensor_tensor(out=Fb[RP, :], in0=krB[RP, :], in1=sinT[RP, :], op=ALU.mult), reads=[B_colv, B_sinT, B_cosT], writes=[B_Fb])
                for kb_ in kbuf:
                    OP("dve", lambda e, kb_=kb_: e.tensor_tensor(out=kb_[RP, NCTX:NK], in0=Fa[RP, :], in1=Fb[RP, :], op=ALU.add),
                       reads=[B_Fa, B_Fb], writes=[B_krope])
            S.barrier()
            Vall = sb(st3, "Vall", [128, 34, H, 65], BF16); B_Vall = Buf()
            OP("pool", lambda e: e.memset(Vall[:, :, :, 64:65], 1.0), writes=[B_Vall])
            vrot = Rot([1, 2, 3, 4])
            for kt in range(34):
                for hh in range(2):
                    pv_ = vrot.next()
                    OP("pe", lambda e, kt=kt, hh=hh, pv_=pv_: e.matmul(
                        psb[pv_][:, 0:384], lhsT=ckvn[:, kt * 128:(kt + 1) * 128],
                        rhs=wv[:, hh * 6:(hh + 1) * 6, :].rearrange("p h d -> p (h d)"), start=True, stop=True),
                       reads=[B_ckvnsb, B_wv], writes=[PB[pv_]])
                    OP("act" if hh else "dve", (lambda e, kt=kt, hh=hh, pv_=pv_: e.activation(
                        out=Vall[:, kt, hh * 6:(hh + 1) * 6, 0:64], in_=psb[pv_][:, 0:384].rearrange("p (h d) -> p h d", d=64), func=AF.Copy))
                       if hh else (lambda e, kt=kt, hh=hh, pv_=pv_: e.tensor_copy(
                           out=Vall[:, kt, hh * 6:(hh + 1) * 6, 0:64], in_=psb[pv_][:, 0:384].rearrange("p (h d) -> p h d", d=64))),
                       reads=[PB[pv_]], writes=[B_Vall])
            S.barrier()
            Qh = [(sb(st3, f"Qh{i}", [128, 512], BF16), Buf()) for i in range(2)]
            qt1 = sb(st3, "qt1", [128, 512], F32); B_qt1 = Buf()
            qt2 = sb(st3, "qt2", [128, 512], F32); B_qt2 = Buf()
            Pts = [(sb(st3, f"Pt{i}", [128, 1024], BF16), Buf()) for i in range(3)]
            Osb = [(sb(st3, f"Osb{i}", [65, 512], F32), Buf()) for i in range(2)]
            atts = [(sb(st3, f"att{i}", [64, 512], BF16), Buf()) for i in range(2)]
            print("attn sbuf remaining", nc.sbuf_bytes_remaining)
            SP_ = [(0, Buf()), (2, Buf())]
            NB_ = H * 8

            def gen_K(h):
                kb_ = kbuf[h % 2]; bkn = B_kn[h % 2]
                for n_, c0 in enumerate(range(0, NK, 512)):
                    c1 = min(NK, c0 + 512)
                    pk = 6 + (n_ % 2)
                    OP("pe", lambda e, h=h, c0=c0, c1=c1, pk=pk: e.matmul(psb[pk][0:64, 0:c1 - c0], lhsT=wuk[:, h, :], rhs=ckvn[:, c0:c1],
                                                                         start=True, stop=True), reads=[B_wuk, B_ckvnsb], writes=[PB[pk]])
                    OP("dve", lambda e, kb_=kb_, c0=c0, c1=c1, pk=pk: e.tensor_copy(out=kb_[0:64, c0:c1], in_=psb[pk][0:64, 0:c1 - c0]),
                       reads=[PB[pk]], writes=[bkn])

            def gen_Q(b):
                h, qb = b // 8, b % 8
                qs = slice(qb * 512, (qb + 1) * 512)
                Q, bQ = Qh[b % 2]
                for kc in range(2):
                    OP("pe", lambda e, h=h, kc=kc, qs=qs: e.matmul(psb[6][0:96, :], lhsT=wuq[:, kc, h * 96:(h + 1) * 96], rhs=cqn[:, kc, qs],
                                                                start=(kc == 0), stop=(kc == 1)), reads=[B_wuq, B_cqnsb], writes=[PB[6]])
                for kc in range(2):
                    OP("pe", lambda e, h=h, kc=kc, qs=qs: e.matmul(psb[7][64:96, :], lhsT=wuqsw[:, kc, h, :], rhs=cqn[:, kc, qs],
                                                                start=(kc == 0), stop=(kc == 1)), reads=[B_wuqsw, B_cqnsb], writes=[PB[7]])
                OP("dve", lambda e, Q=Q: e.tensor_copy(out=Q[0:64, :], in_=psb[6][0:64, :]), reads=[PB[6]], writes=[bQ])
                OP("dve", lambda e, qs=qs: e.tensor_tensor(out=qt1[RP, :], in0=psb[6][RP, :], in1=cosT[RP, qs], op=ALU.mult),
                   reads=[PB[6], B_cosT], writes=[B_qt1])
                OP("dve", lambda e, qs=qs: e.tensor_tensor(out=qt2[RP, :], in0=psb[7][RP, :], in1=sinT[RP, qs], op=ALU.mult),
                   reads=[PB[7], B_sinT], writes=[B_qt2])
                OP("dve", lambda e, Q=Q: e.tensor_tensor(out=Q[RP, :], in0=qt1[RP, :], in1=qt2[RP, :], op=ALU.add),
                   reads=[B_qt1, B_qt2], writes=[bQ])

            def evac1(b):
                pO = 4 + (b % 2)
                Ob, bOb = Osb[b % 2]
                OP("dve", lambda e, Ob=Ob, pO=pO: e.tensor_copy(out=Ob[0:65, :], in_=psb[pO][0:65, :]), reads=[PB[pO]], writes=[bOb])
                OP("dve", lambda e, Ob=Ob: e.reciprocal(out=Ob[64:65, :], in_=Ob[64:65, :]), reads=[bOb], writes=[bOb])

            def evac2(b):
                h, qb = b // 8, b % 8
                qs = slice(qb * 512, (qb + 1) * 512)
                Ob, bOb = Osb[b % 2]
                at, bat = atts[b % 2]
                OP("pe", lambda e, Ob=Ob: e.matmul(psb[7][0:64, :], lhsT=sel64[0:65, :], rhs=Ob[0:65, :], start=True, stop=True),
                   reads=[B_sel64, bOb], writes=[PB[7]])
                OP("dve", lambda e, at=at, Ob=Ob: e.tensor_tensor(out=at[:, :], in0=Ob[0:64, :], in1=psb[7][0:64, :], op=ALU.mult),
                   reads=[bOb, PB[7]], writes=[bat])
                OP("sp", lambda e, at=at, h=h, qs=qs: e.dma_start(out=attn_s[h * 64:(h + 1) * 64, qs], in_=at[:, :]),
                   reads=[bat], writes=[B_attn], dma=True)

            gen_K(0)
            gen_Q(0)
            p_i = 0
            for b in range(NB_):
                h, qb = b // 8, b % 8
                kb_ = kbuf[h % 2]; bkn = B_kn[h % 2]
                Q, bQ = Qh[b % 2]
                pO = 4 + (b % 2)
                pend = []

                def pv_pair(kt0, Pt_, bPt, h=h, pO=pO):
                    for j in range(2):
                        kt = kt0 + j
                        OP("pe", lambda e, kt=kt, j=j, Pt_=Pt_, h=h, pO=pO: e.matmul(psb[pO][0:65, :], lhsT=Vall[:, kt, h, :], rhs=Pt_[:, j * 512:(j + 1) * 512],
                                                                                  start=(kt == 0), stop=(kt == 33)), reads=[B_Vall, bPt], writes=[PB[pO]])
                for pr_ in range(17):
                    kt0 = 2 * pr_
                    sp0, bSp = SP_[pr_ % 2]
                    for j in range(2):
                        kt = kt0 + j
                        OP("pe", lambda e, kt=kt, j=j, kb_=kb_, Q=Q, sp0=sp0: e.matmul(psb[sp0 + j][:, :], lhsT=kb_[0:96, kt * 128:(kt + 1) * 128], rhs=Q[0:96, :],
                                                                                    start=True, stop=True), reads=[bkn, B_krope, bQ], writes=[bSp])
                    Pt_, bPt = Pts[p_i % 3]; p_i += 1
                    OP("act", lambda e, Pt_=Pt_, sp0=sp0: e.activation(out=Pt_[:, :], in_=psall[:, sp0 * 512:(sp0 + 2) * 512], func=AF.Exp),
                       reads=[bSp], writes=[bPt])
                    pend.append((kt0, Pt_, bPt))
                    if len(pend) > 2:
                        pv_pair(*pend.pop(0))
                    if pr_ == 1 and b > 0:
                        evac1(b - 1)
                    if pr_ == 3 and b + 1 < NB_:
                        gen_Q(b + 1)
                    if pr_ == 7 and b > 0:
                        evac2(b - 1)
                    if pr_ == 10 and qb == 7 and h + 1 < H:
                        gen_K(h + 1)
                while pend:
                    pv_pair(*pend.pop(0))
            evac1(NB_ - 1)
            evac2(NB_ - 1)
        S.barrier()

        if dbg == "attn":
            OP("sp", lambda e: e.nop())
            S.emit(nc)
            return nc, {}

        with contextlib.ExitStack() as st4:
            wgu2v, B_wgu2 = load_w_bf16(st4, "wgu2", wgu2_d, 8, 2 * DFF)
            wdn2v, B_wdn2 = load_w_bf16(st4, "wdn2", wdn2_d, NFC, D)
            wglu, B_wglu = load_w_bf16(st4, "wglu", wglu_d, 2, 512)
            g5, B_g5 = bcast_rows(st4, "g5", 5 * D, 0, 1.0)
            g8, B_g8 = bcast_rows(st4, "g8", 8 * D, 0, 0.5)
            gfb, B_gfb = bcast_vec(st4, "gfb", gfin_d, D)
            W3 = common_work(st4)
            xt3v = sb(st4, "xt3", [128, 2, D], F32); B_xt3 = Buf()
            hT3v = sb(st4, "hT3", [128, 8, TN], BF16); B_hT3 = [Buf() for _ in range(8)]
            wo_t, B_wo = load_w_bf16(st4, "wout", wout_d, 8, D)
            ysb = sb(st4, "ysb", [128, 2, 8, TN // 8], F32); B_ysb = [Buf(), Buf()]
            xnf = W3["xn"][:].rearrange("p i d -> p (i d)")
            yn = xnf[:, 0:256]; B_yn = W3["B_xn"][0]
            tg = xnf[:, 256:512]; B_tg = W3["B_xn"][0]
            sgl = xnf[:, 512:768]; B_sgl = W3["B_xn"][0]
            zT = sb(st4, "zT", [128, 2, TN], BF16); B_zT = [Buf(), Buf()]
            ssm = sb(st4, "ssm", [128, 2, TN], F32); B_ssm = [Buf(), Buf()]
            sq1 = [(xnf[:, 1024 + i * 256:1024 + (i + 1) * 256], W3["B_xn"][1]) for i in range(2)]
            sq2 = [(sb(st4, f"sq2_{i}", [128, TN], BF16), Buf()) for i in range(2)]
            rbs = xnf[:, 768:1024]; B_rbs = W3["B_xn"][0]
            rba = xnf[:, 1536:1792]; B_rba = W3["B_xn"][1]
            ssmn = sb(st4, "ssmn", [128, 2, TN], BF16); B_ssmn = [Buf(), Buf()]
            at_t = sb(st4, "at_t", [128, 6, TN], BF16); B_at = Buf()
            print("phase3 sbuf remaining", nc.sbuf_bytes_remaining)
            if dbg == "p3alloc":
                S.barrier()
                OP("sp", lambda e: e.nop())
                S.emit(nc)
                return nc, {}
            sq1_rot = Rot(sq1); sq2_rot = Rot(sq2)
            pr = Rot([2, 3])
            wout_v = wout_d.rearrange("(k p) n -> p k n", p=128)
            def gelu_stage(ti_):
                t0_ = ti_ * TN
                j0_ = ti_ * (TN // 8)
                fs = []

                def loads():
                    OP("sp", lambda e: e.dma_start(out=at_t[:], in_=attn_s.rearrange("(k p) t -> p k t", p=128)[:, :, t0_:t0_ + TN]),
                       reads=[B_attn], writes=[B_at], dma=True)
                    for gc in range(2):
                        OP("sp", lambda e, gc=gc: e.dma_start(out=ysb[:, gc, :, :],
                                                              in_=y_s[:, gc * 128:(gc + 1) * 128, j0_:j0_ + TN // 8].rearrange("t p j -> p t j")),
                           reads=[B_y], writes=[B_ysb[gc]], dma=True)
                fs.append(loads)
                for gc in range(2):
                    fs.append(lambda gc=gc: OP("dve", lambda e: e.tensor_copy(out=yn[:].rearrange("p (j t) -> p j t", t=8),
                                                                            in_=ysb[:, gc, :, :].rearrange("p t j -> p j t")), reads=[B_ysb[gc]], writes=[B_yn]))
                    fs.append(lambda: OP("dve", lambda e: e.tensor_tensor(out=tg[:], in0=yn[:], in1=yn[:], op=ALU.mult), reads=[B_yn], writes=[B_tg]))
                    fs.append(lambda: OP("dve", lambda e: e.tensor_scalar(out=tg[:], in0=tg[:], scalar1=0.044715, scalar2=1.0, op0=ALU.mult, op1=ALU.add),
                                         reads=[B_tg], writes=[B_tg]))
                    fs.append(lambda: OP("dve", lambda e: e.tensor_tensor(out=tg[:], in0=tg[:], in1=yn[:], op=ALU.mult), reads=[B_tg, B_yn], writes=[B_tg]))
                    fs.append(lambda: OP("act", lambda e: e.activation(out=sgl[:], in_=tg[:], func=AF.Sigmoid, scale=2.0 * math.sqrt(2.0 / math.pi)),
                                         reads=[B_tg], writes=[B_sgl]))
                    fs.append(lambda gc=gc: OP("dve", lambda e: e.tensor_tensor(out=zT[:, gc, :], in0=yn[:], in1=sgl[:], op=ALU.mult),
                                               reads=[B_yn, B_sgl], writes=[B_zT[gc]]))
                fs.append(lambda: None)
                fs.append(lambda: None)
                for c2 in range(2):
                    def glu_mm(c2=c2):
                        for (pp, n4) in ((0, c2), (1, 2 + c2)):
                            for gc in range(2):
                                OP("pe", lambda e, pp=pp, n4=n4, gc=gc: e.matmul(psb[pp][:, 0:TN], lhsT=wglu[:, gc, n4 * 128:(n4 + 1) * 128], rhs=zT[:, gc, :],
                                                                              start=(gc == 0), stop=(gc == 1)), reads=[B_wglu[gc], B_zT[gc]], writes=[PB[pp]])

                    def glu_ev(c2=c2):
                        OP("act", lambda e: e.activation(out=sgl[:], in_=psb[1][:, 0:TN], func=AF.Sigmoid), reads=[PB[1]], writes=[B_sgl])
                        OP("dve", lambda e, c2=c2: e.tensor_tensor(out=ssm[:, c2, :], in0=psb[0][:, 0:TN], in1=sgl[:], op=ALU.mult),
                           reads=[PB[0], B_sgl], writes=[B_ssm[c2]])
                    fs.append(glu_mm); fs.append(glu_ev)
                ssq_ = [sq1[0], sq1[1]]
                for c2 in range(2):
                    def ssm_sq(c2=c2):
                        sq, bsq = ssq_[c2]
                        OP("dve", lambda e, sq=sq, c2=c2: e.tensor_tensor(out=sq[:], in0=ssm[:, c2, :], in1=ssm[:, c2, :], op=ALU.mult),
                           reads=[B_ssm[c2]], writes=[bsq])
                    fs.append(ssm_sq)

                def at_sq(kc):
                    sq, bsq = sq2[kc % 2]
                    OP("pool", lambda e, sq=sq, kc=kc: e.tensor_tensor(out=sq[:], in0=at_t[:, kc, :], in1=at_t[:, kc, :], op=ALU.mult),
                       reads=[B_at], writes=[bsq])

                def at_mm(kc):
                    sq, bsq = sq2[kc % 2]
                    OP("pe", lambda e, sq=sq, kc=kc: e.matmul(psb[1][:, 0:TN], lhsT=onesb[:, :], rhs=sq[:], start=(kc == 0), stop=(kc == 5)),
                       reads=[bsq, B_onesb], writes=[PB[1]])

                def ssm_mm():
                    for c2 in range(2):
                        sq, bsq = ssq_[c2]
                        OP("pe", lambda e, sq=sq, c2=c2: e.matmul(psb[0][:, 0:TN], lhsT=ones[:, :], rhs=sq[:], start=(c2 == 0), stop=(c2 == 1)),
                           reads=[bsq, B_ones], writes=[PB[0]])
                fs.append(lambda: at_sq(0)); fs.append(lambda: at_sq(1)); fs.append(ssm_mm)
                for kc in range(6):
                    fs.append(lambda kc=kc: at_mm(kc))
                    if kc + 2 < 6:
                        fs.append(lambda kc=kc: at_sq(kc + 2))

                def ssm_fin():
                    OP("act", lambda e: e.activation(out=rbs[:], in_=psb[0][:, 0:TN], func=AF.Sqrt, scale=1.0 / 256.0, bias=W3["eps"][:, 0:1]),
                       reads=[PB[0], W3["B_eps"]], writes=[B_rbs])
                    OP("dve", lambda e: e.reciprocal(out=rbs[:], in_=rbs[:]), reads=[B_rbs], writes=[B_rbs])

                def ssm_n():
                    for c2 in range(2):
                        OP("dve", lambda e, c2=c2: e.scalar_tensor_tensor(out=ssmn[:, c2, :], in0=ssm[:, c2, :], scalar=gsm[:, 2, c2:c2 + 1], in1=rbs[:],
                                                                          op0=ALU.mult, op1=ALU.mult), reads=[B_ssm[c2], B_gsm, B_rbs], writes=[B_ssmn[c2]])

                def at_fin():
                    OP("act", lambda e: e.activation(out=rba[:], in_=psb[1][:, 0:TN], func=AF.Sqrt, scale=1.0 / 768.0, bias=W3["eps"][:, 0:1]),
                       reads=[PB[1], W3["B_eps"]], writes=[B_rba])
                    OP("dve", lambda e: e.reciprocal(out=rba[:], in_=rba[:]), reads=[B_rba], writes=[B_rba])

                def at_n(k0):
                    for kc in range(k0, k0 + 3):
                        OP("dve", lambda e, kc=kc: e.scalar_tensor_tensor(out=at_t[:, kc, :], in0=at_t[:, kc, :], scalar=gmla[:, kc:kc + 1], in1=rba[:],
                                                                          op0=ALU.mult, op1=ALU.mult), reads=[B_at, B_gmla, B_rba], writes=[B_at])
                fs.append(ssm_fin); fs.append(ssm_n); fs.append(at_fin); fs.append(lambda: at_n(0)); fs.append(lambda: at_n(3))
                return fs

            for ti in range(T // TN):
                t0 = ti * TN
                j0 = ti * (TN // 8)
                if ti == 0:
                    for f_ in gelu_stage(0):
                        f_()
                OP("sp", lambda e, t0=t0: e.dma_start(out=xt3v[:], in_=x1_s[t0:t0 + TN, :].rearrange("(i p) d -> p i d", p=128)),
                   reads=[B_x1], writes=[B_xt3], dma=True)
                dn = W3["dn_banks"]
                for kc in range(8):
                    wo = wo_t[:, kc, :]; bwo = B_wo[kc]
                    gi = 0
                    for i in range(2):
                        for e2 in range(2):
                            pb = dn[gi]; gi += 1
                            if kc < 6:
                                OP("pe", lambda e, wo=wo, kc=kc, i=i, e2=e2, pb=pb: e.matmul(
                                    psb[pb][:, :], lhsT=at_t[:, kc, i * 128:(i + 1) * 128], rhs=wo[:, e2 * 512:(e2 + 1) * 512],
                                    start=(kc == 0), stop=False), reads=[B_at, bwo], writes=[PB[pb]])
                            else:
                                OP("pe", lambda e, wo=wo, kc=kc, i=i, e2=e2, pb=pb: e.matmul(
                                    psb[pb][:, :], lhsT=ssmn[:, kc - 6, i * 128:(i + 1) * 128], rhs=wo[:, e2 * 512:(e2 + 1) * 512],
                                    start=False, stop=(kc == 7)), reads=[B_ssmn[kc - 6], bwo], writes=[PB[pb]])
                gi = 0
                for i in range(2):
                    for e2 in range(2):
                        pb = dn[gi]; gi += 1
                        tmp, btmp = W3["tmp_rot"].next()
                        OP("dve", lambda e, pb=pb, e2=e2, tmp=tmp: e.tensor_tensor(out=tmp[:], in0=psb[pb][:, :], in1=g5[:, e2 * 512:(e2 + 1) * 512], op=ALU.mult),
                           reads=[PB[pb], B_g5], writes=[btmp])
                        OP("pool", lambda e, i=i, e2=e2, tmp=tmp: e.tensor_tensor(out=xt3v[:, i, e2 * 512:(e2 + 1) * 512], in0=xt3v[:, i, e2 * 512:(e2 + 1) * 512],
                                                                               in1=tmp[:], op=ALU.add), reads=[btmp, B_xt3], writes=[B_xt3])
                norm_to_hT(xt3v, B_xt3, 2, 2, 0, hT3v, B_hT3, W3)
                bg_ = gelu_stage(ti + 1) if ti + 1 < T // TN else []
                ffn(xt3v, B_xt3, 2, hT3v, B_hT3, wgu2v, B_wgu2, wdn2v, B_wdn2, g8, B_g8, W3, bg=bg_)
                while bg_:
                    bg_.pop(0)()
                xo = W3["xn"]
                for i in range(2):
                    OP("act", lambda e, i=i: e.activation(out=xo[:, i, :], in_=xt3v[:, i, :], func=AF.Square, accum_out=W3["ssq"][:, i:i + 1]),
                       reads=[B_xt3], writes=[W3["B_xn"][i], W3["B_ssq"]])
                OP("act", lambda e: e.activation(out=W3["rstd"][:, 0:2], in_=W3["ssq"][:, 0:2], func=AF.Sqrt, scale=1.0 / D, bias=W3["eps"][:, 0:1]),
                   reads=[W3["B_ssq"], W3["B_eps"]], writes=[W3["B_rstd"]])
                OP("dve", lambda e: e.reciprocal(out=W3["rstd"][:, 0:2], in_=W3["rstd"][:, 0:2]), reads=[W3["B_rstd"]], writes=[W3["B_rstd"]])
                for i in range(2):
                    OP("dve", lambda e, i=i: e.scalar_tensor_tensor(out=xo[:, i, :], in0=xt3v[:, i, :], scalar=W3["rstd"][:, i:i + 1], in1=gfb[:],
                                                                    op0=ALU.mult, op1=ALU.mult), reads=[B_xt3, W3["B_rstd"], B_gfb], writes=[W3["B_xn"][i]])
                out_dmas.append(OP("pool", lambda e, t0=t0: e.dma_start(out=out_d[t0:t0 + TN, :].rearrange("(i p) d -> p i d", p=128), in_=xo[:]),
                                   reads=W3["B_xn"], dma=True))
        S.barrier()
        OP("sp", lambda e: e.nop(), extra=out_dmas)
        S.emit(nc)
    return nc, {}


def _prep_inputs(inputs):
    f = lambda a: np.ascontiguousarray(np.asarray(a, dtype=np.float32))
    shared = {
        "c_ctx": f(inputs["c_ctx"]).reshape(1, D),
        "w_mod": f(inputs["w_mod"][0]), "b_mod": f(inputs["b_mod"][0]).reshape(1, -1),
        "g_ffn1": f(inputs["g_ffn1"][0]).reshape(1, -1), "w_gu1": f(inputs["w_gu1"][0]), "w_down1": f(inputs["w_down1"][0]),
        "g_mix": f(inputs["g_mix"][0]).reshape(1, -1), "w_in": f(inputs["w_in"][0]),
        "g_cq": f(inputs["g_cq"][0]).reshape(1, -1), "w_uq": f(inputs["w_uq"][0]),
        "g_ckv": f(inputs["g_ckv"][0]).reshape(1, -1), "w_ukv": f(inputs["w_ukv"][0]),
        "lam_re": f(inputs["lam_re"][0]).reshape(32, 64), "lam_im": f(inputs["lam_im"][0]).reshape(32, 64),
        "log_dt": f(inputs["log_dt"][0]).reshape(1, 32),
        "b_re": f(inputs["b_re"][0]).reshape(32, 64, 16), "b_im": f(inputs["b_im"][0]).reshape(32, 64, 16),
        "c_re": f(inputs["c_re"][0]).reshape(512, 64), "c_im": f(inputs["c_im"][0]).reshape(512, 64),
        "d_skip": f(inputs["d_skip"][0]).reshape(1, -1), "w_glu": f(inputs["w_glu"][0]),
        "g_mla_out": f(inputs["g_mla_out"][0]).reshape(1, -1), "g_ssm_out": f(inputs["g_ssm_out"][0]).reshape(1, -1),
        "w_out": f(inputs["w_out"][0]), "g_ffn2": f(inputs["g_ffn2"][0]).reshape(1, -1),
        "w_gu2": f(inputs["w_gu2"][0]), "w_down2": f(inputs["w_down2"][0]),
        "g_final": f(inputs["g_final"]).reshape(1, -1),
    }
    x = f(inputs["x"]); c = f(inputs["c"]); ctx = f(inputs["ctx"])
    maps = []
    for b in range(8):
        m = dict(shared)
        m["x"] = x[b]; m["c"] = c[b].reshape(1, D); m["ctx"] = ctx[b]
        maps.append(m)
    return maps


def kernel(**inputs):
    nc, _ = build_program()
    in_maps = _prep_inputs(inputs)
    res = run_bass_kernel_spmd(nc, in_maps, core_ids=list(range(8)))
    out = np.stack([np.asarray(r["out"], dtype=np.float32) for r in res.results], axis=0)
    return out
```
